# Optimizing a Trainium2 kernel written in Bass

```python
import math
import jax, jax.numpy as jnp
from jax import lax
import numpy as np

D_MODEL = 1024
BATCH = 8
SEQ = 2048
DEPTH = 4
DEC_BATCH = 128
DEC_SEQ = 4
PAST_LEN = 16384
PAGE_SIZE = 128

N_EVEN = (DEPTH + 1) // 2
N_ODD = DEPTH // 2
EPS = 1e-6

POOL_WINDOWS = (2, 4, 8, 16)
N_POOL_GROUPS = len(POOL_WINDOWS)
D_POOL = D_MODEL // 2
POOL_GROUP = D_POOL // N_POOL_GROUPS
POOL_BUF = max(POOL_WINDOWS) - 1

N_MHEADS = 4
D_QK = D_MODEL // 8
D_HV = D_MODEL // 8
D_MQK = N_MHEADS * D_QK
D_MLSTM = N_MHEADS * D_HV
MLSTM_CHUNK = 128

EVEN_SPLITS = [int(s) for s in np.cumsum([D_POOL, D_MQK, D_MQK, D_MLSTM, D_MLSTM, N_MHEADS])]
D_IN_EVEN = D_POOL + 2 * D_MQK + 2 * D_MLSTM + 2 * N_MHEADS
D_MIX_EVEN = D_POOL + D_MLSTM

GMLP_CHUNK = 128
N_SGU_GROUPS = 4
D_SGU = D_MODEL
SGU_GROUP = D_SGU // N_SGU_GROUPS

D_FF = 2816
CONV_W = 3

kernel_name = "hybrid_pool_mlstm_sgu_convffn_step"


def rms_norm(x, g):
    xf = x.astype(jnp.float32)
    y = xf * lax.rsqrt(jnp.mean(xf * xf, axis=-1, keepdims=True) + EPS)
    return (y * g.astype(jnp.float32)).astype(x.dtype)


def layer_norm(x, g, b):
    xf = x.astype(jnp.float32)
    mu = jnp.mean(xf, axis=-1, keepdims=True)
    xc = xf - mu
    y = xc * lax.rsqrt(jnp.mean(xc * xc, axis=-1, keepdims=True) + EPS)
    return (y * g.astype(jnp.float32) + b.astype(jnp.float32)).astype(x.dtype)


def pool_mixer(p, buf, pos0, w_grp, scale):
    B, L, _ = p.shape
    ext = jnp.concatenate([buf.astype(p.dtype), p], axis=1)
    ef = ext.astype(jnp.float32)
    cs = jnp.concatenate([jnp.zeros((B, 1, D_POOL), jnp.float32), jnp.cumsum(ef, axis=1)], axis=1)
    hi = cs[:, POOL_BUF + 1:]
    pos = pos0 + jnp.arange(L)
    outs = []
    for g, w in enumerate(POOL_WINDOWS):
        sl = slice(g * POOL_GROUP, (g + 1) * POOL_GROUP)
        lo = cs[:, POOL_BUF + 1 - w:POOL_BUF + 1 - w + L, sl]
        cnt = jnp.minimum(w, pos + 1).astype(jnp.float32)[None, :, None]
        outs.append((hi[..., sl] - lo) / cnt - ef[:, POOL_BUF:, sl])
    d = jnp.stack(outs, axis=2)
    y = jnp.einsum('blgc,gcd->blgd', d, w_grp.astype(jnp.float32)).reshape(B, L, D_POOL)
    y = y * scale.astype(jnp.float32)
    return y.astype(p.dtype), ext[:, -POOL_BUF:]


def mlstm_chunk(carry, inp):
    C, n, m = carry
    q, k, v, ig, lf = inp
    L = q.shape[2]
    b = jnp.cumsum(lf, axis=-1)
    causal = jnp.tril(jnp.ones((L, L), dtype=bool))
    D = jnp.where(causal, b[..., :, None] - b[..., None, :] + ig[..., None, :], -jnp.inf)
    inter = b + m[..., None]
    m_t = jnp.maximum(inter, jnp.max(D, axis=-1))
    S = jnp.einsum('bhtk,bhsk->bhts', q, k) * jnp.exp(D - m_t[..., None])
    a = jnp.exp(inter - m_t)
    num = a[..., None] * jnp.einsum('bhtk,bhkv->bhtv', q, C) + jnp.einsum('bhts,bhsv->bhtv', S, v)
    den = a * jnp.einsum('bhtk,bhk->bht', q, n) + jnp.sum(S, axis=-1)
    h = num / jnp.maximum(jnp.abs(den), jnp.exp(-m_t))[..., None]
    m_new = m_t[..., -1]
    w_s = jnp.exp(b[..., -1:] - b + ig - m_new[..., None])
    a_L = jnp.exp(b[..., -1] + m - m_new)
    C_new = a_L[..., None, None] * C + jnp.einsum('bhs,bhsk,bhsv->bhkv', w_s, k, v)
    n_new = a_L[..., None] * n + jnp.einsum('bhs,bhsk->bhk', w_s, k)
    return (C_new, n_new, m_new), h


def mlstm(q, k, v, ig, lf, C0, n0, m0):
    B, H, L, _ = q.shape
    c = MLSTM_CHUNK if L % MLSTM_CHUNK == 0 else L
    nc = L // c

    def to_chunks(a):
        return jnp.moveaxis(a.reshape(B, H, nc, c, *a.shape[3:]), 2, 0)

    (C, n, m), h = lax.scan(mlstm_chunk, (C0, n0, m0),
                            (to_chunks(q), to_chunks(k), to_chunks(v), to_chunks(ig), to_chunks(lf)))
    h = jnp.moveaxis(h, 0, 2).reshape(B, H, L, D_HV)
    return h, C, n, m


def even_layer(x, pool_buf, C0, n0, m0, pos0, g_norm, w_in, b_gates, w_pool, pool_scale, g_head, w_out):
    B, L, _ = x.shape
    f32 = jnp.float32
    z = rms_norm(x, g_norm) @ w_in
    p, q, k, v, o, gates = jnp.split(z, EVEN_SPLITS[:5] + [EVEN_SPLITS[4] + 0], axis=-1)[:6] if False else jnp.split(z, EVEN_SPLITS[:5], axis=-1)
    gates = gates.astype(f32) + b_gates.astype(f32)
    ig = jnp.transpose(gates[..., :N_MHEADS], (0, 2, 1))
    lf = jnp.transpose(jax.nn.log_sigmoid(gates[..., N_MHEADS:]), (0, 2, 1))

    def heads(a, d):
        return jnp.transpose(a.reshape(B, L, N_MHEADS, d), (0, 2, 1, 3)).astype(f32)

    qh = heads(q, D_QK)
    kh = heads(k, D_QK) * (D_QK ** -0.5)
    vh = heads(v, D_HV)
    h, C, n, m = mlstm(qh, kh, vh, ig, lf, C0.astype(f32), n0.astype(f32), m0.astype(f32))
    h = rms_norm(jnp.transpose(h, (0, 2, 1, 3)), g_head.reshape(N_MHEADS, D_HV))
    y_b = (jax.nn.sigmoid(o.astype(f32)) * h.reshape(B, L, D_MLSTM)).astype(x.dtype)
    y_a, new_buf = pool_mixer(p, pool_buf, pos0, w_pool, pool_scale)
    y = jnp.concatenate([y_a, y_b], axis=-1) @ w_out
    return x + y, new_buf, C, n, m


def odd_layer(x, g_norm, w_in, ln_g, ln_b, w_s, b_s, w_out):
    B, L, _ = x.shape
    z = jax.nn.gelu(rms_norm(x, g_norm) @ w_in)
    u, v = jnp.split(z, 2, axis=-1)
    v = layer_norm(v, ln_g, ln_b)
    ws = w_s * jnp.tril(jnp.ones((GMLP_CHUNK, GMLP_CHUNK), w_s.dtype))
    vg = v.reshape(B, L, N_SGU_GROUPS, SGU_GROUP)
    if L % GMLP_CHUNK == 0:
        vc = vg.reshape(B, L // GMLP_CHUNK, GMLP_CHUNK, N_SGU_GROUPS, SGU_GROUP)
        s = jnp.einsum('grs,bcsgd->bcrgd', ws, vc) + b_s.T[None, None, :, :, None]
    else:
        s = jnp.einsum('grs,bsgd->brgd', ws[:, :L, :L], vg) + b_s[:, :L].T[None, :, :, None]
    s = s.reshape(B, L, D_SGU)
    y = (u * s) @ w_out
    return x + y, v


def conv_ffn(x, conv_buf, g_norm, w_gate, w_up, conv_w, conv_b, w_down):
    L = x.shape[1]
    h = rms_norm(x, g_norm)
    a = h @ w_gate
    u = h @ w_up
    ext = jnp.concatenate([conv_buf.astype(a.dtype), a], axis=1)
    ac = conv_b + ext[:, 0:L] * conv_w[0]
    for j in range(1, CONV_W):
        ac = ac + ext[:, j:j + L] * conv_w[j]
    y = (jax.nn.gelu(ac) * u) @ w_down
    return x + y, ext[:, -(CONV_W - 1):]


def trunk(x, pool_buf, m_c, m_n, m_m, conv_buf, pos0,
          g_mix, w_in_even, b_gates, w_pool, pool_scale, g_head, w_out_even,
          w_in_odd, ln_v_g, ln_v_b, w_spatial, b_spatial, w_out_odd,
          g_ffn, w_ffn_gate, w_ffn_up, conv_w, conv_b, w_ffn_down, g_final):
    pools, cs, ns, ms, convs, vs = [], [], [], [], [], []
    for l in range(DEPTH):
        if l % 2 == 0:
            e = l // 2
            x, pb, C, n, m = even_layer(x, pool_buf[e], m_c[e], m_n[e], m_m[e], pos0, g_mix[l],
                                        w_in_even[e], b_gates[e], w_pool[e], pool_scale[e], g_head[e], w_out_even[e])
            pools.append(pb); cs.append(C); ns.append(n); ms.append(m)
        else:
            o = l // 2
            x, v = odd_layer(x, g_mix[l], w_in_odd[o], ln_v_g[o], ln_v_b[o], w_spatial[o], b_spatial[o], w_out_odd[o])
            vs.append(v)
        x, cb = conv_ffn(x, conv_buf[l], g_ffn[l], w_ffn_gate[l], w_ffn_up[l], conv_w[l], conv_b[l], w_ffn_down[l])
        convs.append(cb)
    y = rms_norm(x, g_final)
    return (y, jnp.stack(pools), jnp.stack(cs), jnp.stack(ns), jnp.stack(ms), jnp.stack(convs), jnp.stack(vs))


def setup_inputs(seed: int = 0) -> dict:
    key = jax.random.key(seed)
    ks = iter(jax.random.split(key, 40))
    nrm = lambda shape, s=1.0: s * jax.random.normal(next(ks), shape, jnp.float32)
    gain = lambda shape: 1.0 + nrm(shape, 0.05)
    b_gates = jnp.concatenate([nrm((N_EVEN, N_MHEADS), 0.1),
                               3.0 + nrm((N_EVEN, N_MHEADS), 0.1)], axis=-1)
    return {
        "x_prompt": nrm((BATCH, SEQ, D_MODEL)),
        "x_sample": nrm((DEC_BATCH, DEC_SEQ, D_MODEL)),
        "state_pool": nrm((N_EVEN, DEC_BATCH, POOL_BUF, D_POOL)),
        "state_mlstm_c": nrm((N_EVEN, DEC_BATCH, N_MHEADS, D_QK, D_HV), 0.1),
        "state_mlstm_n": nrm((N_EVEN, DEC_BATCH, N_MHEADS, D_QK), 0.1),
        "state_mlstm_m": nrm((N_EVEN, DEC_BATCH, N_MHEADS)),
        "state_ffn_conv": nrm((DEPTH, DEC_BATCH, CONV_W - 1, D_FF)),
        "g_mix": gain((DEPTH, D_MODEL)),
        "w_in_even": nrm((N_EVEN, D_MODEL, D_IN_EVEN), D_MODEL ** -0.5),
        "b_gates": b_gates,
        "w_pool": nrm((N_EVEN, N_POOL_GROUPS, POOL_GROUP, POOL_GROUP), POOL_GROUP ** -0.5),
        "pool_scale": gain((N_EVEN, D_POOL)),
        "g_head": gain((N_EVEN, D_MLSTM)),
        "w_out_even": nrm((N_EVEN, D_MIX_EVEN, D_MODEL), D_MIX_EVEN ** -0.5),
        "w_in_odd": nrm((N_ODD, D_MODEL, 2 * D_SGU), D_MODEL ** -0.5),
        "ln_v_g": gain((N_ODD, D_SGU)),
        "ln_v_b": nrm((N_ODD, D_SGU), 0.02),
        "w_spatial": nrm((N_ODD, N_SGU_GROUPS, GMLP_CHUNK, GMLP_CHUNK), GMLP_CHUNK ** -0.5),
        "b_spatial": 1.0 + nrm((N_ODD, N_SGU_GROUPS, GMLP_CHUNK), 0.1),
        "w_out_odd": nrm((N_ODD, D_SGU, D_MODEL), D_SGU ** -0.5),
        "g_ffn": gain((DEPTH, D_MODEL)),
        "w_ffn_gate": nrm((DEPTH, D_MODEL, D_FF), D_MODEL ** -0.5),
        "w_ffn_up": nrm((DEPTH, D_MODEL, D_FF), D_MODEL ** -0.5),
        "conv_w": nrm((DEPTH, CONV_W, D_FF), CONV_W ** -0.5),
        "conv_b": nrm((DEPTH, D_FF), 0.02),
        "w_ffn_down": nrm((DEPTH, D_FF, D_MODEL), D_FF ** -0.5),
        "g_final": gain((D_MODEL,)),
    }


def reference(x_prompt, x_sample, state_pool, state_mlstm_c, state_mlstm_n, state_mlstm_m, state_ffn_conv,
              g_mix, w_in_even, b_gates, w_pool, pool_scale, g_head, w_out_even,
              w_in_odd, ln_v_g, ln_v_b, w_spatial, b_spatial, w_out_odd,
              g_ffn, w_ffn_gate, w_ffn_up, conv_w, conv_b, w_ffn_down, g_final):
    weights = (g_mix, w_in_even, b_gates, w_pool, pool_scale, g_head, w_out_even,
               w_in_odd, ln_v_g, ln_v_b, w_spatial, b_spatial, w_out_odd,
               g_ffn, w_ffn_gate, w_ffn_up, conv_w, conv_b, w_ffn_down, g_final)
    B = x_prompt.shape[0]
    f32 = jnp.float32
    y_p, pool_p, c_p, n_p, m_p, conv_p, _ = trunk(
        x_prompt,
        jnp.zeros((N_EVEN, B, POOL_BUF, D_POOL), x_prompt.dtype),
        jnp.zeros((N_EVEN, B, N_MHEADS, D_QK, D_HV), f32),
        jnp.zeros((N_EVEN, B, N_MHEADS, D_QK), f32),
        jnp.zeros((N_EVEN, B, N_MHEADS), f32),
        jnp.zeros((DEPTH, B, CONV_W - 1, D_FF), x_prompt.dtype),
        0, *weights)
    y_s, pool_s, c_s, n_s, m_s, conv_s, v_s = trunk(
        x_sample, state_pool, state_mlstm_c, state_mlstm_n, state_mlstm_m, state_ffn_conv,
        PAST_LEN, *weights)
    return (y_p, y_s, pool_p, pool_s, c_p, c_s, n_p, n_s, m_p, m_s, conv_p, conv_s, v_s)
```

```python
import numpy as np
import concourse.bass as bass
import concourse.mybir as mybir

F32 = mybir.dt.float32
BF16 = mybir.dt.bfloat16
I32 = mybir.dt.int32
AF = mybir.ActivationFunctionType
ALU = mybir.AluOpType
AX = mybir.AxisListType

_DTB = {F32: 4, BF16: 2, I32: 4}


def _rect(ap):
    t = ap.tensor
    name = t.name
    pat = list(ap.ap)
    esz = _DTB.get(ap.dtype, 4)
    space = str(ap.space) if hasattr(ap, "space") else ""
    if "DRAM" in space.upper() or "HBM" in space.upper() or type(t).__name__.startswith("DRam"):
        lo = ap.offset
        hi = lo
        for st, cn in pat:
            hi += abs(st) * (cn - 1)
        return (name, 0, 1, lo * esz, (hi + 1) * esz)
    if name.startswith("ps") and name[2:].isdigit():
        return (name, 0, 128, 0, 2048)
    tsz = _DTB.get(t.dtype, 4)
    rowsz = 1
    for s in list(t.shape)[1:]:
        rowsz *= s
    rowb = rowsz * tsz
    offb = ap.offset * esz
    pstep, pcnt = pat[0]
    p0 = offb // rowb
    f0 = offb % rowb
    ext = 0
    for st, cn in pat[1:]:
        ext += abs(st) * (cn - 1)
    f1 = f0 + (ext + 1) * esz
    if pstep == 0:
        p1 = p0 + 1
    else:
        p1 = p0 + pcnt
    return (name, p0, p1, f0, f1)


class Slot:
    def __init__(self, sem, idx):
        self.sem = sem
        self.idx = idx
        self.count = 0


class Sched:
    ENG = ["pe", "act", "dve", "pool", "sp"]

    def __init__(self, nc, es, same_engine_sync=True):
        self.nc = nc
        self.es = es
        self.eng = {"pe": nc.tensor, "act": nc.scalar, "dve": nc.vector, "pool": nc.gpsimd, "sp": nc.sync}
        self.sem = {e: es.enter_context(nc.semaphore("sem_" + e)) for e in self.ENG}
        self.cnt = {e: 0 for e in self.ENG}
        self.know = {e: {f: 0 for f in self.ENG} for e in self.ENG}
        self.dknow = {e: {} for e in self.ENG}
        self.clock = {}
        self.track = {}
        self.prog = {e: [] for e in self.ENG}
        self.slots = []
        self.same_engine_sync = same_engine_sync
        self.n_wait = 0

    def slot(self, name=None):
        i = len(self.slots)
        s = Slot(self.es.enter_context(self.nc.semaphore(name or ("dsem%d" % i))), i)
        self.slots.append(s)
        return s

    def _collect(self, reads, writes):
        deps = []
        for ap in reads:
            name, p0, p1, f0, f1 = _rect(ap)
            for ent in self.track.get(name, ()):
                if ent[0] < p1 and p0 < ent[1] and ent[2] < f1 and f0 < ent[3]:
                    if ent[4] is not None:
                        deps.append(ent[4])
        for ap in writes:
            name, p0, p1, f0, f1 = _rect(ap)
            for ent in self.track.get(name, ()):
                if ent[0] < p1 and p0 < ent[1] and ent[2] < f1 and f0 < ent[3]:
                    if ent[4] is not None:
                        deps.append(ent[4])
                    for e, k in ent[5].items():
                        deps.append(("c", e, k))
                    deps.extend(ent[6])
        return deps

    def _record(self, ev, reads, writes):
        for ap in writes:
            name, p0, p1, f0, f1 = _rect(ap)
            lst = self.track.setdefault(name, [])
            lst[:] = [en for en in lst if not (p0 <= en[0] and en[1] <= p1 and f0 <= en[2] and en[3] <= f1)]
            lst.append([p0, p1, f0, f1, ev, {}, []])
        for ap in reads:
            name, p0, p1, f0, f1 = _rect(ap)
            lst = self.track.setdefault(name, [])
            found = None
            for en in lst:
                if en[0] == p0 and en[1] == p1 and en[2] == f0 and en[3] == f1:
                    found = en
                    break
            if found is None:
                found = [p0, p1, f0, f1, None, {}, []]
                lst.append(found)
            if ev[0] == "c":
                if found[5].get(ev[1], 0) < ev[2]:
                    found[5][ev[1]] = ev[2]
            else:
                found[6].append(ev)

    def _waits_for(self, e, deps):
        need_c = {}
        need_d = {}
        for ev in deps:
            if ev[0] == "c":
                _, f, k = ev
                if f == e and not self.same_engine_sync:
                    continue
                if self.know[e][f] < k and need_c.get(f, 0) < k:
                    need_c[f] = k
            else:
                _, si, val = ev
                val = self.slots[si].count
                if self.dknow[e].get(si, 0) < val and need_d.get(si, 0) < val:
                    need_d[si] = val
        waits = []
        for f, k in need_c.items():
            if self.know[e][f] >= k:
                continue
            waits.append((self.sem[f], k))
            ck = self.clock[(f, k)]
            kn = self.know[e]
            for g, v in ck.items():
                if kn[g] < v:
                    kn[g] = v
        for si, val in need_d.items():
            waits.append((self.slots[si].sem, val))
            self.dknow[e][si] = val
        self.n_wait += len(waits)
        return waits

    def op(self, e, fn, reads=(), writes=()):
        deps = self._collect(reads, writes)
        waits = self._waits_for(e, deps)
        idx = self.cnt[e] + 1
        self.cnt[e] = idx
        ck = dict(self.know[e])
        ck[e] = idx
        self.clock[(e, idx)] = ck
        ev = ("c", e, idx)
        self._record(ev, reads, writes)
        sem_e = self.sem[e]
        if not hasattr(self, "_psrd"):
            self._psrd = {}
            self.ps_multi = []
        for a in writes:
            nm = a.tensor.name
            if nm.startswith("ps") and nm[2:].isdigit() and e == "pe":
                self._psrd[nm] = set()
        for a in reads:
            nm = a.tensor.name
            if nm.startswith("ps") and nm[2:].isdigit():
                st_ = self._psrd.setdefault(nm, set())
                st_.add(e)
                if len(st_) > 1:
                    import traceback as _tb2
                    fr_ = _tb2.extract_stack(limit=5)
                    self.ps_multi.append((nm, sorted(st_), " <- ".join("%s:%d" % (f.name, f.lineno) for f in fr_[:-1][::-1])))
        if not hasattr(self, "oplog"):
            self.oplog = []
        import traceback as _tb
        fr = _tb.extract_stack(limit=4)
        self.oplog.append((e, idx, [(str(getattr(s_, "name", s_)), v) for s_, v in waits], " <- ".join("%s:%d" % (f.name, f.lineno) for f in fr[:-1][::-1]), [_rect(a) for a in writes]))

        def run(engine, waits=waits, fn=fn, sem_e=sem_e):
            for s, v in waits:
                engine.wait_ge(s, v)
            inst = fn(engine)
            inst.then_inc(sem_e, 1)

        self.prog[e].append(run)
        return ev

    def auto_slot(self, q):
        key = "sw" if q == "pool" else "hw"
        if not hasattr(self, "_auto"):
            self._auto = {"hw": [self.slot("asem%d" % i) for i in range(24)], "sw": [self.slot("swsem%d" % i) for i in range(8)]}
            self._autoi = {"hw": 0, "sw": 0}
        lst = self._auto[key]
        s = lst[self._autoi[key] % len(lst)]
        self._autoi[key] += 1
        return s

    def dma(self, q, slot, out, in_, track_out=True, track_in=True, new_group=True, **kw):
        if slot is None:
            slot = self.auto_slot(q)
        reads = [in_] if track_in else []
        writes = [out] if track_out else []
        deps = self._collect(reads, writes)
        if new_group and slot.count > 0:
            deps.append(("d", slot.idx, slot.count))
        waits = self._waits_for(q, deps)
        slot.count += 16
        ev = ("d", slot.idx, slot.count)
        self._record(ev, reads, writes)
        sem = slot.sem

        def run(engine, waits=waits, sem=sem, out=out, in_=in_, kw=kw):
            for s, v in waits:
                engine.wait_ge(s, v)
            engine.dma_start(out=out, in_=in_, **kw).then_inc(sem, 16)

        self.prog[q].append(run)
        return ev

    def wait_all_dma(self, q, slots):
        waits = [(s.sem, s.count) for s in slots if s.count > 0]

        def run(engine, waits=waits):
            for s, v in waits:
                engine.wait_ge(s, v)

        self.prog[q].append(run)

    def finish(self):
        nc = self.nc
        with nc.Block() as block:
            @block.tensor
            def _(eng):
                for f in self.prog["pe"]:
                    f(eng)

            @block.scalar
            def _(eng):
                for f in self.prog["act"]:
                    f(eng)

            @block.vector
            def _(eng):
                for f in self.prog["dve"]:
                    f(eng)

            @block.gpsimd
            def _(eng):
                for f in self.prog["pool"]:
                    f(eng)

            @block.sync
            def _(eng):
                for f in self.prog["sp"]:
                    f(eng)


def even_layer(ctx, e, l):
    g_ = ctx
    S, AR, PS, nps, D, X = g_["S"], g_["AR"], g_["PS"], g_["nps"], g_["D"], g_["X"]
    act, tt, ts, stt, cp, memset = g_["act"], g_["tt"], g_["ts"], g_["stt"], g_["cp"], g_["memset"]
    mmg, mms, trs, rmsnorm, xadd = g_["mmg"], g_["mms"], g_["trs"], g_["rmsnorm"], g_["xadd"]
    ring_next, wload, ld, st = g_["ring_next"], g_["wload"], g_["ld"], g_["st"]
    identF, identB, causal, onesF, selH, bmask, bones = g_["identF"], g_["identB"], g_["causal"], g_["onesF"], g_["selH"], g_["bmask"], g_["bones"]
    invc, gmix, pscale, bgate, nbf, NCD = g_["invc"], g_["gmix"], g_["pscale"], g_["bgate"], g_["nbf"], g_["NCD"]
    Win = D["w_in_even"][e]
    Wout = D["w_out_even"][e]

    H = AR.alloc("H", [128, 8, NT_], BF16)
    rmsnorm(gmix[:, l, :], H)

    def interleave(*gens):
        gens = list(gens)
        while gens:
            for g in list(gens):
                try:
                    next(g)
                except StopIteration:
                    gens.remove(g)

    GV = {}

    def gates_gen():
        ones4 = mkap(onesF, 0, [4, [0, 512]])
        wgt = AR.alloc("wgt", [128, 8, 8], BF16)
        S.dma("pool", ld, wgt, Win[:, 2560:2568].rearrange("(c p) f -> p c f", p=128), **NCD)
        yield
        Q_B, Q_G, Q_M, Q_R, Q_E = range(5)
        TM = AR.alloc("TM", [128, 5, 17, 4], F32)
        DR = AR.alloc("DR", [128, 5, 17, 4], F32)
        BC = AR.alloc("BC", [128, 4, 48], F32)
        ALs = AR.alloc("ALs", [128, 4, 16], F32)
        rows = {n: AR.alloc("row_" + n, [4, 512], F32) for n in ("IG", "LF", "B", "G", "M")}
        carry = AR.alloc("carry", [4, 2], F32)
        BCs = AR.alloc("BCs", [4, 48], F32)
        MLo = AR.alloc("MLo", [4, 17], F32)
        reps = AR.alloc("reps", [4, 2, 64], F32)
        memset("dve", carry, 0.0)
        yield
        S.dma("sp", ld, BCs[:, 32:48], D["sm"][e].rearrange("s h -> h s"), **NCD)
        yield
        psT = PS[6]
        memset("dve", psT[:, :], 0.0)
        yield
        bI = bgate[:, e, 0:1]
        nbF = nbf[:, e:e + 1]
        for ti, (t0, tn) in enumerate(TT):
            psI = nps()
            mmg(psI[0:4, 0:tn], [(wgt[:, dc, 0:4], H[:, dc, t0:t0 + tn]) for dc in range(8)])
            yield
            psF = nps()
            mmg(psF[0:4, 0:tn], [(wgt[:, dc, 4:8], H[:, dc, t0:t0 + tn]) for dc in range(8)])
            yield
            IG, LF, Br, Gr, Mr = (rows[n] for n in ("IG", "LF", "B", "G", "M"))
            act(IG[:, 0:tn], psI[0:4, 0:tn], AF.Identity, bias=bI, scale=1.0)
            yield
            act(LF[:, 0:tn], psF[0:4, 0:tn], AF.Exp, bias=nbF, scale=-1.0)
            yield
            act(LF[:, 0:tn], LF[:, 0:tn], AF.Ln, bias=1.0, scale=1.0)
            yield
            ts("dve", LF[:, 0:tn], LF[:, 0:tn], -1.0, ALU.mult)
            yield
            if ti < 4:
                S.op("dve", lambda en, Br=Br, LF=LF: en.tensor_tensor_scan(out=Br, data0=ones4, data1=LF, initial=carry[:, 0:1], op0=ALU.mult, op1=ALU.add),
                     reads=[ones4, LF, carry[:, 0:1]], writes=[Br])
                yield
                tt("dve", Gr, IG, Br, ALU.subtract)
                yield
                S.op("dve", lambda en, Mr=Mr, Gr=Gr: en.tensor_tensor_scan(out=Mr, data0=Gr, data1=Gr, initial=carry[:, 1:2], op0=ALU.max, op1=ALU.max),
                     reads=[Gr, carry[:, 1:2]], writes=[Mr])
                yield
                cp("dve", carry[:, 0:1], Br[:, 511:512])
                yield
                cp("dve", carry[:, 1:2], Mr[:, 511:512])
                yield
                cp("dve", BCs[:, 4 * ti:4 * ti + 4], mkap(Mr, 127, [4, [128, 4]]))
                yield
                if ti == 3:
                    tt("dve", MLo[:, 0:1], Br[:, 511:512], Mr[:, 511:512], ALU.add)
                    yield
                items = []
                for j in range(4):
                    c = 4 * ti + j
                    for q, R_ in ((Q_B, Br), (Q_G, Gr), (Q_M, Mr)):
                        col = (q * 17 + c) * 4
                        items.append((psT[:, col:col + 4], R_[:, 128 * j:128 * j + 128], identF[0:4, 0:4]))
                trs(items)
                yield
            else:
                v3 = lambda R_: R_[:, 0:64].rearrange("h (s t) -> h s t", t=4)
                B3, G3, M3, L3 = v3(Br), v3(Gr), v3(Mr), v3(LF)
                cp("dve", B3[:, :, 0], L3[:, :, 0])
                yield
                for t in range(1, 4):
                    tt("dve", B3[:, :, t], B3[:, :, t - 1], L3[:, :, t], ALU.add)
                    yield
                tt("dve", Gr[:, 0:64], IG[:, 0:64], Br[:, 0:64], ALU.subtract)
                yield
                tt("dve", M3[:, :, 0], BCs[:, 32:48], G3[:, :, 0], ALU.max)
                yield
                for t in range(1, 4):
                    tt("dve", M3[:, :, t], M3[:, :, t - 1], G3[:, :, t], ALU.max)
                    yield
                cp("dve", BCs[:, 16:32], M3[:, :, 3])
                yield
                tt("dve", MLo[:, 1:17], B3[:, :, 3], M3[:, :, 3], ALU.add)
                yield
                cp("dve", reps[:, 0, :].rearrange("h (s t) -> h s t", t=4), mkap(BCs, 32, [4, [1, 16], [0, 4]]))
                yield
                cp("dve", reps[:, 1, :].rearrange("h (s t) -> h s t", t=4), mkap(BCs, 16, [4, [1, 16], [0, 4]]))
                yield
                items = []
                for q, R_ in ((Q_B, Br[:, 0:64]), (Q_G, Gr[:, 0:64]), (Q_M, Mr[:, 0:64]), (Q_R, reps[:, 0, :]), (Q_E, reps[:, 1, :])):
                    col = (q * 17 + 16) * 4
                    items.append((psT[0:64, col:col + 4], R_, identF[0:4, 0:4]))
                trs(items)
                yield
        cp("dve", TM[:, :, :, :].rearrange("p q c h -> p (q c h)"), psT[:, 0:340])
        yield
        S.dma("sp", st, D["om_p"][e:e + 1, :].rearrange("a h -> h a"), MLo[:, 0:1], **NCD)
        yield
        S.dma("sp", st, D["om_s"][e].rearrange("s h -> h s"), MLo[:, 1:17], **NCD)
        yield
        psB = nps()
        mms([(psB[:, 48 * h:48 * h + 48], selH[:, h, :], BCs[:, :]) for h in range(4)])
        yield
        cp("dve", BC[:, :, :].rearrange("p h c -> p (h c)"), psB[:, 0:192])
        yield
        memset("dve", TM[:, Q_R, 0, :], 0.0)
        yield
        cp("dve", TM[:, Q_R, 1:16, :], mkap(BC, 0, [128, [1, 15], [48, 4]]))
        yield
        cp("dve", TM[:, Q_E, 0:16, :], mkap(BC, 0, [128, [1, 16], [48, 4]]))
        yield
        fl = lambda v: v.rearrange("p c h -> p (c h)")
        tmp = AR.alloc("tmpd", [128, 68], F32)
        tt("dve", tmp, fl(TM[:, Q_G]), fl(TM[:, Q_R]), ALU.subtract)
        yield
        act(fl(DR[:, 0]), tmp, AF.Exp, bias=LNKS_AP(g_), scale=1.0)
        yield
        tmp2 = AR.alloc("tmpd2", [128, 68], F32)
        tt("dve", tmp2, fl(TM[:, Q_R]), fl(TM[:, Q_M]), ALU.subtract)
        yield
        act(fl(DR[:, 1]), tmp2, AF.Exp)
        yield
        tmp3 = tmp
        tt("dve", tmp3, fl(TM[:, Q_G]), fl(TM[:, Q_E]), ALU.subtract)
        yield
        act(fl(DR[:, 2]), tmp3, AF.Exp, bias=LNKS_AP(g_), scale=1.0)
        yield
        tmp4 = tmp2
        tt("dve", tmp4, fl(TM[:, Q_B]), fl(TM[:, Q_M]), ALU.add)
        yield
        act(fl(DR[:, 3]), tmp4, AF.Exp, scale=-1.0)
        yield
        tmp5 = tmp
        tt("dve", tmp5, fl(TM[:, Q_R]), fl(TM[:, Q_E]), ALU.subtract)
        yield
        act(fl(DR[:, 4]), tmp5, AF.Exp)
        yield
        tmp6 = AR.alloc("tmpd6", [128, 4, 16], F32)
        tt("dve", tmp6, BC[:, :, 32:48], BC[:, :, 16:32], ALU.subtract)
        yield
        act(ALs, tmp6, AF.Exp)
        yield
        AR.release("tmpd", "tmpd2", "tmpd6", "wgt", "carry", "BCs", "MLo", "reps",
                   "row_IG", "row_LF", "row_B", "row_G", "row_M")
        GV.update(DR=DR, ALs=ALs)

    def pool_gen():
        RUN = 15 + 2048
        RT = RUN + 16 * 19
        rp = ring_next()
        wp = rp[0][:, :].rearrange("p (c f) -> p c f", c=8)
        wload(rp, wp, Win[:, 0:512].rearrange("(c p) f -> p c f", p=128))
        yield
        wpl = AR.alloc("wpl", [128, 4, 128], BF16)
        S.dma("pool", ld, wpl, D["w_pool"][e].rearrange("g c d -> c g d"))
        yield
        YA = AR.alloc("YA", [128, 4, NT_], BF16)
        PT = AR.alloc("PT", [128, RT], F32)
        SA = AR.alloc("SA", [128, RT], F32)
        SB = AR.alloc("SB", [128, RT], F32)
        Dd = AR.alloc("Dd", [128, NT_], BF16)
        memset("pool", SA, 0.0)
        yield
        memset("pool", SB, 0.0)
        yield
        SPL = [AR.alloc("SPL%d" % i, [120, 512], F32) for i in range(2)]
        for hf in range(2):
            S.dma("sp", ld, SPL[hf], D["spool"][e, 8 * hf:8 * hf + 8].rearrange("s r f -> (s r) f"))
            yield
        sview = lambda buf: mkap(buf, RUN, [128, [19, 16], [1, 19]])
        for g in range(4):
            w = 2 ** (g + 1)
            memset("pool", PT[:, 0:15], 0.0)
            yield
            for hf in range(2):
                ps = nps()
                trs([(ps[:, 0:120], SPL[hf][:, g * 128:(g + 1) * 128], identF[0:120, 0:120])])
                yield
                cp("dve", sview(PT)[:, 8 * hf:8 * hf + 8, 0:15], ps[:, 0:120].rearrange("p (s r) -> p s r", r=15))
                yield
            for ti, (t0, tn) in enumerate(TT):
                ps = nps()
                mmg(ps[:, 0:tn], [(wp[:, dc, g * 128:(g + 1) * 128], H[:, dc, t0:t0 + tn]) for dc in range(8)])
                yield
                if ti < 4:
                    act(PT[:, 15 + t0:15 + t0 + 512], ps[:, 0:512], AF.Copy)
                    yield
                else:
                    act(sview(PT)[:, :, 15:19], ps[:, 0:64].rearrange("p (s t) -> p s t", t=4), AF.Copy)
                    yield
            src = PT
            bufs = [SA, SB]
            for k in range(g + 1):
                sh = 2 ** k
                dst = bufs[k % 2]
                tt("dve", dst[:, sh:RT], src[:, sh:RT], src[:, 0:RT - sh], ALU.add)
                yield
                src = dst
            stt(Dd[:, 0:2048], src[:, 15:RUN], 1.0 / w, PT[:, 15:RUN], ALU.mult, ALU.subtract)
            yield
            fx = AR.alloc("fx", [128, 16], F32)
            tt("dve", fx, src[:, 15:31], invc[:, g, :], ALU.mult)
            yield
            tt("dve", Dd[:, 0:16], fx, PT[:, 15:31], ALU.subtract)
            yield
            AR.release("fx")
            stt(Dd[:, 2048:2112].rearrange("p (s t) -> p s t", t=4), sview(src)[:, :, 15:19], 1.0 / w, sview(PT)[:, :, 15:19], ALU.mult, ALU.subtract)
            yield
            for ti, (t0, tn) in enumerate(TT):
                ps = nps()
                mmg(ps[:, 0:tn], [(wpl[:, g, :], Dd[:, t0:t0 + tn])])
                yield
                act(YA[:, g, t0:t0 + tn], ps[:, 0:tn], AF.Identity, bias=0.0, scale=pscale[:, e, g:g + 1])
                yield
        PO = AR.alloc("PO", [64, 512], F32)
        ps = nps()
        mmg(ps[0:16, :], [(H[:, dc, 2032:2048], wp[:, dc, :]) for dc in range(8)])
        yield
        cp("dve", PO[0:16, :], ps[0:16, :])
        yield
        S.dma("sp", st, D["opool_p"][e], PO[1:16, :])
        yield
        PO2 = AR.alloc("PO2", [64, 512], F32)
        ps = nps()
        mmg(ps[0:64, :], [(H[:, dc, 2048:2112], wp[:, dc, :]) for dc in range(8)])
        yield
        cp("dve", PO2[0:64, :], ps[0:64, :])
        yield
        for s_ in range(16):
            S.dma("sp", st, D["opool_s"][e, s_, 11:15, :], PO2[4 * s_:4 * s_ + 4, :])
            yield
        S.dma("sp", st, D["opool_s"][e, :, 0:11, :], D["spool"][e, :, 4:15, :], track_in=False)
        yield
        ra = ring_next()
        woa = ra[0][:, :].rearrange("p (c f) -> p c f", c=4)
        wload(ra, woa, Wout[0:512, :].rearrange("(c p) f -> p c f", p=128))
        yield
        for ti, (t0, tn) in enumerate(TT):
            for dc in range(8):
                ps = nps()
                mmg(ps[:, 0:tn], [(woa[:, g, dc * 128:(dc + 1) * 128], YA[:, g, t0:t0 + tn]) for g in range(4)])
                yield
                xadd(dc, t0, tn, ps)
                yield
        AR.release("YA", "PT", "SA", "SB", "Dd", "SPL0", "SPL1", "PO", "PO2", "wpl")


    WH = {}

    def load_head_w(h):
        rh = ring_next()
        wh = rh[0][:, :].rearrange("p (c j f) -> p c j f", c=8, j=4)
        for j, cbase in enumerate((512, 1024, 1536, 2048)):
            wload(rh, wh[:, :, j, :], Win[:, cbase + 128 * h:cbase + 128 * h + 128].rearrange("(c p) f -> p c f", p=128), new_group=(j == 0))
        WH[h] = wh

    def load_wob():
        rb = ring_next()
        wob = rb[0][:, :].rearrange("p (c f) -> p c f", c=4)
        wload(rb, wob, Wout[512:1024, :].rearrange("(c p) f -> p c f", p=128))
        WH["wob"] = wob

    load_head_w(0)
    interleave(gates_gen(), pool_gen())
    DR, ALs = GV["DR"], GV["ALs"]
    EG, SC, WS, EMT, AL = DR[:, 0], DR[:, 1], DR[:, 2], DR[:, 3], DR[:, 4]

    YB = AR.alloc("YB", [128, 4, NT_], BF16)
    PPb_ = [AR.alloc("PPb%d" % i, [128, 17, 128], BF16) for i in range(2)]
    PPd_ = [AR.alloc("PPd%d" % i, [128, 17], F32) for i in range(2)]
    SO_ = [AR.alloc("SO%d" % i, [128, 17, 128], BF16) for i in range(2)]
    SSQ_ = [AR.alloc("SSQ%d" % i, [128, 17], F32) for i in range(2)]
    junk = AR.alloc("junk", [128, 128], BF16)
    QT = [AR.alloc("QT%d" % i, [128, 128], BF16) for i in range(3)]
    KT = [AR.alloc("KT%d" % i, [128, 128], BF16) for i in range(2)]
    KW = [AR.alloc("KW%d" % i, [128, 128], BF16) for i in range(2)]
    Vc = [AR.alloc("Vc%d" % i, [128, 129], BF16) for i in range(2)]
    Sp = [AR.alloc("Sp%d" % i, [128, 128], BF16) for i in range(2)]
    OF = [AR.alloc("OF%d" % i, [128, 128], F32) for i in range(2)]
    Cst = AR.alloc("Cst", [128, 129], F32)
    Cb = AR.alloc("Cb", [128, 129], BF16)
    C0 = AR.alloc("C0", [128, 16, 129], F32)
    C0b = AR.alloc("C0b", [128, 16, 129], BF16)
    Vblk = AR.alloc("Vblk", [64, 16, 129], BF16)
    QZ = AR.alloc("QZ", [128, 16, 64], BF16)
    GH = AR.alloc("GH", [128, 512], F32)
    NT0 = AR.alloc("NT0", [128, 64], F32)
    NTO = AR.alloc("NTO", [128, 4, 16], F32)
    NPO = AR.alloc("NPO", [128, 4], F32)
    post = AR.alloc("post", [128, 6, 17], F32)
    S.dma("sp", ld, GH, D["g_head"][e:e + 1, :].to_broadcast([128, 512]))
    for i in range(2):
        memset("pool", Vc[i][:, 128:129], 1.0)
    for i in range(2):
        memset("dve", PPb_[i], 0.0)
        memset("dve", PPd_[i], 0.0)
        memset("dve", SO_[i], 0.0)
        memset("dve", SSQ_[i], 0.0)
    memset("dve", QZ, 0.0)
    nrow = AR.alloc("nrow", [64, 128], F32)
    S.dma("sp", ld, nrow, D["sn"][e].rearrange("s h k -> (s h) k"))
    ps = nps()
    trs([(ps[:, 0:64], nrow, identF[0:64, 0:64])])
    cp("dve", NT0, ps[:, 0:64])
    AR.release("nrow")
    k = 0

    def head_gen(h):
        PPb, PPd, SO, SSQ = PPb_[h % 2], PPd_[h % 2], SO_[h % 2], SSQ_[h % 2]
        wh = WH[h]
        memset("pool", Cst, 0.0)
        memset("pool", Cb, 0.0)
        def stA(c):
            t0, tn = CH[c]
            qt, kt = QT[c % 3], KT[c % 2]
            psQ = nps()
            mmg(psQ[:, 0:tn], [(wh[:, dc, 0, :], H[:, dc, t0:t0 + tn]) for dc in range(8)])
            psK = nps()
            mmg(psK[:, 0:tn], [(wh[:, dc, 1, :], H[:, dc, t0:t0 + tn]) for dc in range(8)])
            act(qt[:, 0:tn], psQ[:, 0:tn], AF.Copy)
            cp("dve", kt[:, 0:tn], psK[:, 0:tn])

        def stB(c):
            t0, tn = CH[c]
            qt, kt, kw, vc, sp = QT[c % 3], KT[c % 2], KW[c % 2], Vc[c % 2], Sp[c % 2]
            psS = nps()
            mms([(psS[0:tn, 0:tn], kt[:, 0:tn], qt[:, 0:tn])])
            ps3 = nps()
            mmg(ps3[0:tn, 0:384], [(H[:, dc, t0:t0 + tn], wh[:, dc, 1:4, :].rearrange("p j f -> p (j f)")) for dc in range(8)])
            mask = causal if c < 16 else bmask[:, :, :].rearrange("p s r -> p (s r)")
            stt(sp[0:tn, 0:tn], psS[0:tn, 0:tn], EG[0:tn, c, h:h + 1], mask[0:tn, 0:tn], ALU.mult, ALU.mult)
            ts("dve", kw[0:tn, :], ps3[0:tn, 0:128], WS[0:tn, c, h:h + 1], ALU.mult)
            cp("dve", vc[0:tn, 0:128], ps3[0:tn, 128:256])
            of = OF[c % 2]
            cp("dve", of[0:tn, :], ps3[0:tn, 256:384])
            act(SO[0:tn, c, :], of[0:tn, :], AF.Sigmoid)

        def stC(c):
            t0, tn = CH[c]
            qt, kw, vc, sp = QT[c % 3], KW[c % 2], Vc[c % 2], Sp[c % 2]
            if 1 <= c < 16:
                cp("act", Cb, Cst)
            psP = nps()
            if c < 16:
                psU = nps()
                mms([(psU[:, 0:129], kw[:, :], vc[:, :])])
                mmg(psP[:, 0:129], [(qt[:, :], Cb[:, :]), (sp[:, :], vc[:, :])])
            else:
                cp("pool", mkap(QZ, 0, [128, [68, 16], [1, 4]]), qt[:, 0:64].rearrange("p (s t) -> p s t", t=4))
                mmg(psP[0:64, 0:129], [(QZ[:, s_, :], C0b[:, s_, :]) for s_ in range(16)] + [(sp[0:64, 0:64], vc[0:64, :])])
            act(PPb[0:tn, c, :], psP[0:tn, 0:128], AF.Copy)
            act(PPd[0:tn, c:c + 1], psP[0:tn, 128:129], AF.Copy)
            act(junk[0:tn, :], psP[0:tn, 0:128], AF.Square, accum=SSQ[0:tn, c:c + 1])
            if c < 16:
                stt(Cst, Cst, AL[:, c, h:h + 1], psU[:, 0:129], ALU.mult, ALU.add)
            else:
                tt("pool", Vblk, mkap(vc, 0, [64, [0, 16], [1, 129]]), mkap(bones, 0, [64, [1, 16], [0, 129]]), ALU.mult)
                for s0 in range(0, 16, 3):
                    ns = min(3, 16 - s0)
                    psU2 = nps()
                    mms([(psU2[:, 0:ns * 129], kw[0:64, :], Vblk[:, s0:s0 + ns, :].rearrange("p s f -> p (s f)"))])
                    for j in range(ns):
                        s_ = s0 + j
                        stt(C0[:, s_, :], C0[:, s_, :], ALs[:, h, s_:s_ + 1], psU2[:, j * 129:(j + 1) * 129], ALU.mult, ALU.add)
                for s4 in range(0, 16, 4):
                    S.dma("sp", st, D["oc_s"][e, s4:s4 + 4, h, :, :].rearrange("s k v -> k s v"), C0[:, s4:s4 + 4, 0:128])
                cp("dve", NTO[:, h, :], C0[:, :, 128])
            if c == 15:
                S.dma("sp", st, D["oc_p"][e, h], Cst[:, 0:128])
                cp("dve", NPO[:, h:h + 1], Cst[:, 128:129])

        stA(0)
        yield
        for i_ in range(17):
            if i_ == 1:
                if h + 1 < 4:
                    load_head_w(h + 1)
                else:
                    load_wob()
            if i_ == 10:
                for s4 in range(0, 16, 4):
                    S.dma("sp", ld, C0[:, s4:s4 + 4, 0:128], D["sc"][e, s4:s4 + 4, h, :, :].rearrange("s k v -> k s v"))
                cp("dve", C0[:, :, 128], mkap(NT0, h, [128, [4, 16]]))
                cp("pool", C0b, C0)
            if i_ + 1 < 17:
                stA(i_ + 1)
                yield
            stB(i_)
            yield
            if i_ >= 1:
                stC(i_ - 1)
                yield
        stC(16)
        yield

    def post_gen(h):
        PPb, PPd, SO, SSQ = PPb_[h % 2], PPd_[h % 2], SO_[h % 2], SSQ_[h % 2]
        den = PPd
        t1, t2, rr, t4, rsd, tot = (post[:, i, :] for i in range(6))
        tt("dve", t1, den, SC[:, :, h], ALU.mult)
        yield
        act(t1, t1, AF.Abs)
        yield
        tt("dve", t2, t1, EMT[:, :, h], ALU.max)
        yield
        S.op("dve", lambda en, t2=t2: en.reciprocal(out=t2, in_=t2), reads=[t2], writes=[t2])
        yield
        tt("dve", rr, t2, SC[:, :, h], ALU.mult)
        yield
        tt("dve", t4, rr, rr, ALU.mult)
        yield
        tt("dve", t4, t4, SSQ, ALU.mult)
        yield
        act(rsd, t4, AF.Sqrt, bias=EPS, scale=1.0 / 128)
        yield
        S.op("dve", lambda en, rsd=rsd: en.reciprocal(out=rsd, in_=rsd), reads=[rsd], writes=[rsd])
        yield
        tt("dve", tot, rr, rsd, ALU.mult)
        yield
        tt("pool", SO, SO, mkap(GH, 128 * h, [128, [0, 17], [1, 128]]), ALU.mult)
        yield
        for c in range(17):
            tn = CH[c][1]
            stt(SO[0:tn, c, :], PPb[0:tn, c, :], tot[0:tn, c:c + 1], SO[0:tn, c, :], ALU.mult, ALU.mult)
            yield
        for c0 in range(0, 17, 4):
            ncx = min(4, 17 - c0)
            ps = nps()
            psb = ps[:, :].bitcast(BF16)
            items = []
            for j in range(ncx):
                c = c0 + j
                tn = CH[c][1]
                items.append((psb[:, j * 128:j * 128 + tn], SO[0:tn, c, :], identB[0:tn, 0:tn]))
            trs(items)
            yield
            if c0 < 16:
                cp("act", YB[:, h, 128 * c0:128 * c0 + 512], psb[:, 0:512])
                yield
            else:
                cp("act", YB[:, h, 2048:2112], psb[:, 0:64])
                yield

    for h in range(4):
        if h > 0:
            interleave(head_gen(h), post_gen(h - 1))
        else:
            interleave(head_gen(h))
    interleave(post_gen(3))
    ps = nps()
    trs([(ps[0:64, 0:128], NTO[:, :, :].rearrange("p h s -> p (h s)"), identF)])
    nout = AR.alloc("nout", [64, 128], F32)
    cp("dve", nout, ps[0:64, 0:128])
    for h in range(4):
        S.dma("sp", st, D["on_s"][e, :, h, :], nout[16 * h:16 * h + 16, :])
    ps = nps()
    trs([(ps[0:4, 0:128], NPO, identF)])
    nout2 = AR.alloc("nout2", [4, 128], F32)
    cp("dve", nout2, ps[0:4, 0:128])
    S.dma("sp", st, D["on_p"][e], nout2)
    wob = WH["wob"]
    for ti, (t0, tn) in enumerate(TT):
        for dc in range(8):
            ps = nps()
            mmg(ps[:, 0:tn], [(wob[:, hh, dc * 128:(dc + 1) * 128], YB[:, hh, t0:t0 + tn]) for hh in range(4)])
            xadd(dc, t0, tn, ps)
    AR.release("H", "TM", "DR", "BC", "ALs", "YB", "PPb0", "PPb1", "PPd0", "PPd1", "SO0", "SO1", "SSQ0", "SSQ1", "junk", "QT0", "QT1", "QT2", "KT0", "KT1", "KW0", "KW1",
               "Vc0", "Vc1", "Sp0", "Sp1", "OF0", "OF1", "Cst", "Cb", "C0", "C0b", "Vblk", "QZ", "GH", "NT0", "NTO", "NPO",
               "post", "nout", "nout2")


def LNKS_AP(g_):
    return g_["lnks"]


from contextlib import ExitStack
from concourse.bass_utils import run_bass_kernel_spmd
import math

NP_ = 2048
NS_ = 64
NT_ = NP_ + NS_
TT = [(0, 512), (512, 512), (1024, 512), (1536, 512), (2048, 64)]
CH = [(128 * c, 128) for c in range(16)] + [(2048, 64)]
EPS = 1e-6
DFF = 2816
NFC = 22
LNKS = math.log(128.0 ** -0.5)
DEBUG_STOP = None


class Arena:
    def __init__(self, nc, es, nbytes):
        self.t = es.enter_context(nc.sbuf_tensor("arena", [128, nbytes // 2], BF16))
        self.nbytes = nbytes
        self.free = [(0, nbytes)]
        self.live = {}
        self.peak = 0

    def alloc(self, name, shape, dt):
        esz = 2 if dt == BF16 else 4
        n = 1
        for s in shape[1:]:
            n *= s
        nb = (n * esz + 63) // 64 * 64
        for i, (o, sz) in enumerate(self.free):
            if sz >= nb:
                off = o
                if sz == nb:
                    self.free.pop(i)
                else:
                    self.free[i] = (o + nb, sz - nb)
                break
        else:
            raise RuntimeError("arena OOM for %s (%d bytes); live=%s" % (name, nb, {k: v[1] for k, v in self.live.items()}))
        assert name not in self.live, name
        self.live[name] = (off, nb)
        used = self.nbytes - sum(s for _, s in self.free)
        self.peak = max(self.peak, used)
        base = self.t[0:shape[0], off // 2: off // 2 + nb // 2]
        if dt != BF16:
            base = base.bitcast(dt)
        v = base[:, 0:n]
        if len(shape) > 2:
            names = " ".join("d%d" % i for i in range(len(shape) - 1))
            kw = {"d%d" % i: shape[i + 1] for i in range(len(shape) - 2)}
            v = v.rearrange("p (%s) -> p %s" % (names, names), **kw)
        return v

    def release(self, *names):
        for name in names:
            off, nb = self.live.pop(name)
            self.free.append((off, nb))
        self.free.sort()
        m = []
        for o, s in self.free:
            if m and m[-1][0] + m[-1][1] == o:
                m[-1] = (m[-1][0], m[-1][1] + s)
            else:
                m.append((o, s))
        self.free = m


def mkap(base, off, dims):
    pst = list(base.ap)[0][0]
    return bass.AP(tensor=base.tensor, offset=base.offset + off, ap=[[pst, dims[0]]] + [list(d) for d in dims[1:]])


def build_program():
    nc = bass.Bass("TRN2", target_bir_lowering=False)
    D = {}

    def din(name, shape):
        D[name] = nc.dram_tensor(name, list(shape), F32, kind="ExternalInput").ap()

    def dout(name, shape):
        D[name] = nc.dram_tensor(name, list(shape), F32, kind="ExternalOutput").ap()

    din("xp", [2048, 1024]); din("xs", [64, 1024]); din("spool", [2, 16, 15, 512])
    din("sc", [2, 16, 4, 128, 128]); din("sn", [2, 16, 4, 128]); din("sm", [2, 16, 4]); din("sconv", [4, 16, 2, 2816])
    din("g_mix", [4, 1024]); din("w_in_even", [2, 1024, 2568]); din("b_gates", [2, 8]); din("w_pool", [2, 4, 128, 128])
    din("pool_scale", [2, 512]); din("g_head", [2, 512]); din("w_out_even", [2, 1024, 1024]); din("w_in_odd", [2, 1024, 2048])
    din("ln_v_g", [2, 1024]); din("ln_v_b", [2, 1024]); din("w_spatial", [2, 4, 128, 128]); din("b_spatial", [2, 4, 128])
    din("w_out_odd", [2, 1024, 1024]); din("g_ffn", [4, 1024]); din("w_ffn_gate", [4, 1024, 2816]); din("w_ffn_up", [4, 1024, 2816])
    din("conv_w", [4, 3, 2816]); din("conv_b", [4, 2816]); din("w_ffn_down", [4, 2816, 1024]); din("g_final", [1024])
    dout("yp", [2048, 1024]); dout("ys", [64, 1024]); dout("opool_p", [2, 15, 512]); dout("opool_s", [2, 16, 15, 512])
    dout("oc_p", [2, 4, 128, 128]); dout("oc_s", [2, 16, 4, 128, 128]); dout("on_p", [2, 4, 128]); dout("on_s", [2, 16, 4, 128])
    dout("om_p", [2, 4]); dout("om_s", [2, 16, 4]); dout("oconv_p", [4, 2, 2816]); dout("oconv_s", [4, 16, 2, 2816])
    dout("ov_s", [2, 64, 1024])
    if DEBUG_STOP is not None:
        dout("dbg_x", [128, 8, NT_])

    with ExitStack() as es:
        S = Sched(nc, es)
        AR = Arena(nc, es, 211968)
        PS = [es.enter_context(nc.psum_tensor("ps%d" % i, [128, 512], F32)) for i in range(8)]
        psrr = [0]

        def nps():
            psrr[0] = (psrr[0] + 1) % 6
            return PS[psrr[0]]

        def aps(*xs):
            return [x for x in xs if x is not None and not isinstance(x, (int, float))]

        def act(out, in_, func, bias=None, scale=None, accum=None):
            kw = {}
            if bias is not None:
                kw["bias"] = bias
            if scale is not None:
                kw["scale"] = scale
            if accum is not None:
                kw["accum_out"] = accum
            S.op("act", lambda e: e.activation(out=out, in_=in_, func=func, **kw),
                 reads=aps(in_, bias, scale), writes=aps(out, accum))

        def tt(eng, out, in0, in1, op):
            S.op(eng, lambda e: e.tensor_tensor(out=out, in0=in0, in1=in1, op=op), reads=[in0, in1], writes=[out])

        def ts(eng, out, in0, s1, op0, s2=None, op1=None):
            if op1 is None:
                S.op(eng, lambda e: e.tensor_scalar(out=out, in0=in0, scalar1=s1, scalar2=None, op0=op0),
                     reads=aps(in0, s1), writes=[out])
            else:
                S.op(eng, lambda e: e.tensor_scalar(out=out, in0=in0, scalar1=s1, scalar2=s2, op0=op0, op1=op1),
                     reads=aps(in0, s1, s2), writes=[out])

        def stt(out, in0, sc, in1, op0, op1):
            S.op("dve", lambda e: e.scalar_tensor_tensor(out=out, in0=in0, scalar=sc, in1=in1, op0=op0, op1=op1),
                 reads=aps(in0, sc, in1), writes=[out])

        def cp(eng, out, in_):
            if eng == "act":
                act(out, in_, AF.Copy)
            else:
                S.op(eng, lambda e: e.tensor_copy(out=out, in_=in_), reads=[in_], writes=[out])

        def memset(eng, out, val):
            S.op(eng, lambda e: e.memset(out, val), writes=[out])

        def mmg(out, pairs):
            n = len(pairs)

            def fn(e):
                inst = None
                for i, (l, r) in enumerate(pairs):
                    inst = e.matmul(out, lhsT=l, rhs=r, start=(i == 0), stop=(i == n - 1))
                return inst
            rd = []
            for l, r in pairs:
                rd.append(l)
                rd.append(r)
            S.op("pe", fn, reads=rd, writes=[out])

        def mms(items):
            def fn(e):
                inst = None
                for (o, l, r) in items:
                    inst = e.matmul(o, lhsT=l, rhs=r, start=True, stop=True)
                return inst
            S.op("pe", fn, reads=[x for it in items for x in it[1:]], writes=[it[0] for it in items])

        def trs(items):
            def fn(e):
                inst = None
                for (o, i_, idn) in items:
                    inst = e.transpose(o, i_, idn)
                return inst
            S.op("pe", fn, reads=[x for it in items for x in it[1:]], writes=[it[0] for it in items])

        X = AR.alloc("X", [128, 8, NT_], F32)
        identF = AR.alloc("identF", [128, 128], F32)
        identB = AR.alloc("identB", [128, 128], BF16)
        causal = AR.alloc("causal", [128, 128], F32)
        onesF = AR.alloc("onesF", [128, 128], F32)
        onesB = AR.alloc("onesB", [128, 128], BF16)
        selH = AR.alloc("selH", [4, 4, 128], F32)
        bmask = AR.alloc("bmask", [64, 16, 4], F32)
        bones = AR.alloc("bones", [64, 16], F32)
        repS = AR.alloc("repS", [4, 16, 4], F32)
        invc = AR.alloc("invc", [128, 4, 16], F32)
        gmix = AR.alloc("gmix", [128, 4, 8], F32)
        gffn = AR.alloc("gffn", [128, 4, 8], F32)
        gfin = AR.alloc("gfin", [128, 8], F32)
        convw = AR.alloc("convw", [128, 4, 3, NFC], F32)
        convb = AR.alloc("convb", [128, 4, NFC], F32)
        pscale = AR.alloc("pscale", [128, 2, 4], F32)
        bgate = AR.alloc("bgate", [4, 2, 2], F32)
        nbf = AR.alloc("nbf", [4, 2], F32)
        RING = [AR.alloc("ring%d" % i, [128, 4096], BF16) for i in range(4)]
        RSLOT = [S.slot("wsem%d" % i) for i in range(4)]
        ringi = [0]
        ld = None
        st = None

        def ring_next():
            i = ringi[0] % 4
            ringi[0] += 1
            return RING[i], RSLOT[i]

        def wload(slotpair, dst, src, new_group=True):
            S.dma("pool", slotpair[1], dst, src, new_group=new_group)

        memset("pool", onesF, 1.0)
        S.op("pool", lambda e: e.affine_select(out=causal, in_=onesF, pattern=[[1, 128]], compare_op=ALU.is_ge, fill=0.0, base=0, channel_multiplier=-1), reads=[onesF], writes=[causal])
        S.op("pool", lambda e: e.affine_select(out=identF, in_=onesF, pattern=[[1, 128]], compare_op=ALU.is_equal, fill=0.0, base=0, channel_multiplier=-1), reads=[onesF], writes=[identF])
        cp("pool", identB, identF)
        cp("pool", onesB, onesF)
        for h in range(4):
            S.op("pool", lambda e, h=h: e.affine_select(out=selH[:, h, :], in_=onesF[0:4, :], pattern=[[0, 128]], compare_op=ALU.is_equal, fill=0.0, base=-h, channel_multiplier=1), reads=[onesF[0:4, :]], writes=[selH[:, h, :]])
        S.op("pool", lambda e: e.affine_select(out=bones, in_=onesF[0:64, 0:16], pattern=[[-4, 16]], compare_op=ALU.is_ge, fill=0.0, base=0, channel_multiplier=1), reads=[onesF[0:64, 0:16]], writes=[bones])
        S.op("pool", lambda e: e.affine_select(out=bones, in_=bones, pattern=[[4, 16]], compare_op=ALU.is_ge, fill=0.0, base=3, channel_multiplier=-1), reads=[bones], writes=[bones])
        tt("pool", bmask, mkap(bones, 0, [64, [1, 16], [0, 4]]), causal[0:64, 0:64].rearrange("p (s r) -> p s r", r=4), ALU.mult)
        S.op("pool", lambda e: e.affine_select(out=repS, in_=onesF[0:4, 0:64].rearrange("p (s r) -> p s r", r=4), pattern=[[0, 16], [1, 4]], compare_op=ALU.is_equal, fill=0.0, base=0, channel_multiplier=-1), reads=[onesF[0:4, 0:64]], writes=[repS])
        itmp = AR.alloc("itmp", [128, 16], I32)
        S.op("pool", lambda e: e.iota(itmp, pattern=[[1, 16]], base=1, channel_multiplier=0), writes=[itmp])
        cp("dve", invc[:, 0, :], itmp)
        for g in range(1, 4):
            ts("dve", invc[:, g, :], invc[:, 0, :], float(2 ** (g + 1)), ALU.min)
        ts("dve", invc[:, 0, :], invc[:, 0, :], 2.0, ALU.min)
        S.op("dve", lambda e: e.reciprocal(out=invc, in_=invc), reads=[invc], writes=[invc])
        AR.release("itmp")

        NCD = dict(allow_slow_non_contiguous=True)
        S.dma("act", ld, gmix, D["g_mix"].rearrange("l (c p) -> p l c", p=128), **NCD)
        S.dma("act", ld, gffn, D["g_ffn"].rearrange("l (c p) -> p l c", p=128), **NCD)
        S.dma("act", ld, gfin, D["g_final"].rearrange("(c p) -> p c", p=128), **NCD)
        S.dma("act", ld, pscale, D["pool_scale"].rearrange("e (g p) -> p e g", p=128), **NCD)
        S.dma("act", ld, bgate, D["b_gates"].rearrange("e (a h) -> h e a", a=2), **NCD)
        ts("dve", nbf, bgate[:, :, 1], -1.0, ALU.mult)
        for l in range(4):
            for j in range(3):
                S.dma("act", ld, convw[:, l, j, :], D["conv_w"][l, j, :].rearrange("(c p) -> p c", p=128), **NCD)
            S.dma("act", ld, convb[:, l, :], D["conv_b"][l, :].rearrange("(c p) -> p c", p=128), **NCD)

        XT = [AR.alloc("XT%d" % i, [128, 1024], F32) for i in range(6)]
        for c, (t0, tn) in enumerate(CH):
            xt = XT[c % 6]
            src = D["xp"][t0:t0 + tn, :] if c < 16 else D["xs"]
            S.dma("sp", ld, xt[0:tn, :], src)
            for half in range(2):
                ps = nps()
                trs([(ps[:, j * 128:j * 128 + tn], xt[0:tn, (4 * half + j) * 128:(4 * half + j + 1) * 128], identF[0:tn, 0:tn]) for j in range(4)])
                cp("act" if half == 0 else "dve", X[:, 4 * half:4 * half + 4, t0:t0 + tn],
                   ps[:, :].rearrange("p (j t) -> p j t", t=128)[:, :, 0:tn])
        AR.release(*["XT%d" % i for i in range(6)])

        def rmsnorm(gcol, H):
            SQ = [AR.alloc("SQ%d" % i, [128, 8, 512], BF16) for i in range(2)]
            RS = [AR.alloc("RS%d" % i, [128, 512], F32) for i in range(2)]
            for ti, (t0, tn) in enumerate(TT):
                sq = SQ[ti % 2]
                rs = RS[ti % 2]
                act(sq[:, :, 0:tn], X[:, :, t0:t0 + tn], AF.Square)
                ps = nps()
                mmg(ps[:, 0:tn], [(onesB, sq[:, dc, 0:tn]) for dc in range(8)])
                act(rs[:, 0:tn], ps[:, 0:tn], AF.Sqrt, bias=EPS, scale=1.0 / 1024)
                S.op("dve", lambda e, rs=rs, tn=tn: e.reciprocal(out=rs[:, 0:tn], in_=rs[:, 0:tn]), reads=[rs[:, 0:tn]], writes=[rs[:, 0:tn]])
                for dc in range(8):
                    stt(H[:, dc, t0:t0 + tn], X[:, dc, t0:t0 + tn], gcol[:, dc:dc + 1], rs[:, 0:tn], ALU.mult, ALU.mult)
            AR.release("SQ0", "SQ1", "RS0", "RS1")

        def xadd(dc, t0, tn, ps):
            tt("dve", X[:, dc, t0:t0 + tn], X[:, dc, t0:t0 + tn], ps[:, 0:tn], ALU.add)

        def ffn(l):
            H = AR.alloc("H", [128, 8, NT_], BF16)
            rmsnorm(gffn[:, l, :], H)
            G = AR.alloc("G", [128, 4, NT_], BF16)
            A = [AR.alloc("A%d" % i, [128, 514], F32) for i in range(2)]
            T = [AR.alloc("T%d" % i, [128, 512], F32) for i in range(2)]
            GE = [AR.alloc("GE%d" % i, [128, 512], F32) for i in range(2)]
            As = AR.alloc("As", [128, 16, 6], F32)
            Ts = AR.alloc("Ts", [128, 16, 4], F32)
            CST = AR.alloc("CST", [128, NFC, 32], F32)
            CSs = AR.alloc("CSs", [128, NFC, 32], F32)
            CSp = AR.alloc("CSp", [128, 2, NFC], F32)
            CTM = AR.alloc("CTM", [32, DFF], F32)
            S.dma("sp", ld, CTM, D["sconv"][l].rearrange("s j f -> (s j) f"))
            for f0 in range(0, NFC, 4):
                nf = min(4, NFC - f0)
                ps = nps()
                trs([(ps[:, j * 32:(j + 1) * 32], CTM[0:32, (f0 + j) * 128:(f0 + j + 1) * 128], identF[0:32, 0:32]) for j in range(nf)])
                cp("dve", CST[:, f0:f0 + nf, :], ps[:, 0:nf * 32].rearrange("p (j t) -> p j t", t=32))
            AR.release("CTM")
            k = 0
            q_gelu, q_mul = [], []

            def step_pipe(s2_new):
                m_prev = q_mul.pop(0) if q_mul else None
                if q_gelu:
                    q_gelu.pop(0)()
                if m_prev is not None:
                    m_prev()
                q_gelu.append(s2_new)
            FW = {}

            def fload(kind, b):
                if b >= 6:
                    return
                c0 = 512 * b
                ncol = min(512, DFF - c0)
                nfc = ncol // 128
                r = ring_next()
                if kind == "g":
                    w_ = r[0][:, 0:8 * ncol].rearrange("p (c f) -> p c f", c=8)
                    wload(r, w_, D["w_ffn_gate"][l][:, c0:c0 + ncol].rearrange("(c p) f -> p c f", p=128))
                elif kind == "u":
                    w_ = r[0][:, 0:8 * ncol].rearrange("p (c f) -> p c f", c=8)
                    wload(r, w_, D["w_ffn_up"][l][:, c0:c0 + ncol].rearrange("(c p) f -> p c f", p=128))
                else:
                    w_ = r[0][:, 0:nfc * 1024].rearrange("p (c f) -> p c f", c=nfc)
                    wload(r, w_, D["w_ffn_down"][l][c0:c0 + ncol, :].rearrange("(c p) f -> p c f", p=128))
                FW[(kind, b)] = w_

            fload("g", 0)
            fload("u", 0)
            fload("d", 0)
            for b in range(6):
                c0 = 512 * b
                ncol = min(512, DFF - c0)
                nfc = ncol // 128
                fload("g", b + 1)
                wg, wu, wd = FW[("g", b)], FW[("u", b)], FW[("d", b)]
                for fc in range(nfc):
                    F = 4 * b + fc
                    w0 = convw[:, l, 0, F:F + 1]
                    w1 = convw[:, l, 1, F:F + 1]
                    w2 = convw[:, l, 2, F:F + 1]
                    cb = convb[:, l, F:F + 1]
                    memset("pool", A[0][:, 0:2], 0.0)
                    for ti, (t0, tn) in enumerate(TT):
                        psA = PS[(0, 1, 4)[k % 3]]
                        psU = PS[(2, 3, 6, 5)[k % 4]]
                        tb = T[k % 2]
                        ge = GE[k % 2]
                        k += 1
                        mmg(psA[:, 0:tn], [(wg[:, dc, fc * 128:(fc + 1) * 128], H[:, dc, t0:t0 + tn]) for dc in range(8)])
                        mmg(psU[:, 0:tn], [(wu[:, dc, fc * 128:(fc + 1) * 128], H[:, dc, t0:t0 + tn]) for dc in range(8)])
                        if ti < 4:
                            ab = A[ti % 2]
                            act(ab[:, 2:514], psA[:, 0:512], AF.Copy)
                            if ti < 3:
                                cp("pool", A[(ti + 1) % 2][:, 0:2], ab[:, 512:514])
                            else:
                                cp("pool", CSp[:, :, F], ab[:, 512:514])
                            ts("pool", tb, ab[:, 0:512], w0, ALU.mult, cb, ALU.add)
                            stt(tb, ab[:, 1:513], w1, tb, ALU.mult, ALU.add)
                            stt(tb, ab[:, 2:514], w2, tb, ALU.mult, ALU.add)
                            def s2(ge=ge, tb=tb, dst=G[:, fc, t0:t0 + 512], pu=psU[:, 0:512]):
                                act(ge, tb, AF.Gelu_apprx_tanh)
                                q_mul.append(lambda: tt("dve", dst, ge, pu, ALU.mult))
                            step_pipe(s2)
                        else:
                            cp("pool", As[:, :, 0:2], CST[:, F, :].rearrange("p (s j) -> p s j", j=2))
                            act(As[:, :, 2:6], psA[:, 0:64].rearrange("p (s t) -> p s t", t=4), AF.Copy)
                            ts("pool", Ts, As[:, :, 0:4], w0, ALU.mult, cb, ALU.add)
                            stt(Ts, As[:, :, 1:5], w1, Ts, ALU.mult, ALU.add)
                            stt(Ts, As[:, :, 2:6], w2, Ts, ALU.mult, ALU.add)
                            def s2(ge=ge, dst=G[:, fc, 2048:2112], pu=psU):
                                act(ge[:, 0:64], Ts[:, :, :].rearrange("p s t -> p (s t)"), AF.Gelu_apprx_tanh)
                                q_mul.append(lambda: tt("dve", dst, ge[:, 0:64], pu[:, 0:64], ALU.mult))
                            step_pipe(s2)
                            cp("pool", CSs[:, F, :].rearrange("p (s j) -> p s j", j=2), As[:, :, 4:6])
                while q_gelu or q_mul:
                    if q_gelu:
                        q_gelu.pop(0)()
                    while q_mul:
                        q_mul.pop(0)()
                fload("u", b + 1)
                fload("d", b + 1)
                for ti, (t0, tn) in enumerate(TT):
                    for dc in range(8):
                        psD = PS[(4, 5, 7)[(ti * 8 + dc) % 3]]
                        mmg(psD[:, 0:tn], [(wd[:, fc, dc * 128:(dc + 1) * 128], G[:, fc, t0:t0 + tn]) for fc in range(nfc)])
                        xadd(dc, t0, tn, psD)
            AR.release("G", "H", "A0", "A1", "T0", "T1", "GE0", "GE1", "As", "Ts", "CST")
            COp = AR.alloc("COp", [NFC, 2, 128], F32)
            for j in range(2):
                ps = nps()
                trs([(ps[0:NFC, 0:128], CSp[:, j, :], identF)])
                cp("dve", COp[:, j, :], ps[0:NFC, 0:128])
                S.dma("sp", st, D["oconv_p"][l, j, :].rearrange("(c p) -> c p", p=128), COp[:, j, :])
            COs = AR.alloc("COs", [32, NFC, 128], F32)
            for f0 in range(0, NFC, 4):
                nf = min(4, NFC - f0)
                ps = nps()
                trs([(ps[0:32, j * 128:(j + 1) * 128], CSs[:, f0 + j, :], identF) for j in range(nf)])
                cp("dve", COs[:, f0:f0 + nf, :], ps[0:32, 0:nf * 128].rearrange("p (j t) -> p j t", t=128))
            S.dma("sp", st, D["oconv_s"][l].rearrange("s j f -> (s j) f"), COs[:, :, :].rearrange("p c f -> p (c f)"))
            AR.release("CSs", "CSp", "COp", "COs")

        def odd(o, l):
            def _ld(src):
                r = ring_next()
                wv_ = r[0][:, :].rearrange("p (c f) -> p c f", c=8)
                wload(r, wv_, src.rearrange("(c p) f -> p c f", p=128))
                return wv_
            WU = [_ld(D["w_in_odd"][o][:, 512 * b:512 * b + 512]) for b in range(2)]
            wv0 = _ld(D["w_in_odd"][o][:, 1024:1536])
            wv1 = _ld(D["w_in_odd"][o][:, 1536:2048])
            H = AR.alloc("H", [128, 8, NT_], BF16)
            rmsnorm(gmix[:, l, :], H)
            UT = AR.alloc("UT", [128, 8, NT_], BF16)
            lng = AR.alloc("lng", [128, 1024], F32)
            lnb = AR.alloc("lnb", [128, 1024], F32)
            BSg = AR.alloc("BSg", [128, 4, 128], F32)
            BS64g = AR.alloc("BS64g", [128, 4, 16, 4], F32)
            WsT = AR.alloc("WsT", [128, 4, 128], BF16)
            W64 = AR.alloc("W64", [64, 4, 64], BF16)
            wsp = AR.alloc("wsp", [128, 4, 128], F32)
            X4 = AR.alloc("X4", [4, 4, 16, 4], F32)
            S.dma("sp", ld, lng, D["ln_v_g"][o:o + 1, :].to_broadcast([128, 1024]))
            S.dma("sp", ld, lnb, D["ln_v_b"][o:o + 1, :].to_broadcast([128, 1024]))
            S.dma("sp", ld, BSg[:, :, :].rearrange("p g r -> p (g r)"), D["b_spatial"][o:o + 1].rearrange("a g r -> a (g r)").to_broadcast([128, 512]))
            S.dma("sp", ld, wsp, D["w_spatial"][o].rearrange("g r s -> r g s"))
            cp("pool", BS64g, mkap(BSg, 0, [128, [128, 4], [0, 16], [1, 4]]))
            ps = nps()
            trs([(ps[:, g * 128:(g + 1) * 128], wsp[:, g, :], identF) for g in range(4)])
            psv = ps[:, :].rearrange("p (g r) -> p g r", g=4)
            tt("dve", WsT, psv, mkap(causal, 0, [128, [0, 4], [1, 128]]), ALU.mult)
            cp("dve", X4, mkap(psv, 0, [4, [128, 4], [0, 16], [1, 4]]))
            ps2 = nps()
            mms([(ps2[0:64, 0:256], repS[:, :, :].rearrange("k s r -> k (s r)"), X4[:, :, :, :].rearrange("k g s r -> k (g s r)"))])
            tt("dve", W64, ps2[0:64, 0:256].rearrange("p (g t) -> p g t", g=4),
               mkap(bmask, 0, [64, [0, 4], [1, 64]]), ALU.mult)
            AR.release("wsp", "X4")
            for b in range(2):
                wv = WU[b]
                for fc in range(4):
                    for ti, (t0, tn) in enumerate(TT):
                        ps = nps()
                        mmg(ps[:, 0:tn], [(wv[:, dc, fc * 128:(fc + 1) * 128], H[:, dc, t0:t0 + tn]) for dc in range(8)])
                        act(UT[:, 4 * b + fc, t0:t0 + tn], ps[:, 0:tn], AF.Gelu_apprx_tanh)
            WO = [_ld(D["w_out_odd"][o][:, 512 * b:512 * b + 512]) for b in range(2)]
            V = [AR.alloc("V%d" % i, [128, 1024], F32) for i in range(3)]
            VLb = [AR.alloc("VLb%d" % i, [128, 1024], BF16) for i in range(3)]
            STt = [AR.alloc("STt%d" % i, [128, 16], F32) for i in range(3)]
            TMP = [AR.alloc("TMPo%d" % i, [128, 4, 128], F32) for i in range(2)]
            def vstage(c):
                t0, tn = CH[c]
                v = V[c % 3]
                vb = VLb[c % 3]
                sv = STt[c % 3]
                for j, wvj in enumerate((wv0, wv1)):
                    ps = nps()
                    mmg(ps[0:tn, :], [(H[:, dc, t0:t0 + tn], wvj[:, dc, :]) for dc in range(8)])
                    act(v[0:tn, 512 * j:512 * j + 512], ps[0:tn, :], AF.Gelu_apprx_tanh)
                    S.op("dve", lambda e, sv=sv, v=v, j=j, tn=tn: e.bn_stats(out=sv[0:tn, 6 * j:6 * j + 6], in_=v[0:tn, 512 * j:512 * j + 512]),
                         reads=[v[0:tn, 512 * j:512 * j + 512]], writes=[sv[0:tn, 6 * j:6 * j + 6]])
                S.op("dve", lambda e, sv=sv, tn=tn: e.bn_aggr(out=sv[0:tn, 12:14], in_=sv[0:tn, 0:12]), reads=[sv[0:tn, 0:12]], writes=[sv[0:tn, 12:14]])
                act(sv[0:tn, 14:15], sv[0:tn, 13:14], AF.Sqrt, bias=EPS, scale=1.0)
                S.op("dve", lambda e, sv=sv, tn=tn: e.reciprocal(out=sv[0:tn, 14:15], in_=sv[0:tn, 14:15]), reads=[sv[0:tn, 14:15]], writes=[sv[0:tn, 14:15]])
                ts("dve", sv[0:tn, 15:16], sv[0:tn, 12:13], sv[0:tn, 14:15], ALU.mult, -1.0, ALU.mult)
                act(v[0:tn, :], v[0:tn, :], AF.Identity, bias=sv[0:tn, 15:16], scale=sv[0:tn, 14:15])
                tt("pool", v[0:tn, :], v[0:tn, :], lng[0:tn, :], ALU.mult)
                if c < 16:
                    tt("pool", vb[0:tn, :], v[0:tn, :], lnb[0:tn, :], ALU.add)
                else:
                    tt("pool", v[0:tn, :], v[0:tn, :], lnb[0:tn, :], ALU.add)
                    S.dma("sp", st, D["ov_s"][o], v[0:tn, :])
                    cp("pool", vb[0:tn, :], v[0:tn, :])

            def sstage(c):
                t0, tn = CH[c]
                vb = VLb[c % 3]
                for half in range(2):
                    ps = nps()
                    items = []
                    for j in range(4):
                        dj = 4 * half + j
                        g = dj // 2
                        rhs = WsT[:, g, :] if c < 16 else W64[:, g, :]
                        items.append((ps[:, j * 128:j * 128 + tn], vb[0:tn, dj * 128:(dj + 1) * 128], rhs))
                    mms(items)
                    tm = TMP[half]
                    psv = ps[:, :].rearrange("p (a b t) -> p a b t", a=2, b=2)[:, :, :, 0:tn]
                    if c < 16:
                        bias = mkap(BSg, 2 * half * 128, [128, [128, 2], [0, 2], [1, tn]])
                    else:
                        bias = mkap(BS64g, 2 * half * 64, [128, [64, 2], [0, 2], [1, 64]])
                    tt("dve", tm[:, :, :].rearrange("p (a b) t -> p a b t", a=2)[:, :, :, 0:tn], psv, bias, ALU.add)
                    uv = UT[:, 4 * half:4 * half + 4, t0:t0 + tn]
                    tt("dve", uv, tm[:, :, 0:tn], uv, ALU.mult)
            for c in range(17):
                vstage(c)
                if c >= 2:
                    sstage(c - 2)
            sstage(15)
            sstage(16)
            AR.release("V0", "V1", "V2", "VLb0", "VLb1", "VLb2", "STt0", "STt1", "STt2", "TMPo0", "TMPo1", "H")
            for b in range(2):
                wo = WO[b]
                for dcl in range(4):
                    dc = 4 * b + dcl
                    for ti, (t0, tn) in enumerate(TT):
                        ps = nps()
                        mmg(ps[:, 0:tn], [(wo[:, fc, dcl * 128:(dcl + 1) * 128], UT[:, fc, t0:t0 + tn]) for fc in range(8)])
                        xadd(dc, t0, tn, ps)
            AR.release("UT", "lng", "lnb", "BSg", "BS64g", "WsT", "W64")

        EVEN_IMPL = {}

        def final():
            SQ = [AR.alloc("SQ%d" % i, [128, 8, 512], BF16) for i in range(2)]
            RS = [AR.alloc("RS%d" % i, [128, 512], F32) for i in range(2)]
            YF = [AR.alloc("YF%d" % i, [128, 8, 512], F32) for i in range(2)]
            YT = [AR.alloc("YT%d" % i, [128, 1024], F32) for i in range(6)]
            k = 0
            for ti, (t0, tn) in enumerate(TT):
                sq = SQ[ti % 2]
                rs = RS[ti % 2]
                yf = YF[ti % 2]
                act(sq[:, :, 0:tn], X[:, :, t0:t0 + tn], AF.Square)
                ps = nps()
                mmg(ps[:, 0:tn], [(onesB, sq[:, dc, 0:tn]) for dc in range(8)])
                act(rs[:, 0:tn], ps[:, 0:tn], AF.Sqrt, bias=EPS, scale=1.0 / 1024)
                S.op("dve", lambda e, rs=rs, tn=tn: e.reciprocal(out=rs[:, 0:tn], in_=rs[:, 0:tn]), reads=[rs[:, 0:tn]], writes=[rs[:, 0:tn]])
                for dc in range(8):
                    stt(yf[:, dc, 0:tn], X[:, dc, t0:t0 + tn], gfin[:, dc:dc + 1], rs[:, 0:tn], ALU.mult, ALU.mult)
                for c0 in range(0, tn, 128):
                    cn = min(128, tn - c0)
                    yt = YT[k % 6]
                    k += 1
                    for half in range(2):
                        ps = nps()
                        trs([(ps[0:cn, j * 128:(j + 1) * 128], yf[:, 4 * half + j, c0:c0 + cn], identF) for j in range(4)])
                        cp("act" if half == 0 else "dve", yt[0:cn, 512 * half:512 * half + 512], ps[0:cn, :])
                    if ti < 4:
                        S.dma("sp", st, D["yp"][t0 + c0:t0 + c0 + cn, :], yt[0:cn, :])
                    else:
                        S.dma("sp", st, D["ys"], yt[0:cn, :])
            AR.release("SQ0", "SQ1", "RS0", "RS1", "YF0", "YF1", *["YT%d" % i for i in range(6)])

        def dump_x():
            S.dma("sp", st, D["dbg_x"], X)

        ctx = dict(nc=nc, S=S, AR=AR, PS=PS, nps=nps, D=D, X=X, act=act, tt=tt, ts=ts, stt=stt, cp=cp, memset=memset,
                   mmg=mmg, mms=mms, trs=trs, rmsnorm=rmsnorm, xadd=xadd, ring_next=ring_next, wload=wload, ld=ld, st=st,
                   identF=identF, identB=identB, causal=causal, onesF=onesF, onesB=onesB, selH=selH, bmask=bmask, bones=bones,
                   invc=invc, gmix=gmix, pscale=pscale, bgate=bgate, nbf=nbf, NCD=NCD, lnks=LNKS)

        if DEBUG_STOP is None:
            for l in range(4):
                if l % 2 == 0:
                    even_layer(ctx, l // 2, l)
                else:
                    odd(l // 2, l)
                ffn(l)
            final()
        else:
            for kind, l in DEBUG_STOP:
                if kind == "even":
                    even_layer(ctx, l // 2, l)
                elif kind == "odd":
                    odd(l // 2, l)
                elif kind == "ffn":
                    ffn(l)
            dump_x()
        S.wait_all_dma("sp", S.slots)
        S.finish()
        import os as _os2
        if _os2.environ.get("SCHED_DUMP"):
            for en in S.ENG:
                print("==== engine", en)
                for rec in [r for r in S.oplog if r[0] == en][-int(_os2.environ["SCHED_DUMP"]):]:
                    print(rec)
        if getattr(S, "ps_multi", None):
            print("[kernel] WARNING multi-engine PSUM readers:", len(S.ps_multi), S.ps_multi[:6], flush=True)
        print("[kernel] arena peak bytes/partition:", AR.peak, " instr:", dict(S.cnt), " waits:", S.n_wait, flush=True)
    return nc


def _shard_inputs(inp):
    A = lambda a: np.ascontiguousarray(a, dtype=np.float32)
    wnames = ["g_mix", "w_in_even", "b_gates", "w_pool", "pool_scale", "g_head", "w_out_even", "w_in_odd", "ln_v_g", "ln_v_b",
              "w_spatial", "b_spatial", "w_out_odd", "g_ffn", "w_ffn_gate", "w_ffn_up", "conv_w", "conv_b", "w_ffn_down", "g_final"]
    W = {n: A(inp[n]) for n in wnames}
    maps = []
    for c in range(8):
        s = slice(16 * c, 16 * c + 16)
        m = dict(W)
        m["xp"] = A(inp["x_prompt"][c])
        m["xs"] = A(np.asarray(inp["x_sample"])[s].reshape(64, 1024))
        m["spool"] = A(np.asarray(inp["state_pool"])[:, s])
        m["sc"] = A(np.asarray(inp["state_mlstm_c"])[:, s])
        m["sn"] = A(np.asarray(inp["state_mlstm_n"])[:, s])
        m["sm"] = A(np.asarray(inp["state_mlstm_m"])[:, s])
        m["sconv"] = A(np.asarray(inp["state_ffn_conv"])[:, s])
        maps.append(m)
    return maps


def kernel(**inputs):
    inputs = {k: np.asarray(v) for k, v in inputs.items()}
    nc = build_program()
    maps = _shard_inputs(inputs)
    res = run_bass_kernel_spmd(nc, maps, core_ids=list(range(8)))
    R = res.results
    f = np.float32
    y_p = np.zeros((8, 2048, 1024), f); y_s = np.zeros((128, 4, 1024), f)
    pool_p = np.zeros((2, 8, 15, 512), f); pool_s = np.zeros((2, 128, 15, 512), f)
    c_p = np.zeros((2, 8, 4, 128, 128), f); c_s = np.zeros((2, 128, 4, 128, 128), f)
    n_p = np.zeros((2, 8, 4, 128), f); n_s = np.zeros((2, 128, 4, 128), f)
    m_p = np.zeros((2, 8, 4), f); m_s = np.zeros((2, 128, 4), f)
    cv_p = np.zeros((4, 8, 2, 2816), f); cv_s = np.zeros((4, 128, 2, 2816), f)
    v_s = np.zeros((2, 128, 4, 1024), f)
    for c in range(8):
        r = R[c]
        s = slice(16 * c, 16 * c + 16)
        y_p[c] = r["yp"]; y_s[s] = r["ys"].reshape(16, 4, 1024)
        pool_p[:, c] = r["opool_p"]; pool_s[:, s] = r["opool_s"]
        c_p[:, c] = r["oc_p"]; c_s[:, s] = r["oc_s"]
        n_p[:, c] = r["on_p"]; n_s[:, s] = r["on_s"]
        m_p[:, c] = r["om_p"]; m_s[:, s] = r["om_s"]
        cv_p[:, c] = r["oconv_p"]; cv_s[:, s] = r["oconv_s"]
        v_s[:, s] = r["ov_s"].reshape(2, 16, 4, 1024)
    return (y_p, y_s, pool_p, pool_s, c_p, c_s, n_p, n_s, m_p, m_s, cv_p, cv_s, v_s)
```

```python
import numpy as np
import concourse.bass as bass
import concourse.mybir as mybir

F32 = mybir.dt.float32
BF16 = mybir.dt.bfloat16
I32 = mybir.dt.int32
AF = mybir.ActivationFunctionType
ALU = mybir.AluOpType
AX = mybir.AxisListType

_DTB = {F32: 4, BF16: 2, I32: 4}


def _rect(ap):
    t = ap.tensor
    name = t.name
    pat = list(ap.ap)
    esz = _DTB.get(ap.dtype, 4)
    space = str(ap.space) if hasattr(ap, "space") else ""
    if "DRAM" in space.upper() or "HBM" in space.upper() or type(t).__name__.startswith("DRam"):
        lo = ap.offset
        hi = lo
        for st, cn in pat:
            hi += abs(st) * (cn - 1)
        return (name, 0, 1, lo * esz, (hi + 1) * esz)
    if name.startswith("ps") and name[2:].isdigit():
        return (name, 0, 128, 0, 2048)
    tsz = _DTB.get(t.dtype, 4)
    rowsz = 1
    for s in list(t.shape)[1:]:
        rowsz *= s
    rowb = rowsz * tsz
    offb = ap.offset * esz
    pstep, pcnt = pat[0]
    p0 = offb // rowb
    f0 = offb % rowb
    ext = 0
    for st, cn in pat[1:]:
        ext += abs(st) * (cn - 1)
    f1 = f0 + (ext + 1) * esz
    if pstep == 0:
        p1 = p0 + 1
    else:
        p1 = p0 + pcnt
    return (name, p0, p1, f0, f1)


class Slot:
    def __init__(self, sem, idx):
        self.sem = sem
        self.idx = idx
        self.count = 0


class Sched:
    ENG = ["pe", "act", "dve", "pool", "sp"]

    def __init__(self, nc, es, same_engine_sync=True):
        self.nc = nc
        self.es = es
        self.eng = {"pe": nc.tensor, "act": nc.scalar, "dve": nc.vector, "pool": nc.gpsimd, "sp": nc.sync}
        self.sem = {e: es.enter_context(nc.semaphore("sem_" + e)) for e in self.ENG}
        self.cnt = {e: 0 for e in self.ENG}
        self.know = {e: {f: 0 for f in self.ENG} for e in self.ENG}
        self.dknow = {e: {} for e in self.ENG}
        self.clock = {}
        self.track = {}
        self.prog = {e: [] for e in self.ENG}
        self.slots = []
        self.same_engine_sync = same_engine_sync
        self.n_wait = 0

    def slot(self, name=None):
        i = len(self.slots)
        s = Slot(self.es.enter_context(self.nc.semaphore(name or ("dsem%d" % i))), i)
        self.slots.append(s)
        return s

    def _collect(self, reads, writes):
        deps = []
        for ap in reads:
            name, p0, p1, f0, f1 = _rect(ap)
            for ent in self.track.get(name, ()):
                if ent[0] < p1 and p0 < ent[1] and ent[2] < f1 and f0 < ent[3]:
                    if ent[4] is not None:
                        deps.append(ent[4])
        for ap in writes:
            name, p0, p1, f0, f1 = _rect(ap)
            for ent in self.track.get(name, ()):
                if ent[0] < p1 and p0 < ent[1] and ent[2] < f1 and f0 < ent[3]:
                    if ent[4] is not None:
                        deps.append(ent[4])
                    for e, k in ent[5].items():
                        deps.append(("c", e, k))
                    deps.extend(ent[6])
        return deps

    def _record(self, ev, reads, writes):
        for ap in writes:
            name, p0, p1, f0, f1 = _rect(ap)
            lst = self.track.setdefault(name, [])
            lst[:] = [en for en in lst if not (p0 <= en[0] and en[1] <= p1 and f0 <= en[2] and en[3] <= f1)]
            lst.append([p0, p1, f0, f1, ev, {}, []])
        for ap in reads:
            name, p0, p1, f0, f1 = _rect(ap)
            lst = self.track.setdefault(name, [])
            found = None
            for en in lst:
                if en[0] == p0 and en[1] == p1 and en[2] == f0 and en[3] == f1:
                    found = en
                    break
            if found is None:
                found = [p0, p1, f0, f1, None, {}, []]
                lst.append(found)
            if ev[0] == "c":
                if found[5].get(ev[1], 0) < ev[2]:
                    found[5][ev[1]] = ev[2]
            else:
                found[6].append(ev)

    def _waits_for(self, e, deps):
        need_c = {}
        need_d = {}
        for ev in deps:
            if ev[0] == "c":
                _, f, k = ev
                if f == e and not self.same_engine_sync:
                    continue
                if self.know[e][f] < k and need_c.get(f, 0) < k:
                    need_c[f] = k
            else:
                _, si, val = ev
                val = self.slots[si].count
                if self.dknow[e].get(si, 0) < val and need_d.get(si, 0) < val:
                    need_d[si] = val
        waits = []
        for f, k in need_c.items():
            if self.know[e][f] >= k:
                continue
            waits.append((self.sem[f], k))
            ck = self.clock[(f, k)]
            kn = self.know[e]
            for g, v in ck.items():
                if kn[g] < v:
                    kn[g] = v
        for si, val in need_d.items():
            waits.append((self.slots[si].sem, val))
            self.dknow[e][si] = val
        self.n_wait += len(waits)
        return waits

    def op(self, e, fn, reads=(), writes=()):
        deps = self._collect(reads, writes)
        waits = self._waits_for(e, deps)
        idx = self.cnt[e] + 1
        self.cnt[e] = idx
        ck = dict(self.know[e])
        ck[e] = idx
        self.clock[(e, idx)] = ck
        ev = ("c", e, idx)
        self._record(ev, reads, writes)
        sem_e = self.sem[e]
        if not hasattr(self, "_psrd"):
            self._psrd = {}
            self.ps_multi = []
        for a in writes:
            nm = a.tensor.name
            if nm.startswith("ps") and nm[2:].isdigit() and e == "pe":
                self._psrd[nm] = set()
        for a in reads:
            nm = a.tensor.name
            if nm.startswith("ps") and nm[2:].isdigit():
                st_ = self._psrd.setdefault(nm, set())
                st_.add(e)
                if len(st_) > 1:
                    import traceback as _tb2
                    fr_ = _tb2.extract_stack(limit=5)
                    self.ps_multi.append((nm, sorted(st_), " <- ".join("%s:%d" % (f.name, f.lineno) for f in fr_[:-1][::-1])))
        if not hasattr(self, "oplog"):
            self.oplog = []
        import traceback as _tb
        fr = _tb.extract_stack(limit=4)
        self.oplog.append((e, idx, [(str(getattr(s_, "name", s_)), v) for s_, v in waits], " <- ".join("%s:%d" % (f.name, f.lineno) for f in fr[:-1][::-1]), [_rect(a) for a in writes]))

        def run(engine, waits=waits, fn=fn, sem_e=sem_e):
            for s, v in waits:
                engine.wait_ge(s, v)
            inst = fn(engine)
            inst.then_inc(sem_e, 1)

        self.prog[e].append(run)
        return ev

    def auto_slot(self, q):
        key = "sw" if q == "pool" else "hw"
        if not hasattr(self, "_auto"):
            self._auto = {"hw": [self.slot("asem%d" % i) for i in range(24)], "sw": [self.slot("swsem%d" % i) for i in range(8)]}
            self._autoi = {"hw": 0, "sw": 0}
        lst = self._auto[key]
        s = lst[self._autoi[key] % len(lst)]
        self._autoi[key] += 1
        return s

    def dma(self, q, slot, out, in_, track_out=True, track_in=True, new_group=True, **kw):
        if slot is None:
            slot = self.auto_slot(q)
        reads = [in_] if track_in else []
        writes = [out] if track_out else []
        deps = self._collect(reads, writes)
        if new_group and slot.count > 0:
            deps.append(("d", slot.idx, slot.count))
        waits = self._waits_for(q, deps)
        slot.count += 16
        ev = ("d", slot.idx, slot.count)
        self._record(ev, reads, writes)
        sem = slot.sem

        def run(engine, waits=waits, sem=sem, out=out, in_=in_, kw=kw):
            for s, v in waits:
                engine.wait_ge(s, v)
            engine.dma_start(out=out, in_=in_, **kw).then_inc(sem, 16)

        self.prog[q].append(run)
        return ev

    def wait_all_dma(self, q, slots):
        waits = [(s.sem, s.count) for s in slots if s.count > 0]

        def run(engine, waits=waits):
            for s, v in waits:
                engine.wait_ge(s, v)

        self.prog[q].append(run)

    def finish(self):
        nc = self.nc
        with nc.Block() as block:
            @block.tensor
            def _(eng):
                for f in self.prog["pe"]:
                    f(eng)

            @block.scalar
            def _(eng):
                for f in self.prog["act"]:
                    f(eng)

            @block.vector
            def _(eng):
                for f in self.prog["dve"]:
                    f(eng)

            @block.gpsimd
            def _(eng):
                for f in self.prog["pool"]:
                    f(eng)

            @block.sync
            def _(eng):
                for f in self.prog["sp"]:
                    f(eng)


def even_layer(ctx, e, l):
    g_ = ctx
    S, AR, PS, nps, D, X = g_["S"], g_["AR"], g_["PS"], g_["nps"], g_["D"], g_["X"]
    act, tt, ts, stt, cp, memset = g_["act"], g_["tt"], g_["ts"], g_["stt"], g_["cp"], g_["memset"]
    mmg, mms, trs, rmsnorm, xadd = g_["mmg"], g_["mms"], g_["trs"], g_["rmsnorm"], g_["xadd"]
    ring_next, wload, ld, st = g_["ring_next"], g_["wload"], g_["ld"], g_["st"]
    identF, identB, causal, onesF, selH, bmask, bones = g_["identF"], g_["identB"], g_["causal"], g_["onesF"], g_["selH"], g_["bmask"], g_["bones"]
    invc, gmix, pscale, bgate, nbf, NCD = g_["invc"], g_["gmix"], g_["pscale"], g_["bgate"], g_["nbf"], g_["NCD"]
    Win = D["w_in_even"][e]
    Wout = D["w_out_even"][e]

    H = AR.alloc("H", [128, 8, NT_], BF16)
    rmsnorm(gmix[:, l, :], H)

    def interleave(*gens):
        gens = list(gens)
        while gens:
            for g in list(gens):
                try:
                    next(g)
                except StopIteration:
                    gens.remove(g)

    GV = {}

    def gates_gen():
        ones4 = mkap(onesF, 0, [4, [0, 512]])
        wgt = AR.alloc("wgt", [128, 8, 8], BF16)
        S.dma("pool", ld, wgt, Win[:, 2560:2568].rearrange("(c p) f -> p c f", p=128), **NCD)
        yield
        Q_B, Q_G, Q_M, Q_R, Q_E = range(5)
        TM = AR.alloc("TM", [128, 5, 17, 4], F32)
        DR = AR.alloc("DR", [128, 5, 17, 4], F32)
        BC = AR.alloc("BC", [128, 4, 48], F32)
        ALs = AR.alloc("ALs", [128, 4, 16], F32)
        rows = {n: AR.alloc("row_" + n, [4, 512], F32) for n in ("IG", "LF", "B", "G", "M")}
        carry = AR.alloc("carry", [4, 2], F32)
        BCs = AR.alloc("BCs", [4, 48], F32)
        MLo = AR.alloc("MLo", [4, 17], F32)
        reps = AR.alloc("reps", [4, 2, 64], F32)
        memset("dve", carry, 0.0)
        yield
        S.dma("sp", ld, BCs[:, 32:48], D["sm"][e].rearrange("s h -> h s"), **NCD)
        yield
        psT = PS[6]
        memset("dve", psT[:, :], 0.0)
        yield
        bI = bgate[:, e, 0:1]
        nbF = nbf[:, e:e + 1]
        for ti, (t0, tn) in enumerate(TT):
            psI = nps()
            mmg(psI[0:4, 0:tn], [(wgt[:, dc, 0:4], H[:, dc, t0:t0 + tn]) for dc in range(8)])
            yield
            psF = nps()
            mmg(psF[0:4, 0:tn], [(wgt[:, dc, 4:8], H[:, dc, t0:t0 + tn]) for dc in range(8)])
            yield
            IG, LF, Br, Gr, Mr = (rows[n] for n in ("IG", "LF", "B", "G", "M"))
            act(IG[:, 0:tn], psI[0:4, 0:tn], AF.Identity, bias=bI, scale=1.0)
            yield
            act(LF[:, 0:tn], psF[0:4, 0:tn], AF.Exp, bias=nbF, scale=-1.0)
            yield
            act(LF[:, 0:tn], LF[:, 0:tn], AF.Ln, bias=1.0, scale=1.0)
            yield
            ts("dve", LF[:, 0:tn], LF[:, 0:tn], -1.0, ALU.mult)
            yield
            if ti < 4:
                S.op("dve", lambda en, Br=Br, LF=LF: en.tensor_tensor_scan(out=Br, data0=ones4, data1=LF, initial=carry[:, 0:1], op0=ALU.mult, op1=ALU.add),
                     reads=[ones4, LF, carry[:, 0:1]], writes=[Br])
                yield
                tt("dve", Gr, IG, Br, ALU.subtract)
                yield
                S.op("dve", lambda en, Mr=Mr, Gr=Gr: en.tensor_tensor_scan(out=Mr, data0=Gr, data1=Gr, initial=carry[:, 1:2], op0=ALU.max, op1=ALU.max),
                     reads=[Gr, carry[:, 1:2]], writes=[Mr])
                yield
                cp("dve", carry[:, 0:1], Br[:, 511:512])
                yield
                cp("dve", carry[:, 1:2], Mr[:, 511:512])
                yield
                cp("dve", BCs[:, 4 * ti:4 * ti + 4], mkap(Mr, 127, [4, [128, 4]]))
                yield
                if ti == 3:
                    tt("dve", MLo[:, 0:1], Br[:, 511:512], Mr[:, 511:512], ALU.add)
                    yield
                items = []
                for j in range(4):
                    c = 4 * ti + j
                    for q, R_ in ((Q_B, Br), (Q_G, Gr), (Q_M, Mr)):
                        col = (q * 17 + c) * 4
                        items.append((psT[:, col:col + 4], R_[:, 128 * j:128 * j + 128], identF[0:4, 0:4]))
                trs(items)
                yield
            else:
                v3 = lambda R_: R_[:, 0:64].rearrange("h (s t) -> h s t", t=4)
                B3, G3, M3, L3 = v3(Br), v3(Gr), v3(Mr), v3(LF)
                cp("dve", B3[:, :, 0], L3[:, :, 0])
                yield
                for t in range(1, 4):
                    tt("dve", B3[:, :, t], B3[:, :, t - 1], L3[:, :, t], ALU.add)
                    yield
                tt("dve", Gr[:, 0:64], IG[:, 0:64], Br[:, 0:64], ALU.subtract)
                yield
                tt("dve", M3[:, :, 0], BCs[:, 32:48], G3[:, :, 0], ALU.max)
                yield
                for t in range(1, 4):
                    tt("dve", M3[:, :, t], M3[:, :, t - 1], G3[:, :, t], ALU.max)
                    yield
                cp("dve", BCs[:, 16:32], M3[:, :, 3])
                yield
                tt("dve", MLo[:, 1:17], B3[:, :, 3], M3[:, :, 3], ALU.add)
                yield
                cp("dve", reps[:, 0, :].rearrange("h (s t) -> h s t", t=4), mkap(BCs, 32, [4, [1, 16], [0, 4]]))
                yield
                cp("dve", reps[:, 1, :].rearrange("h (s t) -> h s t", t=4), mkap(BCs, 16, [4, [1, 16], [0, 4]]))
                yield
                items = []
                for q, R_ in ((Q_B, Br[:, 0:64]), (Q_G, Gr[:, 0:64]), (Q_M, Mr[:, 0:64]), (Q_R, reps[:, 0, :]), (Q_E, reps[:, 1, :])):
                    col = (q * 17 + 16) * 4
                    items.append((psT[0:64, col:col + 4], R_, identF[0:4, 0:4]))
                trs(items)
                yield
        cp("dve", TM[:, :, :, :].rearrange("p q c h -> p (q c h)"), psT[:, 0:340])
        yield
        S.dma("sp", st, D["om_p"][e:e + 1, :].rearrange("a h -> h a"), MLo[:, 0:1], **NCD)
        yield
        S.dma("sp", st, D["om_s"][e].rearrange("s h -> h s"), MLo[:, 1:17], **NCD)
        yield
        psB = nps()
        mms([(psB[:, 48 * h:48 * h + 48], selH[:, h, :], BCs[:, :]) for h in range(4)])
        yield
        cp("dve", BC[:, :, :].rearrange("p h c -> p (h c)"), psB[:, 0:192])
        yield
        memset("dve", TM[:, Q_R, 0, :], 0.0)
        yield
        cp("dve", TM[:, Q_R, 1:16, :], mkap(BC, 0, [128, [1, 15], [48, 4]]))
        yield
        cp("dve", TM[:, Q_E, 0:16, :], mkap(BC, 0, [128, [1, 16], [48, 4]]))
        yield
        fl = lambda v: v.rearrange("p c h -> p (c h)")
        tmp = AR.alloc("tmpd", [128, 68], F32)
        tt("dve", tmp, fl(TM[:, Q_G]), fl(TM[:, Q_R]), ALU.subtract)
        yield
        act(fl(DR[:, 0]), tmp, AF.Exp, bias=LNKS_AP(g_), scale=1.0)
        yield
        tmp2 = AR.alloc("tmpd2", [128, 68], F32)
        tt("dve", tmp2, fl(TM[:, Q_R]), fl(TM[:, Q_M]), ALU.subtract)
        yield
        act(fl(DR[:, 1]), tmp2, AF.Exp)
        yield
        tmp3 = tmp
        tt("dve", tmp3, fl(TM[:, Q_G]), fl(TM[:, Q_E]), ALU.subtract)
        yield
        act(fl(DR[:, 2]), tmp3, AF.Exp, bias=LNKS_AP(g_), scale=1.0)
        yield
        tmp4 = tmp2
        tt("dve", tmp4, fl(TM[:, Q_B]), fl(TM[:, Q_M]), ALU.add)
        yield
        act(fl(DR[:, 3]), tmp4, AF.Exp, scale=-1.0)
        yield
        tmp5 = tmp
        tt("dve", tmp5, fl(TM[:, Q_R]), fl(TM[:, Q_E]), ALU.subtract)
        yield
        act(fl(DR[:, 4]), tmp5, AF.Exp)
        yield
        tmp6 = AR.alloc("tmpd6", [128, 4, 16], F32)
        tt("dve", tmp6, BC[:, :, 32:48], BC[:, :, 16:32], ALU.subtract)
        yield
        act(ALs, tmp6, AF.Exp)
        yield
        AR.release("tmpd", "tmpd2", "tmpd6", "wgt", "carry", "BCs", "MLo", "reps",
                   "row_IG", "row_LF", "row_B", "row_G", "row_M")
        GV.update(DR=DR, ALs=ALs)

    def pool_gen():
        RUN = 15 + 2048
        RT = RUN + 16 * 19
        rp = ring_next()
        wp = rp[0][:, :].rearrange("p (c f) -> p c f", c=8)
        wload(rp, wp, Win[:, 0:512].rearrange("(c p) f -> p c f", p=128))
        yield
        wpl = AR.alloc("wpl", [128, 4, 128], BF16)
        S.dma("pool", ld, wpl, D["w_pool"][e].rearrange("g c d -> c g d"))
        yield
        YA = AR.alloc("YA", [128, 4, NT_], BF16)
        PT = AR.alloc("PT", [128, RT], F32)
        SA = AR.alloc("SA", [128, RT], F32)
        SB = AR.alloc("SB", [128, RT], F32)
        Dd = AR.alloc("Dd", [128, NT_], BF16)
        memset("pool", SA, 0.0)
        yield
        memset("pool", SB, 0.0)
        yield
        SPL = [AR.alloc("SPL%d" % i, [120, 512], F32) for i in range(2)]
        for hf in range(2):
            S.dma("sp", ld, SPL[hf], D["spool"][e, 8 * hf:8 * hf + 8].rearrange("s r f -> (s r) f"))
            yield
        sview = lambda buf: mkap(buf, RUN, [128, [19, 16], [1, 19]])
        for g in range(4):
            w = 2 ** (g + 1)
            memset("pool", PT[:, 0:15], 0.0)
            yield
            for hf in range(2):
                ps = nps()
                trs([(ps[:, 0:120], SPL[hf][:, g * 128:(g + 1) * 128], identF[0:120, 0:120])])
                yield
                cp("dve", sview(PT)[:, 8 * hf:8 * hf + 8, 0:15], ps[:, 0:120].rearrange("p (s r) -> p s r", r=15))
                yield
            for ti, (t0, tn) in enumerate(TT):
                ps = nps()
                mmg(ps[:, 0:tn], [(wp[:, dc, g * 128:(g + 1) * 128], H[:, dc, t0:t0 + tn]) for dc in range(8)])
                yield
                if ti < 4:
                    act(PT[:, 15 + t0:15 + t0 + 512], ps[:, 0:512], AF.Copy)
                    yield
                else:
                    act(sview(PT)[:, :, 15:19], ps[:, 0:64].rearrange("p (s t) -> p s t", t=4), AF.Copy)
                    yield
            src = PT
            bufs = [SA, SB]
            for k in range(g + 1):
                sh = 2 ** k
                dst = bufs[k % 2]
                tt("dve", dst[:, sh:RT], src[:, sh:RT], src[:, 0:RT - sh], ALU.add)
                yield
                src = dst
            stt(Dd[:, 0:2048], src[:, 15:RUN], 1.0 / w, PT[:, 15:RUN], ALU.mult, ALU.subtract)
            yield
            fx = AR.alloc("fx", [128, 16], F32)
            tt("dve", fx, src[:, 15:31], invc[:, g, :], ALU.mult)
            yield
            tt("dve", Dd[:, 0:16], fx, PT[:, 15:31], ALU.subtract)
            yield
            AR.release("fx")
            stt(Dd[:, 2048:2112].rearrange("p (s t) -> p s t", t=4), sview(src)[:, :, 15:19], 1.0 / w, sview(PT)[:, :, 15:19], ALU.mult, ALU.subtract)
            yield
            for ti, (t0, tn) in enumerate(TT):
                ps = nps()
                mmg(ps[:, 0:tn], [(wpl[:, g, :], Dd[:, t0:t0 + tn])])
                yield
                act(YA[:, g, t0:t0 + tn], ps[:, 0:tn], AF.Identity, bias=0.0, scale=pscale[:, e, g:g + 1])
                yield
        PO = AR.alloc("PO", [64, 512], F32)
        ps = nps()
        mmg(ps[0:16, :], [(H[:, dc, 2032:2048], wp[:, dc, :]) for dc in range(8)])
        yield
        cp("dve", PO[0:16, :], ps[0:16, :])
        yield
        S.dma("sp", st, D["opool_p"][e], PO[1:16, :])
        yield
        PO2 = AR.alloc("PO2", [64, 512], F32)
        ps = nps()
        mmg(ps[0:64, :], [(H[:, dc, 2048:2112], wp[:, dc, :]) for dc in range(8)])
        yield
        cp("dve", PO2[0:64, :], ps[0:64, :])
        yield
        for s_ in range(16):
            S.dma("sp", st, D["opool_s"][e, s_, 11:15, :], PO2[4 * s_:4 * s_ + 4, :])
            yield
        S.dma("sp", st, D["opool_s"][e, :, 0:11, :], D["spool"][e, :, 4:15, :], track_in=False)
        yield
        ra = ring_next()
        woa = ra[0][:, :].rearrange("p (c f) -> p c f", c=4)
        wload(ra, woa, Wout[0:512, :].rearrange("(c p) f -> p c f", p=128))
        yield
        for ti, (t0, tn) in enumerate(TT):
            for dc in range(8):
                ps = nps()
                mmg(ps[:, 0:tn], [(woa[:, g, dc * 128:(dc + 1) * 128], YA[:, g, t0:t0 + tn]) for g in range(4)])
                yield
                xadd(dc, t0, tn, ps)
                yield
        AR.release("YA", "PT", "SA", "SB", "Dd", "SPL0", "SPL1", "PO", "PO2", "wpl")


    WH = {}

    def load_head_w(h):
        rh = ring_next()
        wh = rh[0][:, :].rearrange("p (c j f) -> p c j f", c=8, j=4)
        for j, cbase in enumerate((512, 1024, 1536, 2048)):
            wload(rh, wh[:, :, j, :], Win[:, cbase + 128 * h:cbase + 128 * h + 128].rearrange("(c p) f -> p c f", p=128), new_group=(j == 0))
        WH[h] = wh

    def load_wob():
        rb = ring_next()
        wob = rb[0][:, :].rearrange("p (c f) -> p c f", c=4)
        wload(rb, wob, Wout[512:1024, :].rearrange("(c p) f -> p c f", p=128))
        WH["wob"] = wob

    load_head_w(0)
    interleave(gates_gen(), pool_gen())
    DR, ALs = GV["DR"], GV["ALs"]
    EG, SC, WS, EMT, AL = DR[:, 0], DR[:, 1], DR[:, 2], DR[:, 3], DR[:, 4]

    YB = AR.alloc("YB", [128, 4, NT_], BF16)
    PPb_ = [AR.alloc("PPb%d" % i, [128, 17, 128], BF16) for i in range(2)]
    PPd_ = [AR.alloc("PPd%d" % i, [128, 17], F32) for i in range(2)]
    SO_ = [AR.alloc("SO%d" % i, [128, 17, 128], BF16) for i in range(2)]
    SSQ_ = [AR.alloc("SSQ%d" % i, [128, 17], F32) for i in range(2)]
    junk = AR.alloc("junk", [128, 128], BF16)
    QT = [AR.alloc("QT%d" % i, [128, 128], BF16) for i in range(3)]
    KT = [AR.alloc("KT%d" % i, [128, 128], BF16) for i in range(2)]
    KW = [AR.alloc("KW%d" % i, [128, 128], BF16) for i in range(2)]
    Vc = [AR.alloc("Vc%d" % i, [128, 129], BF16) for i in range(2)]
    Sp = [AR.alloc("Sp%d" % i, [128, 128], BF16) for i in range(2)]
    OF = [AR.alloc("OF%d" % i, [128, 128], F32) for i in range(2)]
    Cst = AR.alloc("Cst", [128, 129], F32)
    Cb = AR.alloc("Cb", [128, 129], BF16)
    C0 = AR.alloc("C0", [128, 16, 129], F32)
    C0b = AR.alloc("C0b", [128, 16, 129], BF16)
    Vblk = AR.alloc("Vblk", [64, 16, 129], BF16)
    QZ = AR.alloc("QZ", [128, 16, 64], BF16)
    GH = AR.alloc("GH", [128, 512], F32)
    NT0 = AR.alloc("NT0", [128, 64], F32)
    NTO = AR.alloc("NTO", [128, 4, 16], F32)
    NPO = AR.alloc("NPO", [128, 4], F32)
    post = AR.alloc("post", [128, 6, 17], F32)
    S.dma("sp", ld, GH, D["g_head"][e:e + 1, :].to_broadcast([128, 512]))
    for i in range(2):
        memset("pool", Vc[i][:, 128:129], 1.0)
    for i in range(2):
        memset("dve", PPb_[i], 0.0)
        memset("dve", PPd_[i], 0.0)
        memset("dve", SO_[i], 0.0)
        memset("dve", SSQ_[i], 0.0)
    memset("dve", QZ, 0.0)
    nrow = AR.alloc("nrow", [64, 128], F32)
    S.dma("sp", ld, nrow, D["sn"][e].rearrange("s h k -> (s h) k"))
    ps = nps()
    trs([(ps[:, 0:64], nrow, identF[0:64, 0:64])])
    cp("dve", NT0, ps[:, 0:64])
    AR.release("nrow")
    k = 0

    def head_gen(h):
        PPb, PPd, SO, SSQ = PPb_[h % 2], PPd_[h % 2], SO_[h % 2], SSQ_[h % 2]
        wh = WH[h]
        memset("pool", Cst, 0.0)
        memset("pool", Cb, 0.0)
        def stA(c):
            t0, tn = CH[c]
            qt, kt = QT[c % 3], KT[c % 2]
            psQ = nps()
            mmg(psQ[:, 0:tn], [(wh[:, dc, 0, :], H[:, dc, t0:t0 + tn]) for dc in range(8)])
            psK = nps()
            mmg(psK[:, 0:tn], [(wh[:, dc, 1, :], H[:, dc, t0:t0 + tn]) for dc in range(8)])
            act(qt[:, 0:tn], psQ[:, 0:tn], AF.Copy)
            cp("dve", kt[:, 0:tn], psK[:, 0:tn])

        def stB(c):
            t0, tn = CH[c]
            qt, kt, kw, vc, sp = QT[c % 3], KT[c % 2], KW[c % 2], Vc[c % 2], Sp[c % 2]
            psS = nps()
            mms([(psS[0:tn, 0:tn], kt[:, 0:tn], qt[:, 0:tn])])
            ps3 = nps()
            mmg(ps3[0:tn, 0:384], [(H[:, dc, t0:t0 + tn], wh[:, dc, 1:4, :].rearrange("p j f -> p (j f)")) for dc in range(8)])
            mask = causal if c < 16 else bmask[:, :, :].rearrange("p s r -> p (s r)")
            stt(sp[0:tn, 0:tn], psS[0:tn, 0:tn], EG[0:tn, c, h:h + 1], mask[0:tn, 0:tn], ALU.mult, ALU.mult)
            ts("dve", kw[0:tn, :], ps3[0:tn, 0:128], WS[0:tn, c, h:h + 1], ALU.mult)
            cp("dve", vc[0:tn, 0:128], ps3[0:tn, 128:256])
            of = OF[c % 2]
            cp("dve", of[0:tn, :], ps3[0:tn, 256:384])
            act(SO[0:tn, c, :], of[0:tn, :], AF.Sigmoid)

        def stC(c):
            t0, tn = CH[c]
            qt, kw, vc, sp = QT[c % 3], KW[c % 2], Vc[c % 2], Sp[c % 2]
            psP = nps()
            if c < 16:
                psU = nps()
                mms([(psU[:, 0:129], kw[:, :], vc[:, :])])
                mmg(psP[:, 0:129], [(qt[:, :], Cb[:, :]), (sp[:, :], vc[:, :])])
            else:
                cp("pool", mkap(QZ, 0, [128, [68, 16], [1, 4]]), qt[:, 0:64].rearrange("p (s t) -> p s t", t=4))
                mmg(psP[0:64, 0:129], [(QZ[:, s_, :], C0b[:, s_, :]) for s_ in range(16)] + [(sp[0:64, 0:64], vc[0:64, :])])
            act(PPb[0:tn, c, :], psP[0:tn, 0:128], AF.Copy)
            act(PPd[0:tn, c:c + 1], psP[0:tn, 128:129], AF.Copy)
            act(junk[0:tn, :], psP[0:tn, 0:128], AF.Square, accum=SSQ[0:tn, c:c + 1])
            if c < 16:
                stt(Cst, Cst, AL[:, c, h:h + 1], psU[:, 0:129], ALU.mult, ALU.add)
                cp("act", Cb, Cst)
            else:
                tt("pool", Vblk, mkap(vc, 0, [64, [0, 16], [1, 129]]), mkap(bones, 0, [64, [1, 16], [0, 129]]), ALU.mult)
                for s0 in range(0, 16, 3):
                    ns = min(3, 16 - s0)
                    psU2 = nps()
                    mms([(psU2[:, 0:ns * 129], kw[0:64, :], Vblk[:, s0:s0 + ns, :].rearrange("p s f -> p (s f)"))])
                    for j in range(ns):
                        s_ = s0 + j
                        stt(C0[:, s_, :], C0[:, s_, :], ALs[:, h, s_:s_ + 1], psU2[:, j * 129:(j + 1) * 129], ALU.mult, ALU.add)
                for s4 in range(0, 16, 4):
                    S.dma("sp", st, D["oc_s"][e, s4:s4 + 4, h, :, :].rearrange("s k v -> k s v"), C0[:, s4:s4 + 4, 0:128])
                cp("dve", NTO[:, h, :], C0[:, :, 128])
            if c == 15:
                S.dma("sp", st, D["oc_p"][e, h], Cst[:, 0:128])
                cp("dve", NPO[:, h:h + 1], Cst[:, 128:129])

        stA(0)
        yield
        for i_ in range(17):
            if i_ == 1:
                if h + 1 < 4:
                    load_head_w(h + 1)
                else:
                    load_wob()
            if i_ == 10:
                for s4 in range(0, 16, 4):
                    S.dma("sp", ld, C0[:, s4:s4 + 4, 0:128], D["sc"][e, s4:s4 + 4, h, :, :].rearrange("s k v -> k s v"))
                cp("dve", C0[:, :, 128], mkap(NT0, h, [128, [4, 16]]))
                cp("pool", C0b, C0)
            if i_ + 1 < 17:
                stA(i_ + 1)
                yield
            stB(i_)
            yield
            if i_ >= 1:
                stC(i_ - 1)
                yield
        stC(16)
        yield

    def post_gen(h):
        PPb, PPd, SO, SSQ = PPb_[h % 2], PPd_[h % 2], SO_[h % 2], SSQ_[h % 2]
        den = PPd
        t1, t2, rr, t4, rsd, tot = (post[:, i, :] for i in range(6))
        tt("dve", t1, den, SC[:, :, h], ALU.mult)
        yield
        act(t1, t1, AF.Abs)
        yield
        tt("dve", t2, t1, EMT[:, :, h], ALU.max)
        yield
        S.op("dve", lambda en, t2=t2: en.reciprocal(out=t2, in_=t2), reads=[t2], writes=[t2])
        yield
        tt("dve", rr, t2, SC[:, :, h], ALU.mult)
        yield
        tt("dve", t4, rr, rr, ALU.mult)
        yield
        tt("dve", t4, t4, SSQ, ALU.mult)
        yield
        act(rsd, t4, AF.Sqrt, bias=EPS, scale=1.0 / 128)
        yield
        S.op("dve", lambda en, rsd=rsd: en.reciprocal(out=rsd, in_=rsd), reads=[rsd], writes=[rsd])
        yield
        tt("dve", tot, rr, rsd, ALU.mult)
        yield
        tt("pool", SO, SO, mkap(GH, 128 * h, [128, [0, 17], [1, 128]]), ALU.mult)
        yield
        for c in range(17):
            tn = CH[c][1]
            stt(SO[0:tn, c, :], PPb[0:tn, c, :], tot[0:tn, c:c + 1], SO[0:tn, c, :], ALU.mult, ALU.mult)
            yield
        for c0 in range(0, 17, 4):
            ncx = min(4, 17 - c0)
            ps = nps()
            psb = ps[:, :].bitcast(BF16)
            items = []
            for j in range(ncx):
                c = c0 + j
                tn = CH[c][1]
                items.append((psb[:, j * 128:j * 128 + tn], SO[0:tn, c, :], identB[0:tn, 0:tn]))
            trs(items)
            yield
            if c0 < 16:
                cp("act", YB[:, h, 128 * c0:128 * c0 + 512], psb[:, 0:512])
                yield
            else:
                cp("act", YB[:, h, 2048:2112], psb[:, 0:64])
                yield

    for h in range(4):
        if h > 0:
            interleave(head_gen(h), post_gen(h - 1))
        else:
            interleave(head_gen(h))
    interleave(post_gen(3))
    ps = nps()
    trs([(ps[0:64, 0:128], NTO[:, :, :].rearrange("p h s -> p (h s)"), identF)])
    nout = AR.alloc("nout", [64, 128], F32)
    cp("dve", nout, ps[0:64, 0:128])
    for h in range(4):
        S.dma("sp", st, D["on_s"][e, :, h, :], nout[16 * h:16 * h + 16, :])
    ps = nps()
    trs([(ps[0:4, 0:128], NPO, identF)])
    nout2 = AR.alloc("nout2", [4, 128], F32)
    cp("dve", nout2, ps[0:4, 0:128])
    S.dma("sp", st, D["on_p"][e], nout2)
    wob = WH["wob"]
    for ti, (t0, tn) in enumerate(TT):
        for dc in range(8):
            ps = nps()
            mmg(ps[:, 0:tn], [(wob[:, hh, dc * 128:(dc + 1) * 128], YB[:, hh, t0:t0 + tn]) for hh in range(4)])
            xadd(dc, t0, tn, ps)
    AR.release("H", "TM", "DR", "BC", "ALs", "YB", "PPb0", "PPb1", "PPd0", "PPd1", "SO0", "SO1", "SSQ0", "SSQ1", "junk", "QT0", "QT1", "QT2", "KT0", "KT1", "KW0", "KW1",
               "Vc0", "Vc1", "Sp0", "Sp1", "OF0", "OF1", "Cst", "Cb", "C0", "C0b", "Vblk", "QZ", "GH", "NT0", "NTO", "NPO",
               "post", "nout", "nout2")


def LNKS_AP(g_):
    return g_["lnks"]


from contextlib import ExitStack
from concourse.bass_utils import run_bass_kernel_spmd
import math

NP_ = 2048
NS_ = 64
NT_ = NP_ + NS_
TT = [(0, 512), (512, 512), (1024, 512), (1536, 512), (2048, 64)]
CH = [(128 * c, 128) for c in range(16)] + [(2048, 64)]
EPS = 1e-6
DFF = 2816
NFC = 22
LNKS = math.log(128.0 ** -0.5)
DEBUG_STOP = None


class Arena:
    def __init__(self, nc, es, nbytes):
        self.t = es.enter_context(nc.sbuf_tensor("arena", [128, nbytes // 2], BF16))
        self.nbytes = nbytes
        self.free = [(0, nbytes)]
        self.live = {}
        self.peak = 0

    def alloc(self, name, shape, dt):
        esz = 2 if dt == BF16 else 4
        n = 1
        for s in shape[1:]:
            n *= s
        nb = (n * esz + 63) // 64 * 64
        for i, (o, sz) in enumerate(self.free):
            if sz >= nb:
                off = o
                if sz == nb:
                    self.free.pop(i)
                else:
                    self.free[i] = (o + nb, sz - nb)
                break
        else:
            raise RuntimeError("arena OOM for %s (%d bytes); live=%s" % (name, nb, {k: v[1] for k, v in self.live.items()}))
        assert name not in self.live, name
        self.live[name] = (off, nb)
        used = self.nbytes - sum(s for _, s in self.free)
        self.peak = max(self.peak, used)
        base = self.t[0:shape[0], off // 2: off // 2 + nb // 2]
        if dt != BF16:
            base = base.bitcast(dt)
        v = base[:, 0:n]
        if len(shape) > 2:
            names = " ".join("d%d" % i for i in range(len(shape) - 1))
            kw = {"d%d" % i: shape[i + 1] for i in range(len(shape) - 2)}
            v = v.rearrange("p (%s) -> p %s" % (names, names), **kw)
        return v

    def release(self, *names):
        for name in names:
            off, nb = self.live.pop(name)
            self.free.append((off, nb))
        self.free.sort()
        m = []
        for o, s in self.free:
            if m and m[-1][0] + m[-1][1] == o:
                m[-1] = (m[-1][0], m[-1][1] + s)
            else:
                m.append((o, s))
        self.free = m


def mkap(base, off, dims):
    pst = list(base.ap)[0][0]
    return bass.AP(tensor=base.tensor, offset=base.offset + off, ap=[[pst, dims[0]]] + [list(d) for d in dims[1:]])


def build_program():
    nc = bass.Bass("TRN2", target_bir_lowering=False)
    D = {}

    def din(name, shape):
        D[name] = nc.dram_tensor(name, list(shape), F32, kind="ExternalInput").ap()

    def dout(name, shape):
        D[name] = nc.dram_tensor(name, list(shape), F32, kind="ExternalOutput").ap()

    din("xp", [2048, 1024]); din("xs", [64, 1024]); din("spool", [2, 16, 15, 512])
    din("sc", [2, 16, 4, 128, 128]); din("sn", [2, 16, 4, 128]); din("sm", [2, 16, 4]); din("sconv", [4, 16, 2, 2816])
    din("g_mix", [4, 1024]); din("w_in_even", [2, 1024, 2568]); din("b_gates", [2, 8]); din("w_pool", [2, 4, 128, 128])
    din("pool_scale", [2, 512]); din("g_head", [2, 512]); din("w_out_even", [2, 1024, 1024]); din("w_in_odd", [2, 1024, 2048])
    din("ln_v_g", [2, 1024]); din("ln_v_b", [2, 1024]); din("w_spatial", [2, 4, 128, 128]); din("b_spatial", [2, 4, 128])
    din("w_out_odd", [2, 1024, 1024]); din("g_ffn", [4, 1024]); din("w_ffn_gate", [4, 1024, 2816]); din("w_ffn_up", [4, 1024, 2816])
    din("conv_w", [4, 3, 2816]); din("conv_b", [4, 2816]); din("w_ffn_down", [4, 2816, 1024]); din("g_final", [1024])
    dout("yp", [2048, 1024]); dout("ys", [64, 1024]); dout("opool_p", [2, 15, 512]); dout("opool_s", [2, 16, 15, 512])
    dout("oc_p", [2, 4, 128, 128]); dout("oc_s", [2, 16, 4, 128, 128]); dout("on_p", [2, 4, 128]); dout("on_s", [2, 16, 4, 128])
    dout("om_p", [2, 4]); dout("om_s", [2, 16, 4]); dout("oconv_p", [4, 2, 2816]); dout("oconv_s", [4, 16, 2, 2816])
    dout("ov_s", [2, 64, 1024])
    if DEBUG_STOP is not None:
        dout("dbg_x", [128, 8, NT_])

    with ExitStack() as es:
        S = Sched(nc, es)
        AR = Arena(nc, es, 211968)
        PS = [es.enter_context(nc.psum_tensor("ps%d" % i, [128, 512], F32)) for i in range(8)]
        psrr = [0]

        def nps():
            psrr[0] = (psrr[0] + 1) % 6
            return PS[psrr[0]]

        def aps(*xs):
            return [x for x in xs if x is not None and not isinstance(x, (int, float))]

        def act(out, in_, func, bias=None, scale=None, accum=None):
            kw = {}
            if bias is not None:
                kw["bias"] = bias
            if scale is not None:
                kw["scale"] = scale
            if accum is not None:
                kw["accum_out"] = accum
            S.op("act", lambda e: e.activation(out=out, in_=in_, func=func, **kw),
                 reads=aps(in_, bias, scale), writes=aps(out, accum))

        def tt(eng, out, in0, in1, op):
            S.op(eng, lambda e: e.tensor_tensor(out=out, in0=in0, in1=in1, op=op), reads=[in0, in1], writes=[out])

        def ts(eng, out, in0, s1, op0, s2=None, op1=None):
            if op1 is None:
                S.op(eng, lambda e: e.tensor_scalar(out=out, in0=in0, scalar1=s1, scalar2=None, op0=op0),
                     reads=aps(in0, s1), writes=[out])
            else:
                S.op(eng, lambda e: e.tensor_scalar(out=out, in0=in0, scalar1=s1, scalar2=s2, op0=op0, op1=op1),
                     reads=aps(in0, s1, s2), writes=[out])

        def stt(out, in0, sc, in1, op0, op1):
            S.op("dve", lambda e: e.scalar_tensor_tensor(out=out, in0=in0, scalar=sc, in1=in1, op0=op0, op1=op1),
                 reads=aps(in0, sc, in1), writes=[out])

        def cp(eng, out, in_):
            if eng == "act":
                act(out, in_, AF.Copy)
            else:
                S.op(eng, lambda e: e.tensor_copy(out=out, in_=in_), reads=[in_], writes=[out])

        def memset(eng, out, val):
            S.op(eng, lambda e: e.memset(out, val), writes=[out])

        def mmg(out, pairs):
            n = len(pairs)

            def fn(e):
                inst = None
                for i, (l, r) in enumerate(pairs):
                    inst = e.matmul(out, lhsT=l, rhs=r, start=(i == 0), stop=(i == n - 1))
                return inst
            rd = []
            for l, r in pairs:
                rd.append(l)
                rd.append(r)
            S.op("pe", fn, reads=rd, writes=[out])

        def mms(items):
            def fn(e):
                inst = None
                for (o, l, r) in items:
                    inst = e.matmul(o, lhsT=l, rhs=r, start=True, stop=True)
                return inst
            S.op("pe", fn, reads=[x for it in items for x in it[1:]], writes=[it[0] for it in items])

        def trs(items):
            def fn(e):
                inst = None
                for (o, i_, idn) in items:
                    inst = e.transpose(o, i_, idn)
                return inst
            S.op("pe", fn, reads=[x for it in items for x in it[1:]], writes=[it[0] for it in items])

        X = AR.alloc("X", [128, 8, NT_], F32)
        identF = AR.alloc("identF", [128, 128], F32)
        identB = AR.alloc("identB", [128, 128], BF16)
        causal = AR.alloc("causal", [128, 128], F32)
        onesF = AR.alloc("onesF", [128, 128], F32)
        onesB = AR.alloc("onesB", [128, 128], BF16)
        selH = AR.alloc("selH", [4, 4, 128], F32)
        bmask = AR.alloc("bmask", [64, 16, 4], F32)
        bones = AR.alloc("bones", [64, 16], F32)
        repS = AR.alloc("repS", [4, 16, 4], F32)
        invc = AR.alloc("invc", [128, 4, 16], F32)
        gmix = AR.alloc("gmix", [128, 4, 8], F32)
        gffn = AR.alloc("gffn", [128, 4, 8], F32)
        gfin = AR.alloc("gfin", [128, 8], F32)
        convw = AR.alloc("convw", [128, 4, 3, NFC], F32)
        convb = AR.alloc("convb", [128, 4, NFC], F32)
        pscale = AR.alloc("pscale", [128, 2, 4], F32)
        bgate = AR.alloc("bgate", [4, 2, 2], F32)
        nbf = AR.alloc("nbf", [4, 2], F32)
        RING = [AR.alloc("ring%d" % i, [128, 4096], BF16) for i in range(4)]
        RSLOT = [S.slot("wsem%d" % i) for i in range(4)]
        ringi = [0]
        ld = None
        st = None

        def ring_next():
            i = ringi[0] % 4
            ringi[0] += 1
            return RING[i], RSLOT[i]

        def wload(slotpair, dst, src, new_group=True):
            S.dma("pool", slotpair[1], dst, src, new_group=new_group)

        memset("pool", onesF, 1.0)
        S.op("pool", lambda e: e.affine_select(out=causal, in_=onesF, pattern=[[1, 128]], compare_op=ALU.is_ge, fill=0.0, base=0, channel_multiplier=-1), reads=[onesF], writes=[causal])
        S.op("pool", lambda e: e.affine_select(out=identF, in_=onesF, pattern=[[1, 128]], compare_op=ALU.is_equal, fill=0.0, base=0, channel_multiplier=-1), reads=[onesF], writes=[identF])
        cp("pool", identB, identF)
        cp("pool", onesB, onesF)
        for h in range(4):
            S.op("pool", lambda e, h=h: e.affine_select(out=selH[:, h, :], in_=onesF[0:4, :], pattern=[[0, 128]], compare_op=ALU.is_equal, fill=0.0, base=-h, channel_multiplier=1), reads=[onesF[0:4, :]], writes=[selH[:, h, :]])
        S.op("pool", lambda e: e.affine_select(out=bones, in_=onesF[0:64, 0:16], pattern=[[-4, 16]], compare_op=ALU.is_ge, fill=0.0, base=0, channel_multiplier=1), reads=[onesF[0:64, 0:16]], writes=[bones])
        S.op("pool", lambda e: e.affine_select(out=bones, in_=bones, pattern=[[4, 16]], compare_op=ALU.is_ge, fill=0.0, base=3, channel_multiplier=-1), reads=[bones], writes=[bones])
        tt("pool", bmask, mkap(bones, 0, [64, [1, 16], [0, 4]]), causal[0:64, 0:64].rearrange("p (s r) -> p s r", r=4), ALU.mult)
        S.op("pool", lambda e: e.affine_select(out=repS, in_=onesF[0:4, 0:64].rearrange("p (s r) -> p s r", r=4), pattern=[[0, 16], [1, 4]], compare_op=ALU.is_equal, fill=0.0, base=0, channel_multiplier=-1), reads=[onesF[0:4, 0:64]], writes=[repS])
        itmp = AR.alloc("itmp", [128, 16], I32)
        S.op("pool", lambda e: e.iota(itmp, pattern=[[1, 16]], base=1, channel_multiplier=0), writes=[itmp])
        cp("dve", invc[:, 0, :], itmp)
        for g in range(1, 4):
            ts("dve", invc[:, g, :], invc[:, 0, :], float(2 ** (g + 1)), ALU.min)
        ts("dve", invc[:, 0, :], invc[:, 0, :], 2.0, ALU.min)
        S.op("dve", lambda e: e.reciprocal(out=invc, in_=invc), reads=[invc], writes=[invc])
        AR.release("itmp")

        NCD = dict(allow_slow_non_contiguous=True)
        S.dma("act", ld, gmix, D["g_mix"].rearrange("l (c p) -> p l c", p=128), **NCD)
        S.dma("act", ld, gffn, D["g_ffn"].rearrange("l (c p) -> p l c", p=128), **NCD)
        S.dma("act", ld, gfin, D["g_final"].rearrange("(c p) -> p c", p=128), **NCD)
        S.dma("act", ld, pscale, D["pool_scale"].rearrange("e (g p) -> p e g", p=128), **NCD)
        S.dma("act", ld, bgate, D["b_gates"].rearrange("e (a h) -> h e a", a=2), **NCD)
        ts("dve", nbf, bgate[:, :, 1], -1.0, ALU.mult)
        for l in range(4):
            for j in range(3):
                S.dma("act", ld, convw[:, l, j, :], D["conv_w"][l, j, :].rearrange("(c p) -> p c", p=128), **NCD)
            S.dma("act", ld, convb[:, l, :], D["conv_b"][l, :].rearrange("(c p) -> p c", p=128), **NCD)

        XT = [AR.alloc("XT%d" % i, [128, 1024], F32) for i in range(6)]
        for c, (t0, tn) in enumerate(CH):
            xt = XT[c % 6]
            src = D["xp"][t0:t0 + tn, :] if c < 16 else D["xs"]
            S.dma("sp", ld, xt[0:tn, :], src)
            for half in range(2):
                ps = nps()
                trs([(ps[:, j * 128:j * 128 + tn], xt[0:tn, (4 * half + j) * 128:(4 * half + j + 1) * 128], identF[0:tn, 0:tn]) for j in range(4)])
                cp("act" if half == 0 else "dve", X[:, 4 * half:4 * half + 4, t0:t0 + tn],
                   ps[:, :].rearrange("p (j t) -> p j t", t=128)[:, :, 0:tn])
        AR.release(*["XT%d" % i for i in range(6)])

        def rmsnorm(gcol, H):
            SQ = [AR.alloc("SQ%d" % i, [128, 8, 512], BF16) for i in range(2)]
            RS = [AR.alloc("RS%d" % i, [128, 512], F32) for i in range(2)]
            for ti, (t0, tn) in enumerate(TT):
                sq = SQ[ti % 2]
                rs = RS[ti % 2]
                act(sq[:, :, 0:tn], X[:, :, t0:t0 + tn], AF.Square)
                ps = nps()
                mmg(ps[:, 0:tn], [(onesB, sq[:, dc, 0:tn]) for dc in range(8)])
                act(rs[:, 0:tn], ps[:, 0:tn], AF.Sqrt, bias=EPS, scale=1.0 / 1024)
                S.op("dve", lambda e, rs=rs, tn=tn: e.reciprocal(out=rs[:, 0:tn], in_=rs[:, 0:tn]), reads=[rs[:, 0:tn]], writes=[rs[:, 0:tn]])
                for dc in range(8):
                    stt(H[:, dc, t0:t0 + tn], X[:, dc, t0:t0 + tn], gcol[:, dc:dc + 1], rs[:, 0:tn], ALU.mult, ALU.mult)
            AR.release("SQ0", "SQ1", "RS0", "RS1")

        def xadd(dc, t0, tn, ps):
            tt("dve", X[:, dc, t0:t0 + tn], X[:, dc, t0:t0 + tn], ps[:, 0:tn], ALU.add)

        def ffn(l):
            H = AR.alloc("H", [128, 8, NT_], BF16)
            rmsnorm(gffn[:, l, :], H)
            G = AR.alloc("G", [128, 4, NT_], BF16)
            A = [AR.alloc("A%d" % i, [128, 514], F32) for i in range(2)]
            T = [AR.alloc("T%d" % i, [128, 512], F32) for i in range(2)]
            GE = [AR.alloc("GE%d" % i, [128, 512], F32) for i in range(2)]
            As = AR.alloc("As", [128, 16, 6], F32)
            Ts = AR.alloc("Ts", [128, 16, 4], F32)
            CST = AR.alloc("CST", [128, NFC, 32], F32)
            CSs = AR.alloc("CSs", [128, NFC, 32], F32)
            CSp = AR.alloc("CSp", [128, 2, NFC], F32)
            CTM = AR.alloc("CTM", [32, DFF], F32)
            S.dma("sp", ld, CTM, D["sconv"][l].rearrange("s j f -> (s j) f"))
            for f0 in range(0, NFC, 4):
                nf = min(4, NFC - f0)
                ps = nps()
                trs([(ps[:, j * 32:(j + 1) * 32], CTM[0:32, (f0 + j) * 128:(f0 + j + 1) * 128], identF[0:32, 0:32]) for j in range(nf)])
                cp("dve", CST[:, f0:f0 + nf, :], ps[:, 0:nf * 32].rearrange("p (j t) -> p j t", t=32))
            AR.release("CTM")
            k = 0
            q_gelu, q_mul = [], []

            def step_pipe(s2_new):
                m_prev = q_mul.pop(0) if q_mul else None
                if q_gelu:
                    q_gelu.pop(0)()
                if m_prev is not None:
                    m_prev()
                q_gelu.append(s2_new)
            FW = {}

            def fload(kind, b):
                if b >= 6:
                    return
                c0 = 512 * b
                ncol = min(512, DFF - c0)
                nfc = ncol // 128
                r = ring_next()
                if kind == "g":
                    w_ = r[0][:, 0:8 * ncol].rearrange("p (c f) -> p c f", c=8)
                    wload(r, w_, D["w_ffn_gate"][l][:, c0:c0 + ncol].rearrange("(c p) f -> p c f", p=128))
                elif kind == "u":
                    w_ = r[0][:, 0:8 * ncol].rearrange("p (c f) -> p c f", c=8)
                    wload(r, w_, D["w_ffn_up"][l][:, c0:c0 + ncol].rearrange("(c p) f -> p c f", p=128))
                else:
                    w_ = r[0][:, 0:nfc * 1024].rearrange("p (c f) -> p c f", c=nfc)
                    wload(r, w_, D["w_ffn_down"][l][c0:c0 + ncol, :].rearrange("(c p) f -> p c f", p=128))
                FW[(kind, b)] = w_

            fload("g", 0)
            fload("u", 0)
            fload("d", 0)
            for b in range(6):
                c0 = 512 * b
                ncol = min(512, DFF - c0)
                nfc = ncol // 128
                fload("g", b + 1)
                wg, wu, wd = FW[("g", b)], FW[("u", b)], FW[("d", b)]
                for fc in range(nfc):
                    F = 4 * b + fc
                    w0 = convw[:, l, 0, F:F + 1]
                    w1 = convw[:, l, 1, F:F + 1]
                    w2 = convw[:, l, 2, F:F + 1]
                    cb = convb[:, l, F:F + 1]
                    memset("pool", A[0][:, 0:2], 0.0)
                    for ti, (t0, tn) in enumerate(TT):
                        psA = PS[(0, 1, 4)[k % 3]]
                        psU = PS[(2, 3, 6, 5)[k % 4]]
                        tb = T[k % 2]
                        ge = GE[k % 2]
                        k += 1
                        mmg(psA[:, 0:tn], [(wg[:, dc, fc * 128:(fc + 1) * 128], H[:, dc, t0:t0 + tn]) for dc in range(8)])
                        mmg(psU[:, 0:tn], [(wu[:, dc, fc * 128:(fc + 1) * 128], H[:, dc, t0:t0 + tn]) for dc in range(8)])
                        if ti < 4:
                            ab = A[ti % 2]
                            act(ab[:, 2:514], psA[:, 0:512], AF.Copy)
                            if ti < 3:
                                cp("pool", A[(ti + 1) % 2][:, 0:2], ab[:, 512:514])
                            else:
                                cp("pool", CSp[:, :, F], ab[:, 512:514])
                            ts("pool", tb, ab[:, 0:512], w0, ALU.mult, cb, ALU.add)
                            stt(tb, ab[:, 1:513], w1, tb, ALU.mult, ALU.add)
                            stt(tb, ab[:, 2:514], w2, tb, ALU.mult, ALU.add)
                            def s2(ge=ge, tb=tb, dst=G[:, fc, t0:t0 + 512], pu=psU[:, 0:512]):
                                act(ge, tb, AF.Gelu_apprx_tanh)
                                q_mul.append(lambda: tt("dve", dst, ge, pu, ALU.mult))
                            step_pipe(s2)
                        else:
                            cp("pool", As[:, :, 0:2], CST[:, F, :].rearrange("p (s j) -> p s j", j=2))
                            act(As[:, :, 2:6], psA[:, 0:64].rearrange("p (s t) -> p s t", t=4), AF.Copy)
                            ts("pool", Ts, As[:, :, 0:4], w0, ALU.mult, cb, ALU.add)
                            stt(Ts, As[:, :, 1:5], w1, Ts, ALU.mult, ALU.add)
                            stt(Ts, As[:, :, 2:6], w2, Ts, ALU.mult, ALU.add)
                            def s2(ge=ge, dst=G[:, fc, 2048:2112], pu=psU):
                                act(ge[:, 0:64], Ts[:, :, :].rearrange("p s t -> p (s t)"), AF.Gelu_apprx_tanh)
                                q_mul.append(lambda: tt("dve", dst, ge[:, 0:64], pu[:, 0:64], ALU.mult))
                            step_pipe(s2)
                            cp("pool", CSs[:, F, :].rearrange("p (s j) -> p s j", j=2), As[:, :, 4:6])
                while q_gelu or q_mul:
                    if q_gelu:
                        q_gelu.pop(0)()
                    while q_mul:
                        q_mul.pop(0)()
                fload("u", b + 1)
                fload("d", b + 1)
                for ti, (t0, tn) in enumerate(TT):
                    for dc in range(8):
                        psD = PS[(4, 5, 7)[(ti * 8 + dc) % 3]]
                        mmg(psD[:, 0:tn], [(wd[:, fc, dc * 128:(dc + 1) * 128], G[:, fc, t0:t0 + tn]) for fc in range(nfc)])
                        xadd(dc, t0, tn, psD)
            AR.release("G", "H", "A0", "A1", "T0", "T1", "GE0", "GE1", "As", "Ts", "CST")
            COp = AR.alloc("COp", [NFC, 2, 128], F32)
            for j in range(2):
                ps = nps()
                trs([(ps[0:NFC, 0:128], CSp[:, j, :], identF)])
                cp("dve", COp[:, j, :], ps[0:NFC, 0:128])
                S.dma("sp", st, D["oconv_p"][l, j, :].rearrange("(c p) -> c p", p=128), COp[:, j, :])
            COs = AR.alloc("COs", [32, NFC, 128], F32)
            for f0 in range(0, NFC, 4):
                nf = min(4, NFC - f0)
                ps = nps()
                trs([(ps[0:32, j * 128:(j + 1) * 128], CSs[:, f0 + j, :], identF) for j in range(nf)])
                cp("dve", COs[:, f0:f0 + nf, :], ps[0:32, 0:nf * 128].rearrange("p (j t) -> p j t", t=128))
            S.dma("sp", st, D["oconv_s"][l].rearrange("s j f -> (s j) f"), COs[:, :, :].rearrange("p c f -> p (c f)"))
            AR.release("CSs", "CSp", "COp", "COs")

        def odd(o, l):
            def _ld(src):
                r = ring_next()
                wv_ = r[0][:, :].rearrange("p (c f) -> p c f", c=8)
                wload(r, wv_, src.rearrange("(c p) f -> p c f", p=128))
                return wv_
            WU = [_ld(D["w_in_odd"][o][:, 512 * b:512 * b + 512]) for b in range(2)]
            wv0 = _ld(D["w_in_odd"][o][:, 1024:1536])
            wv1 = _ld(D["w_in_odd"][o][:, 1536:2048])
            H = AR.alloc("H", [128, 8, NT_], BF16)
            rmsnorm(gmix[:, l, :], H)
            UT = AR.alloc("UT", [128, 8, NT_], BF16)
            lng = AR.alloc("lng", [128, 1024], F32)
            lnb = AR.alloc("lnb", [128, 1024], F32)
            BSg = AR.alloc("BSg", [128, 4, 128], F32)
            BS64g = AR.alloc("BS64g", [128, 4, 16, 4], F32)
            WsT = AR.alloc("WsT", [128, 4, 128], BF16)
            W64 = AR.alloc("W64", [64, 4, 64], BF16)
            wsp = AR.alloc("wsp", [128, 4, 128], F32)
            X4 = AR.alloc("X4", [4, 4, 16, 4], F32)
            S.dma("sp", ld, lng, D["ln_v_g"][o:o + 1, :].to_broadcast([128, 1024]))
            S.dma("sp", ld, lnb, D["ln_v_b"][o:o + 1, :].to_broadcast([128, 1024]))
            S.dma("sp", ld, BSg[:, :, :].rearrange("p g r -> p (g r)"), D["b_spatial"][o:o + 1].rearrange("a g r -> a (g r)").to_broadcast([128, 512]))
            S.dma("sp", ld, wsp, D["w_spatial"][o].rearrange("g r s -> r g s"))
            cp("pool", BS64g, mkap(BSg, 0, [128, [128, 4], [0, 16], [1, 4]]))
            ps = nps()
            trs([(ps[:, g * 128:(g + 1) * 128], wsp[:, g, :], identF) for g in range(4)])
            psv = ps[:, :].rearrange("p (g r) -> p g r", g=4)
            tt("dve", WsT, psv, mkap(causal, 0, [128, [0, 4], [1, 128]]), ALU.mult)
            cp("dve", X4, mkap(psv, 0, [4, [128, 4], [0, 16], [1, 4]]))
            ps2 = nps()
            mms([(ps2[0:64, 0:256], repS[:, :, :].rearrange("k s r -> k (s r)"), X4[:, :, :, :].rearrange("k g s r -> k (g s r)"))])
            tt("dve", W64, ps2[0:64, 0:256].rearrange("p (g t) -> p g t", g=4),
               mkap(bmask, 0, [64, [0, 4], [1, 64]]), ALU.mult)
            AR.release("wsp", "X4")
            for b in range(2):
                wv = WU[b]
                for fc in range(4):
                    for ti, (t0, tn) in enumerate(TT):
                        ps = nps()
                        mmg(ps[:, 0:tn], [(wv[:, dc, fc * 128:(fc + 1) * 128], H[:, dc, t0:t0 + tn]) for dc in range(8)])
                        act(UT[:, 4 * b + fc, t0:t0 + tn], ps[:, 0:tn], AF.Gelu_apprx_tanh)
            WO = [_ld(D["w_out_odd"][o][:, 512 * b:512 * b + 512]) for b in range(2)]
            V = [AR.alloc("V%d" % i, [128, 1024], F32) for i in range(3)]
            VLb = [AR.alloc("VLb%d" % i, [128, 1024], BF16) for i in range(3)]
            STt = [AR.alloc("STt%d" % i, [128, 16], F32) for i in range(3)]
            TMP = [AR.alloc("TMPo%d" % i, [128, 4, 128], F32) for i in range(2)]
            def vstage(c):
                t0, tn = CH[c]
                v = V[c % 3]
                vb = VLb[c % 3]
                sv = STt[c % 3]
                for j, wvj in enumerate((wv0, wv1)):
                    ps = nps()
                    mmg(ps[0:tn, :], [(H[:, dc, t0:t0 + tn], wvj[:, dc, :]) for dc in range(8)])
                    act(v[0:tn, 512 * j:512 * j + 512], ps[0:tn, :], AF.Gelu_apprx_tanh)
                    S.op("dve", lambda e, sv=sv, v=v, j=j, tn=tn: e.bn_stats(out=sv[0:tn, 6 * j:6 * j + 6], in_=v[0:tn, 512 * j:512 * j + 512]),
                         reads=[v[0:tn, 512 * j:512 * j + 512]], writes=[sv[0:tn, 6 * j:6 * j + 6]])
                S.op("dve", lambda e, sv=sv, tn=tn: e.bn_aggr(out=sv[0:tn, 12:14], in_=sv[0:tn, 0:12]), reads=[sv[0:tn, 0:12]], writes=[sv[0:tn, 12:14]])
                act(sv[0:tn, 14:15], sv[0:tn, 13:14], AF.Sqrt, bias=EPS, scale=1.0)
                S.op("dve", lambda e, sv=sv, tn=tn: e.reciprocal(out=sv[0:tn, 14:15], in_=sv[0:tn, 14:15]), reads=[sv[0:tn, 14:15]], writes=[sv[0:tn, 14:15]])
                ts("dve", sv[0:tn, 15:16], sv[0:tn, 12:13], sv[0:tn, 14:15], ALU.mult, -1.0, ALU.mult)
                act(v[0:tn, :], v[0:tn, :], AF.Identity, bias=sv[0:tn, 15:16], scale=sv[0:tn, 14:15])
                tt("pool", v[0:tn, :], v[0:tn, :], lng[0:tn, :], ALU.mult)
                if c < 16:
                    tt("pool", vb[0:tn, :], v[0:tn, :], lnb[0:tn, :], ALU.add)
                else:
                    tt("pool", v[0:tn, :], v[0:tn, :], lnb[0:tn, :], ALU.add)
                    S.dma("sp", st, D["ov_s"][o], v[0:tn, :])
                    cp("pool", vb[0:tn, :], v[0:tn, :])

            def sstage(c):
                t0, tn = CH[c]
                vb = VLb[c % 3]
                for half in range(2):
                    ps = nps()
                    items = []
                    for j in range(4):
                        dj = 4 * half + j
                        g = dj // 2
                        rhs = WsT[:, g, :] if c < 16 else W64[:, g, :]
                        items.append((ps[:, j * 128:j * 128 + tn], vb[0:tn, dj * 128:(dj + 1) * 128], rhs))
                    mms(items)
                    tm = TMP[half]
                    psv = ps[:, :].rearrange("p (a b t) -> p a b t", a=2, b=2)[:, :, :, 0:tn]
                    if c < 16:
                        bias = mkap(BSg, 2 * half * 128, [128, [128, 2], [0, 2], [1, tn]])
                    else:
                        bias = mkap(BS64g, 2 * half * 64, [128, [64, 2], [0, 2], [1, 64]])
                    tt("dve", tm[:, :, :].rearrange("p (a b) t -> p a b t", a=2)[:, :, :, 0:tn], psv, bias, ALU.add)
                    uv = UT[:, 4 * half:4 * half + 4, t0:t0 + tn]
                    tt("dve", uv, tm[:, :, 0:tn], uv, ALU.mult)
            for c in range(17):
                vstage(c)
                if c >= 2:
                    sstage(c - 2)
            sstage(15)
            sstage(16)
            AR.release("V0", "V1", "V2", "VLb0", "VLb1", "VLb2", "STt0", "STt1", "STt2", "TMPo0", "TMPo1", "H")
            for b in range(2):
                wo = WO[b]
                for dcl in range(4):
                    dc = 4 * b + dcl
                    for ti, (t0, tn) in enumerate(TT):
                        ps = nps()
                        mmg(ps[:, 0:tn], [(wo[:, fc, dcl * 128:(dcl + 1) * 128], UT[:, fc, t0:t0 + tn]) for fc in range(8)])
                        xadd(dc, t0, tn, ps)
            AR.release("UT", "lng", "lnb", "BSg", "BS64g", "WsT", "W64")

        EVEN_IMPL = {}

        def final():
            SQ = [AR.alloc("SQ%d" % i, [128, 8, 512], BF16) for i in range(2)]
            RS = [AR.alloc("RS%d" % i, [128, 512], F32) for i in range(2)]
            YF = [AR.alloc("YF%d" % i, [128, 8, 512], F32) for i in range(2)]
            YT = [AR.alloc("YT%d" % i, [128, 1024], F32) for i in range(6)]
            k = 0
            for ti, (t0, tn) in enumerate(TT):
                sq = SQ[ti % 2]
                rs = RS[ti % 2]
                yf = YF[ti % 2]
                act(sq[:, :, 0:tn], X[:, :, t0:t0 + tn], AF.Square)
                ps = nps()
                mmg(ps[:, 0:tn], [(onesB, sq[:, dc, 0:tn]) for dc in range(8)])
                act(rs[:, 0:tn], ps[:, 0:tn], AF.Sqrt, bias=EPS, scale=1.0 / 1024)
                S.op("dve", lambda e, rs=rs, tn=tn: e.reciprocal(out=rs[:, 0:tn], in_=rs[:, 0:tn]), reads=[rs[:, 0:tn]], writes=[rs[:, 0:tn]])
                for dc in range(8):
                    stt(yf[:, dc, 0:tn], X[:, dc, t0:t0 + tn], gfin[:, dc:dc + 1], rs[:, 0:tn], ALU.mult, ALU.mult)
                for c0 in range(0, tn, 128):
                    cn = min(128, tn - c0)
                    yt = YT[k % 6]
                    k += 1
                    for half in range(2):
                        ps = nps()
                        trs([(ps[0:cn, j * 128:(j + 1) * 128], yf[:, 4 * half + j, c0:c0 + cn], identF) for j in range(4)])
                        cp("act" if half == 0 else "dve", yt[0:cn, 512 * half:512 * half + 512], ps[0:cn, :])
                    if ti < 4:
                        S.dma("sp", st, D["yp"][t0 + c0:t0 + c0 + cn, :], yt[0:cn, :])
                    else:
                        S.dma("sp", st, D["ys"], yt[0:cn, :])
            AR.release("SQ0", "SQ1", "RS0", "RS1", "YF0", "YF1", *["YT%d" % i for i in range(6)])

        def dump_x():
            S.dma("sp", st, D["dbg_x"], X)

        ctx = dict(nc=nc, S=S, AR=AR, PS=PS, nps=nps, D=D, X=X, act=act, tt=tt, ts=ts, stt=stt, cp=cp, memset=memset,
                   mmg=mmg, mms=mms, trs=trs, rmsnorm=rmsnorm, xadd=xadd, ring_next=ring_next, wload=wload, ld=ld, st=st,
                   identF=identF, identB=identB, causal=causal, onesF=onesF, onesB=onesB, selH=selH, bmask=bmask, bones=bones,
                   invc=invc, gmix=gmix, pscale=pscale, bgate=bgate, nbf=nbf, NCD=NCD, lnks=LNKS)

        if DEBUG_STOP is None:
            for l in range(4):
                if l % 2 == 0:
                    even_layer(ctx, l // 2, l)
                else:
                    odd(l // 2, l)
                ffn(l)
            final()
        else:
            for kind, l in DEBUG_STOP:
                if kind == "even":
                    even_layer(ctx, l // 2, l)
                elif kind == "odd":
                    odd(l // 2, l)
                elif kind == "ffn":
                    ffn(l)
            dump_x()
        S.wait_all_dma("sp", S.slots)
        S.finish()
        import os as _os2
        if _os2.environ.get("SCHED_DUMP"):
            for en in S.ENG:
                print("==== engine", en)
                for rec in [r for r in S.oplog if r[0] == en][-int(_os2.environ["SCHED_DUMP"]):]:
                    print(rec)
        if getattr(S, "ps_multi", None):
            print("[kernel] WARNING multi-engine PSUM readers:", len(S.ps_multi), S.ps_multi[:6], flush=True)
        print("[kernel] arena peak bytes/partition:", AR.peak, " instr:", dict(S.cnt), " waits:", S.n_wait, flush=True)
    return nc


def _shard_inputs(inp):
    A = lambda a: np.ascontiguousarray(a, dtype=np.float32)
    wnames = ["g_mix", "w_in_even", "b_gates", "w_pool", "pool_scale", "g_head", "w_out_even", "w_in_odd", "ln_v_g", "ln_v_b",
              "w_spatial", "b_spatial", "w_out_odd", "g_ffn", "w_ffn_gate", "w_ffn_up", "conv_w", "conv_b", "w_ffn_down", "g_final"]
    W = {n: A(inp[n]) for n in wnames}
    maps = []
    for c in range(8):
        s = slice(16 * c, 16 * c + 16)
        m = dict(W)
        m["xp"] = A(inp["x_prompt"][c])
        m["xs"] = A(np.asarray(inp["x_sample"])[s].reshape(64, 1024))
        m["spool"] = A(np.asarray(inp["state_pool"])[:, s])
        m["sc"] = A(np.asarray(inp["state_mlstm_c"])[:, s])
        m["sn"] = A(np.asarray(inp["state_mlstm_n"])[:, s])
        m["sm"] = A(np.asarray(inp["state_mlstm_m"])[:, s])
        m["sconv"] = A(np.asarray(inp["state_ffn_conv"])[:, s])
        maps.append(m)
    return maps


def kernel(**inputs):
    inputs = {k: np.asarray(v) for k, v in inputs.items()}
    nc = build_program()
    maps = _shard_inputs(inputs)
    res = run_bass_kernel_spmd(nc, maps, core_ids=list(range(8)))
    R = res.results
    f = np.float32
    y_p = np.zeros((8, 2048, 1024), f); y_s = np.zeros((128, 4, 1024), f)
    pool_p = np.zeros((2, 8, 15, 512), f); pool_s = np.zeros((2, 128, 15, 512), f)
    c_p = np.zeros((2, 8, 4, 128, 128), f); c_s = np.zeros((2, 128, 4, 128, 128), f)
    n_p = np.zeros((2, 8, 4, 128), f); n_s = np.zeros((2, 128, 4, 128), f)
    m_p = np.zeros((2, 8, 4), f); m_s = np.zeros((2, 128, 4), f)
    cv_p = np.zeros((4, 8, 2, 2816), f); cv_s = np.zeros((4, 128, 2, 2816), f)
    v_s = np.zeros((2, 128, 4, 1024), f)
    for c in range(8):
        r = R[c]
        s = slice(16 * c, 16 * c + 16)
        y_p[c] = r["yp"]; y_s[s] = r["ys"].reshape(16, 4, 1024)
        pool_p[:, c] = r["opool_p"]; pool_s[:, s] = r["opool_s"]
        c_p[:, c] = r["oc_p"]; c_s[:, s] = r["oc_s"]
        n_p[:, c] = r["on_p"]; n_s[:, s] = r["on_s"]
        m_p[:, c] = r["om_p"]; m_s[:, s] = r["om_s"]
        cv_p[:, c] = r["oconv_p"]; cv_s[:, s] = r["oconv_s"]
        v_s[:, s] = r["ov_s"].reshape(2, 16, 4, 1024)
    return (y_p, y_s, pool_p, pool_s, c_p, c_s, n_p, n_s, m_p, m_s, cv_p, cv_s, v_s)
```

```python
import numpy as np
import concourse.bass as bass
import concourse.mybir as mybir

F32 = mybir.dt.float32
BF16 = mybir.dt.bfloat16
I32 = mybir.dt.int32
AF = mybir.ActivationFunctionType
ALU = mybir.AluOpType
AX = mybir.AxisListType

_DTB = {F32: 4, BF16: 2, I32: 4}


def _rect(ap):
    t = ap.tensor
    name = t.name
    pat = list(ap.ap)
    esz = _DTB.get(ap.dtype, 4)
    space = str(ap.space) if hasattr(ap, "space") else ""
    if "DRAM" in space.upper() or "HBM" in space.upper() or type(t).__name__.startswith("DRam"):
        lo = ap.offset
        hi = lo
        for st, cn in pat:
            hi += abs(st) * (cn - 1)
        return (name, 0, 1, lo * esz, (hi + 1) * esz)
    if name.startswith("ps") and name[2:].isdigit():
        return (name, 0, 128, 0, 2048)
    tsz = _DTB.get(t.dtype, 4)
    rowsz = 1
    for s in list(t.shape)[1:]:
        rowsz *= s
    rowb = rowsz * tsz
    offb = ap.offset * esz
    pstep, pcnt = pat[0]
    p0 = offb // rowb
    f0 = offb % rowb
    ext = 0
    for st, cn in pat[1:]:
        ext += abs(st) * (cn - 1)
    f1 = f0 + (ext + 1) * esz
    if pstep == 0:
        p1 = p0 + 1
    else:
        p1 = p0 + pcnt
    return (name, p0, p1, f0, f1)


class Slot:
    def __init__(self, sem, idx):
        self.sem = sem
        self.idx = idx
        self.count = 0


class Sched:
    ENG = ["pe", "act", "dve", "pool", "sp"]

    def __init__(self, nc, es, same_engine_sync=True):
        self.nc = nc
        self.es = es
        self.eng = {"pe": nc.tensor, "act": nc.scalar, "dve": nc.vector, "pool": nc.gpsimd, "sp": nc.sync}
        self.sem = {e: es.enter_context(nc.semaphore("sem_" + e)) for e in self.ENG}
        self.cnt = {e: 0 for e in self.ENG}
        self.know = {e: {f: 0 for f in self.ENG} for e in self.ENG}
        self.dknow = {e: {} for e in self.ENG}
        self.clock = {}
        self.track = {}
        self.prog = {e: [] for e in self.ENG}
        self.slots = []
        self.same_engine_sync = same_engine_sync
        self.n_wait = 0

    def slot(self, name=None):
        i = len(self.slots)
        s = Slot(self.es.enter_context(self.nc.semaphore(name or ("dsem%d" % i))), i)
        self.slots.append(s)
        return s

    def _collect(self, reads, writes):
        deps = []
        for ap in reads:
            name, p0, p1, f0, f1 = _rect(ap)
            for ent in self.track.get(name, ()):
                if ent[0] < p1 and p0 < ent[1] and ent[2] < f1 and f0 < ent[3]:
                    if ent[4] is not None:
                        deps.append(ent[4])
        for ap in writes:
            name, p0, p1, f0, f1 = _rect(ap)
            for ent in self.track.get(name, ()):
                if ent[0] < p1 and p0 < ent[1] and ent[2] < f1 and f0 < ent[3]:
                    if ent[4] is not None:
                        deps.append(ent[4])
                    for e, k in ent[5].items():
                        deps.append(("c", e, k))
                    deps.extend(ent[6])
        return deps

    def _record(self, ev, reads, writes):
        for ap in writes:
            name, p0, p1, f0, f1 = _rect(ap)
            lst = self.track.setdefault(name, [])
            lst[:] = [en for en in lst if not (p0 <= en[0] and en[1] <= p1 and f0 <= en[2] and en[3] <= f1)]
            lst.append([p0, p1, f0, f1, ev, {}, []])
        for ap in reads:
            name, p0, p1, f0, f1 = _rect(ap)
            lst = self.track.setdefault(name, [])
            found = None
            for en in lst:
                if en[0] == p0 and en[1] == p1 and en[2] == f0 and en[3] == f1:
                    found = en
                    break
            if found is None:
                found = [p0, p1, f0, f1, None, {}, []]
                lst.append(found)
            if ev[0] == "c":
                if found[5].get(ev[1], 0) < ev[2]:
                    found[5][ev[1]] = ev[2]
            else:
                found[6].append(ev)

    def _waits_for(self, e, deps):
        need_c = {}
        need_d = {}
        for ev in deps:
            if ev[0] == "c":
                _, f, k = ev
                if f == e and not self.same_engine_sync:
                    continue
                if self.know[e][f] < k and need_c.get(f, 0) < k:
                    need_c[f] = k
            else:
                _, si, val = ev
                val = self.slots[si].count
                if self.dknow[e].get(si, 0) < val and need_d.get(si, 0) < val:
                    need_d[si] = val
        waits = []
        for f, k in need_c.items():
            if self.know[e][f] >= k:
                continue
            waits.append((self.sem[f], k))
            ck = self.clock[(f, k)]
            kn = self.know[e]
            for g, v in ck.items():
                if kn[g] < v:
                    kn[g] = v
        for si, val in need_d.items():
            waits.append((self.slots[si].sem, val))
            self.dknow[e][si] = val
        self.n_wait += len(waits)
        return waits

    def op(self, e, fn, reads=(), writes=()):
        deps = self._collect(reads, writes)
        waits = self._waits_for(e, deps)
        idx = self.cnt[e] + 1
        self.cnt[e] = idx
        ck = dict(self.know[e])
        ck[e] = idx
        self.clock[(e, idx)] = ck
        ev = ("c", e, idx)
        self._record(ev, reads, writes)
        sem_e = self.sem[e]
        if not hasattr(self, "_psrd"):
            self._psrd = {}
            self.ps_multi = []
        for a in writes:
            nm = a.tensor.name
            if nm.startswith("ps") and nm[2:].isdigit() and e == "pe":
                self._psrd[nm] = set()
        for a in reads:
            nm = a.tensor.name
            if nm.startswith("ps") and nm[2:].isdigit():
                st_ = self._psrd.setdefault(nm, set())
                st_.add(e)
                if len(st_) > 1:
                    import traceback as _tb2
                    fr_ = _tb2.extract_stack(limit=5)
                    self.ps_multi.append((nm, sorted(st_), " <- ".join("%s:%d" % (f.name, f.lineno) for f in fr_[:-1][::-1])))
        if not hasattr(self, "oplog"):
            self.oplog = []
        import traceback as _tb
        fr = _tb.extract_stack(limit=4)
        self.oplog.append((e, idx, [(str(getattr(s_, "name", s_)), v) for s_, v in waits], " <- ".join("%s:%d" % (f.name, f.lineno) for f in fr[:-1][::-1]), [_rect(a) for a in writes]))

        def run(engine, waits=waits, fn=fn, sem_e=sem_e):
            for s, v in waits:
                engine.wait_ge(s, v)
            inst = fn(engine)
            inst.then_inc(sem_e, 1)

        self.prog[e].append(run)
        return ev

    def auto_slot(self, q):
        key = "sw" if q == "pool" else "hw"
        if not hasattr(self, "_auto"):
            self._auto = {"hw": [self.slot("asem%d" % i) for i in range(24)], "sw": [self.slot("swsem%d" % i) for i in range(8)]}
            self._autoi = {"hw": 0, "sw": 0}
        lst = self._auto[key]
        s = lst[self._autoi[key] % len(lst)]
        self._autoi[key] += 1
        return s

    def dma(self, q, slot, out, in_, track_out=True, track_in=True, new_group=True, **kw):
        if slot is None:
            slot = self.auto_slot(q)
        reads = [in_] if track_in else []
        writes = [out] if track_out else []
        deps = self._collect(reads, writes)
        if new_group and slot.count > 0:
            deps.append(("d", slot.idx, slot.count))
        waits = self._waits_for(q, deps)
        slot.count += 16
        ev = ("d", slot.idx, slot.count)
        self._record(ev, reads, writes)
        sem = slot.sem

        def run(engine, waits=waits, sem=sem, out=out, in_=in_, kw=kw):
            for s, v in waits:
                engine.wait_ge(s, v)
            engine.dma_start(out=out, in_=in_, **kw).then_inc(sem, 16)

        self.prog[q].append(run)
        return ev

    def wait_all_dma(self, q, slots):
        waits = [(s.sem, s.count) for s in slots if s.count > 0]

        def run(engine, waits=waits):
            for s, v in waits:
                engine.wait_ge(s, v)

        self.prog[q].append(run)

    def finish(self):
        nc = self.nc
        with nc.Block() as block:
            @block.tensor
            def _(eng):
                for f in self.prog["pe"]:
                    f(eng)

            @block.scalar
            def _(eng):
                for f in self.prog["act"]:
                    f(eng)

            @block.vector
            def _(eng):
                for f in self.prog["dve"]:
                    f(eng)

            @block.gpsimd
            def _(eng):
                for f in self.prog["pool"]:
                    f(eng)

            @block.sync
            def _(eng):
                for f in self.prog["sp"]:
                    f(eng)


def even_layer(ctx, e, l):
    g_ = ctx
    S, AR, PS, nps, D, X = g_["S"], g_["AR"], g_["PS"], g_["nps"], g_["D"], g_["X"]
    act, tt, ts, stt, cp, memset = g_["act"], g_["tt"], g_["ts"], g_["stt"], g_["cp"], g_["memset"]
    mmg, mms, trs, rmsnorm, xadd = g_["mmg"], g_["mms"], g_["trs"], g_["rmsnorm"], g_["xadd"]
    ring_next, wload, ld, st = g_["ring_next"], g_["wload"], g_["ld"], g_["st"]
    identF, identB, causal, onesF, selH, bmask, bones = g_["identF"], g_["identB"], g_["causal"], g_["onesF"], g_["selH"], g_["bmask"], g_["bones"]
    invc, gmix, pscale, bgate, nbf, NCD = g_["invc"], g_["gmix"], g_["pscale"], g_["bgate"], g_["nbf"], g_["NCD"]
    Win = D["w_in_even"][e]
    Wout = D["w_out_even"][e]

    H = AR.alloc("H", [128, 8, NT_], BF16)
    rmsnorm(gmix[:, l, :], H)

    def interleave(*gens):
        gens = list(gens)
        while gens:
            for g in list(gens):
                try:
                    next(g)
                except StopIteration:
                    gens.remove(g)

    GV = {}

    def gates_gen():
        ones4 = mkap(onesF, 0, [4, [0, 512]])
        wgt = AR.alloc("wgt", [128, 8, 8], BF16)
        S.dma("pool", ld, wgt, Win[:, 2560:2568].rearrange("(c p) f -> p c f", p=128), **NCD)
        yield
        Q_B, Q_G, Q_M, Q_R, Q_E = range(5)
        TM = AR.alloc("TM", [128, 5, 17, 4], F32)
        DR = AR.alloc("DR", [128, 5, 17, 4], F32)
        BC = AR.alloc("BC", [128, 4, 48], F32)
        ALs = AR.alloc("ALs", [128, 4, 16], F32)
        rows = {n: AR.alloc("row_" + n, [4, 512], F32) for n in ("IG", "LF", "B", "G", "M")}
        carry = AR.alloc("carry", [4, 2], F32)
        BCs = AR.alloc("BCs", [4, 48], F32)
        MLo = AR.alloc("MLo", [4, 17], F32)
        reps = AR.alloc("reps", [4, 2, 64], F32)
        memset("dve", carry, 0.0)
        yield
        S.dma("sp", ld, BCs[:, 32:48], D["sm"][e].rearrange("s h -> h s"), **NCD)
        yield
        psT = PS[6]
        memset("dve", psT[:, :], 0.0)
        yield
        bI = bgate[:, e, 0:1]
        nbF = nbf[:, e:e + 1]
        for ti, (t0, tn) in enumerate(TT):
            psI = nps()
            mmg(psI[0:4, 0:tn], [(wgt[:, dc, 0:4], H[:, dc, t0:t0 + tn]) for dc in range(8)])
            yield
            psF = nps()
            mmg(psF[0:4, 0:tn], [(wgt[:, dc, 4:8], H[:, dc, t0:t0 + tn]) for dc in range(8)])
            yield
            IG, LF, Br, Gr, Mr = (rows[n] for n in ("IG", "LF", "B", "G", "M"))
            act(IG[:, 0:tn], psI[0:4, 0:tn], AF.Identity, bias=bI, scale=1.0)
            yield
            act(LF[:, 0:tn], psF[0:4, 0:tn], AF.Exp, bias=nbF, scale=-1.0)
            yield
            act(LF[:, 0:tn], LF[:, 0:tn], AF.Ln, bias=1.0, scale=1.0)
            yield
            ts("dve", LF[:, 0:tn], LF[:, 0:tn], -1.0, ALU.mult)
            yield
            if ti < 4:
                S.op("dve", lambda en, Br=Br, LF=LF: en.tensor_tensor_scan(out=Br, data0=ones4, data1=LF, initial=carry[:, 0:1], op0=ALU.mult, op1=ALU.add),
                     reads=[ones4, LF, carry[:, 0:1]], writes=[Br])
                yield
                tt("dve", Gr, IG, Br, ALU.subtract)
                yield
                S.op("dve", lambda en, Mr=Mr, Gr=Gr: en.tensor_tensor_scan(out=Mr, data0=Gr, data1=Gr, initial=carry[:, 1:2], op0=ALU.max, op1=ALU.max),
                     reads=[Gr, carry[:, 1:2]], writes=[Mr])
                yield
                cp("dve", carry[:, 0:1], Br[:, 511:512])
                yield
                cp("dve", carry[:, 1:2], Mr[:, 511:512])
                yield
                cp("dve", BCs[:, 4 * ti:4 * ti + 4], mkap(Mr, 127, [4, [128, 4]]))
                yield
                if ti == 3:
                    tt("dve", MLo[:, 0:1], Br[:, 511:512], Mr[:, 511:512], ALU.add)
                    yield
                items = []
                for j in range(4):
                    c = 4 * ti + j
                    for q, R_ in ((Q_B, Br), (Q_G, Gr), (Q_M, Mr)):
                        col = (q * 17 + c) * 4
                        items.append((psT[:, col:col + 4], R_[:, 128 * j:128 * j + 128], identF[0:4, 0:4]))
                trs(items)
                yield
            else:
                v3 = lambda R_: R_[:, 0:64].rearrange("h (s t) -> h s t", t=4)
                B3, G3, M3, L3 = v3(Br), v3(Gr), v3(Mr), v3(LF)
                cp("dve", B3[:, :, 0], L3[:, :, 0])
                yield
                for t in range(1, 4):
                    tt("dve", B3[:, :, t], B3[:, :, t - 1], L3[:, :, t], ALU.add)
                    yield
                tt("dve", Gr[:, 0:64], IG[:, 0:64], Br[:, 0:64], ALU.subtract)
                yield
                tt("dve", M3[:, :, 0], BCs[:, 32:48], G3[:, :, 0], ALU.max)
                yield
                for t in range(1, 4):
                    tt("dve", M3[:, :, t], M3[:, :, t - 1], G3[:, :, t], ALU.max)
                    yield
                cp("dve", BCs[:, 16:32], M3[:, :, 3])
                yield
                tt("dve", MLo[:, 1:17], B3[:, :, 3], M3[:, :, 3], ALU.add)
                yield
                cp("dve", reps[:, 0, :].rearrange("h (s t) -> h s t", t=4), mkap(BCs, 32, [4, [1, 16], [0, 4]]))
                yield
                cp("dve", reps[:, 1, :].rearrange("h (s t) -> h s t", t=4), mkap(BCs, 16, [4, [1, 16], [0, 4]]))
                yield
                items = []
                for q, R_ in ((Q_B, Br[:, 0:64]), (Q_G, Gr[:, 0:64]), (Q_M, Mr[:, 0:64]), (Q_R, reps[:, 0, :]), (Q_E, reps[:, 1, :])):
                    col = (q * 17 + 16) * 4
                    items.append((psT[0:64, col:col + 4], R_, identF[0:4, 0:4]))
                trs(items)
                yield
        cp("dve", TM[:, :, :, :].rearrange("p q c h -> p (q c h)"), psT[:, 0:340])
        yield
        S.dma("sp", st, D["om_p"][e:e + 1, :].rearrange("a h -> h a"), MLo[:, 0:1], **NCD)
        yield
        S.dma("sp", st, D["om_s"][e].rearrange("s h -> h s"), MLo[:, 1:17], **NCD)
        yield
        psB = nps()
        mms([(psB[:, 48 * h:48 * h + 48], selH[:, h, :], BCs[:, :]) for h in range(4)])
        yield
        cp("dve", BC[:, :, :].rearrange("p h c -> p (h c)"), psB[:, 0:192])
        yield
        memset("dve", TM[:, Q_R, 0, :], 0.0)
        yield
        cp("dve", TM[:, Q_R, 1:16, :], mkap(BC, 0, [128, [1, 15], [48, 4]]))
        yield
        cp("dve", TM[:, Q_E, 0:16, :], mkap(BC, 0, [128, [1, 16], [48, 4]]))
        yield
        fl = lambda v: v.rearrange("p c h -> p (c h)")
        tmp = AR.alloc("tmpd", [128, 68], F32)
        tt("dve", tmp, fl(TM[:, Q_G]), fl(TM[:, Q_R]), ALU.subtract)
        yield
        act(fl(DR[:, 0]), tmp, AF.Exp, bias=LNKS_AP(g_), scale=1.0)
        yield
        tmp2 = AR.alloc("tmpd2", [128, 68], F32)
        tt("dve", tmp2, fl(TM[:, Q_R]), fl(TM[:, Q_M]), ALU.subtract)
        yield
        act(fl(DR[:, 1]), tmp2, AF.Exp)
        yield
        tmp3 = tmp
        tt("dve", tmp3, fl(TM[:, Q_G]), fl(TM[:, Q_E]), ALU.subtract)
        yield
        act(fl(DR[:, 2]), tmp3, AF.Exp, bias=LNKS_AP(g_), scale=1.0)
        yield
        tmp4 = tmp2
        tt("dve", tmp4, fl(TM[:, Q_B]), fl(TM[:, Q_M]), ALU.add)
        yield
        act(fl(DR[:, 3]), tmp4, AF.Exp, scale=-1.0)
        yield
        tmp5 = tmp
        tt("dve", tmp5, fl(TM[:, Q_R]), fl(TM[:, Q_E]), ALU.subtract)
        yield
        act(fl(DR[:, 4]), tmp5, AF.Exp)
        yield
        tmp6 = AR.alloc("tmpd6", [128, 4, 16], F32)
        tt("dve", tmp6, BC[:, :, 32:48], BC[:, :, 16:32], ALU.subtract)
        yield
        act(ALs, tmp6, AF.Exp)
        yield
        AR.release("tmpd", "tmpd2", "tmpd6", "wgt", "carry", "BCs", "MLo", "reps",
                   "row_IG", "row_LF", "row_B", "row_G", "row_M")
        GV.update(DR=DR, ALs=ALs)

    def pool_gen():
        RUN = 15 + 2048
        RT = RUN + 16 * 19
        rp = ring_next()
        wp = rp[0][:, :].rearrange("p (c f) -> p c f", c=8)
        wload(rp, wp, Win[:, 0:512].rearrange("(c p) f -> p c f", p=128))
        yield
        wpl = AR.alloc("wpl", [128, 4, 128], BF16)
        S.dma("pool", ld, wpl, D["w_pool"][e].rearrange("g c d -> c g d"))
        yield
        YA = AR.alloc("YA", [128, 4, NT_], BF16)
        PT = AR.alloc("PT", [128, RT], F32)
        SA = AR.alloc("SA", [128, RT], F32)
        SB = AR.alloc("SB", [128, RT], F32)
        Dd = AR.alloc("Dd", [128, NT_], BF16)
        memset("pool", SA, 0.0)
        yield
        memset("pool", SB, 0.0)
        yield
        SPL = [AR.alloc("SPL%d" % i, [120, 512], F32) for i in range(2)]
        for hf in range(2):
            S.dma("sp", ld, SPL[hf], D["spool"][e, 8 * hf:8 * hf + 8].rearrange("s r f -> (s r) f"))
            yield
        sview = lambda buf: mkap(buf, RUN, [128, [19, 16], [1, 19]])
        for g in range(4):
            w = 2 ** (g + 1)
            memset("pool", PT[:, 0:15], 0.0)
            yield
            for hf in range(2):
                ps = nps()
                trs([(ps[:, 0:120], SPL[hf][:, g * 128:(g + 1) * 128], identF[0:120, 0:120])])
                yield
                cp("dve", sview(PT)[:, 8 * hf:8 * hf + 8, 0:15], ps[:, 0:120].rearrange("p (s r) -> p s r", r=15))
                yield
            for ti, (t0, tn) in enumerate(TT):
                ps = nps()
                mmg(ps[:, 0:tn], [(wp[:, dc, g * 128:(g + 1) * 128], H[:, dc, t0:t0 + tn]) for dc in range(8)])
                yield
                if ti < 4:
                    act(PT[:, 15 + t0:15 + t0 + 512], ps[:, 0:512], AF.Copy)
                    yield
                else:
                    act(sview(PT)[:, :, 15:19], ps[:, 0:64].rearrange("p (s t) -> p s t", t=4), AF.Copy)
                    yield
            src = PT
            bufs = [SA, SB]
            for k in range(g + 1):
                sh = 2 ** k
                dst = bufs[k % 2]
                tt("dve", dst[:, sh:RT], src[:, sh:RT], src[:, 0:RT - sh], ALU.add)
                yield
                src = dst
            stt(Dd[:, 0:2048], src[:, 15:RUN], 1.0 / w, PT[:, 15:RUN], ALU.mult, ALU.subtract)
            yield
            fx = AR.alloc("fx", [128, 16], F32)
            tt("dve", fx, src[:, 15:31], invc[:, g, :], ALU.mult)
            yield
            tt("dve", Dd[:, 0:16], fx, PT[:, 15:31], ALU.subtract)
            yield
            AR.release("fx")
            stt(Dd[:, 2048:2112].rearrange("p (s t) -> p s t", t=4), sview(src)[:, :, 15:19], 1.0 / w, sview(PT)[:, :, 15:19], ALU.mult, ALU.subtract)
            yield
            for ti, (t0, tn) in enumerate(TT):
                ps = nps()
                mmg(ps[:, 0:tn], [(wpl[:, g, :], Dd[:, t0:t0 + tn])])
                yield
                act(YA[:, g, t0:t0 + tn], ps[:, 0:tn], AF.Identity, bias=0.0, scale=pscale[:, e, g:g + 1])
                yield
        PO = AR.alloc("PO", [64, 512], F32)
        ps = nps()
        mmg(ps[0:16, :], [(H[:, dc, 2032:2048], wp[:, dc, :]) for dc in range(8)])
        yield
        cp("dve", PO[0:16, :], ps[0:16, :])
        yield
        S.dma("sp", st, D["opool_p"][e], PO[1:16, :])
        yield
        PO2 = AR.alloc("PO2", [64, 512], F32)
        ps = nps()
        mmg(ps[0:64, :], [(H[:, dc, 2048:2112], wp[:, dc, :]) for dc in range(8)])
        yield
        cp("dve", PO2[0:64, :], ps[0:64, :])
        yield
        for s_ in range(16):
            S.dma("sp", st, D["opool_s"][e, s_, 11:15, :], PO2[4 * s_:4 * s_ + 4, :])
            yield
        S.dma("sp", st, D["opool_s"][e, :, 0:11, :], D["spool"][e, :, 4:15, :], track_in=False)
        yield
        ra = ring_next()
        woa = ra[0][:, :].rearrange("p (c f) -> p c f", c=4)
        wload(ra, woa, Wout[0:512, :].rearrange("(c p) f -> p c f", p=128))
        yield
        for ti, (t0, tn) in enumerate(TT):
            for dc in range(8):
                ps = nps()
                mmg(ps[:, 0:tn], [(woa[:, g, dc * 128:(dc + 1) * 128], YA[:, g, t0:t0 + tn]) for g in range(4)])
                yield
                xadd(dc, t0, tn, ps)
                yield
        AR.release("YA", "PT", "SA", "SB", "Dd", "SPL0", "SPL1", "PO", "PO2", "wpl")


    WH = {}

    def load_head_w(h):
        rh = ring_next()
        wh = rh[0][:, :].rearrange("p (c j f) -> p c j f", c=8, j=4)
        for j, cbase in enumerate((512, 1024, 1536, 2048)):
            wload(rh, wh[:, :, j, :], Win[:, cbase + 128 * h:cbase + 128 * h + 128].rearrange("(c p) f -> p c f", p=128), new_group=(j == 0))
        WH[h] = wh

    def load_wob():
        rb = ring_next()
        wob = rb[0][:, :].rearrange("p (c f) -> p c f", c=4)
        wload(rb, wob, Wout[512:1024, :].rearrange("(c p) f -> p c f", p=128))
        WH["wob"] = wob

    load_head_w(0)
    interleave(gates_gen(), pool_gen())
    DR, ALs = GV["DR"], GV["ALs"]
    EG, SC, WS, EMT, AL = DR[:, 0], DR[:, 1], DR[:, 2], DR[:, 3], DR[:, 4]

    YB = AR.alloc("YB", [128, 4, NT_], BF16)
    PPb_ = [AR.alloc("PPb%d" % i, [128, 17, 128], BF16) for i in range(2)]
    PPd_ = [AR.alloc("PPd%d" % i, [128, 17], F32) for i in range(2)]
    SO_ = [AR.alloc("SO%d" % i, [128, 17, 128], BF16) for i in range(2)]
    SSQ_ = [AR.alloc("SSQ%d" % i, [128, 17], F32) for i in range(2)]
    junk = AR.alloc("junk", [128, 128], BF16)
    QT = [AR.alloc("QT%d" % i, [128, 128], BF16) for i in range(3)]
    KT = [AR.alloc("KT%d" % i, [128, 128], BF16) for i in range(2)]
    KW = [AR.alloc("KW%d" % i, [128, 128], BF16) for i in range(2)]
    Vc = [AR.alloc("Vc%d" % i, [128, 129], BF16) for i in range(2)]
    Sp = [AR.alloc("Sp%d" % i, [128, 128], BF16) for i in range(2)]
    OF = [AR.alloc("OF%d" % i, [128, 128], F32) for i in range(2)]
    Cst = AR.alloc("Cst", [128, 129], F32)
    Cb = AR.alloc("Cb", [128, 129], BF16)
    C0 = AR.alloc("C0", [128, 16, 129], F32)
    C0b = AR.alloc("C0b", [128, 16, 129], BF16)
    Vblk = AR.alloc("Vblk", [64, 16, 129], BF16)
    QZ = AR.alloc("QZ", [128, 16, 64], BF16)
    GH = AR.alloc("GH", [128, 512], F32)
    NT0 = AR.alloc("NT0", [128, 64], F32)
    NTO = AR.alloc("NTO", [128, 4, 16], F32)
    NPO = AR.alloc("NPO", [128, 4], F32)
    post = AR.alloc("post", [128, 6, 17], F32)
    S.dma("sp", ld, GH, D["g_head"][e:e + 1, :].to_broadcast([128, 512]))
    for i in range(2):
        memset("pool", Vc[i][:, 128:129], 1.0)
    for i in range(2):
        memset("dve", PPb_[i], 0.0)
        memset("dve", PPd_[i], 0.0)
        memset("dve", SO_[i], 0.0)
        memset("dve", SSQ_[i], 0.0)
    memset("dve", QZ, 0.0)
    nrow = AR.alloc("nrow", [64, 128], F32)
    S.dma("sp", ld, nrow, D["sn"][e].rearrange("s h k -> (s h) k"))
    ps = nps()
    trs([(ps[:, 0:64], nrow, identF[0:64, 0:64])])
    cp("dve", NT0, ps[:, 0:64])
    AR.release("nrow")
    k = 0

    def head_gen(h):
        PPb, PPd, SO, SSQ = PPb_[h % 2], PPd_[h % 2], SO_[h % 2], SSQ_[h % 2]
        wh = WH[h]
        memset("pool", Cst, 0.0)
        memset("pool", Cb, 0.0)
        def stA(c):
            t0, tn = CH[c]
            qt, kt = QT[c % 3], KT[c % 2]
            psQ = nps()
            mmg(psQ[:, 0:tn], [(wh[:, dc, 0, :], H[:, dc, t0:t0 + tn]) for dc in range(8)])
            psK = nps()
            mmg(psK[:, 0:tn], [(wh[:, dc, 1, :], H[:, dc, t0:t0 + tn]) for dc in range(8)])
            act(qt[:, 0:tn], psQ[:, 0:tn], AF.Copy)
            cp("dve", kt[:, 0:tn], psK[:, 0:tn])

        def stB(c):
            t0, tn = CH[c]
            qt, kt, kw, vc, sp = QT[c % 3], KT[c % 2], KW[c % 2], Vc[c % 2], Sp[c % 2]
            psS = nps()
            mms([(psS[0:tn, 0:tn], kt[:, 0:tn], qt[:, 0:tn])])
            ps3 = nps()
            mmg(ps3[0:tn, 0:384], [(H[:, dc, t0:t0 + tn], wh[:, dc, 1:4, :].rearrange("p j f -> p (j f)")) for dc in range(8)])
            mask = causal if c < 16 else bmask[:, :, :].rearrange("p s r -> p (s r)")
            stt(sp[0:tn, 0:tn], psS[0:tn, 0:tn], EG[0:tn, c, h:h + 1], mask[0:tn, 0:tn], ALU.mult, ALU.mult)
            ts("dve", kw[0:tn, :], ps3[0:tn, 0:128], WS[0:tn, c, h:h + 1], ALU.mult)
            cp("dve", vc[0:tn, 0:128], ps3[0:tn, 128:256])
            of = OF[c % 2]
            cp("dve", of[0:tn, :], ps3[0:tn, 256:384])
            act(SO[0:tn, c, :], of[0:tn, :], AF.Sigmoid)

        def stC(c):
            t0, tn = CH[c]
            qt, kw, vc, sp = QT[c % 3], KW[c % 2], Vc[c % 2], Sp[c % 2]
            psP = nps()
            if c < 16:
                psU = nps()
                mms([(psU[:, 0:129], kw[:, :], vc[:, :])])
                mmg(psP[:, 0:129], [(qt[:, :], Cb[:, :]), (sp[:, :], vc[:, :])])
            else:
                cp("pool", mkap(QZ, 0, [128, [68, 16], [1, 4]]), qt[:, 0:64].rearrange("p (s t) -> p s t", t=4))
                mmg(psP[0:64, 0:129], [(QZ[:, s_, :], C0b[:, s_, :]) for s_ in range(16)] + [(sp[0:64, 0:64], vc[0:64, :])])
            act(PPb[0:tn, c, :], psP[0:tn, 0:128], AF.Copy)
            act(PPd[0:tn, c:c + 1], psP[0:tn, 128:129], AF.Copy)
            act(junk[0:tn, :], psP[0:tn, 0:128], AF.Square, accum=SSQ[0:tn, c:c + 1])
            if c < 16:
                stt(Cst, Cst, AL[:, c, h:h + 1], psU[:, 0:129], ALU.mult, ALU.add)
                cp("act", Cb, Cst)
            else:
                tt("pool", Vblk, mkap(vc, 0, [64, [0, 16], [1, 129]]), mkap(bones, 0, [64, [1, 16], [0, 129]]), ALU.mult)
                for s0 in range(0, 16, 3):
                    ns = min(3, 16 - s0)
                    psU2 = nps()
                    mms([(psU2[:, 0:ns * 129], kw[0:64, :], Vblk[:, s0:s0 + ns, :].rearrange("p s f -> p (s f)"))])
                    for j in range(ns):
                        s_ = s0 + j
                        stt(C0[:, s_, :], C0[:, s_, :], ALs[:, h, s_:s_ + 1], psU2[:, j * 129:(j + 1) * 129], ALU.mult, ALU.add)
                for s4 in range(0, 16, 4):
                    S.dma("sp", st, D["oc_s"][e, s4:s4 + 4, h, :, :].rearrange("s k v -> k s v"), C0[:, s4:s4 + 4, 0:128])
                cp("dve", NTO[:, h, :], C0[:, :, 128])
            if c == 15:
                S.dma("sp", st, D["oc_p"][e, h], Cst[:, 0:128])
                cp("dve", NPO[:, h:h + 1], Cst[:, 128:129])

        stA(0)
        yield
        for i_ in range(17):
            if i_ == 1:
                if h + 1 < 4:
                    load_head_w(h + 1)
                else:
                    load_wob()
            if i_ == 10:
                for s4 in range(0, 16, 4):
                    S.dma("sp", ld, C0[:, s4:s4 + 4, 0:128], D["sc"][e, s4:s4 + 4, h, :, :].rearrange("s k v -> k s v"))
                cp("dve", C0[:, :, 128], mkap(NT0, h, [128, [4, 16]]))
                cp("pool", C0b, C0)
            if i_ + 1 < 17:
                stA(i_ + 1)
                yield
            stB(i_)
            yield
            if i_ >= 1:
                stC(i_ - 1)
                yield
        stC(16)
        yield

    def post_gen(h):
        PPb, PPd, SO, SSQ = PPb_[h % 2], PPd_[h % 2], SO_[h % 2], SSQ_[h % 2]
        den = PPd
        t1, t2, rr, t4, rsd, tot = (post[:, i, :] for i in range(6))
        tt("dve", t1, den, SC[:, :, h], ALU.mult)
        yield
        act(t1, t1, AF.Abs)
        yield
        tt("dve", t2, t1, EMT[:, :, h], ALU.max)
        yield
        S.op("dve", lambda en, t2=t2: en.reciprocal(out=t2, in_=t2), reads=[t2], writes=[t2])
        yield
        tt("dve", rr, t2, SC[:, :, h], ALU.mult)
        yield
        tt("dve", t4, rr, rr, ALU.mult)
        yield
        tt("dve", t4, t4, SSQ, ALU.mult)
        yield
        act(rsd, t4, AF.Sqrt, bias=EPS, scale=1.0 / 128)
        yield
        S.op("dve", lambda en, rsd=rsd: en.reciprocal(out=rsd, in_=rsd), reads=[rsd], writes=[rsd])
        yield
        tt("dve", tot, rr, rsd, ALU.mult)
        yield
        tt("pool", SO, SO, mkap(GH, 128 * h, [128, [0, 17], [1, 128]]), ALU.mult)
        yield
        for c in range(17):
            tn = CH[c][1]
            stt(SO[0:tn, c, :], PPb[0:tn, c, :], tot[0:tn, c:c + 1], SO[0:tn, c, :], ALU.mult, ALU.mult)
            yield
        for c0 in range(0, 17, 4):
            ncx = min(4, 17 - c0)
            ps = nps()
            psb = ps[:, :].bitcast(BF16)
            items = []
            for j in range(ncx):
                c = c0 + j
                tn = CH[c][1]
                items.append((psb[:, j * 128:j * 128 + tn], SO[0:tn, c, :], identB[0:tn, 0:tn]))
            trs(items)
            yield
            if c0 < 16:
                cp("act", YB[:, h, 128 * c0:128 * c0 + 512], psb[:, 0:512])
                yield
            else:
                cp("act", YB[:, h, 2048:2112], psb[:, 0:64])
                yield

    for h in range(4):
        if h > 0:
            interleave(head_gen(h), post_gen(h - 1))
        else:
            interleave(head_gen(h))
    interleave(post_gen(3))
    ps = nps()
    trs([(ps[0:64, 0:128], NTO[:, :, :].rearrange("p h s -> p (h s)"), identF)])
    nout = AR.alloc("nout", [64, 128], F32)
    cp("dve", nout, ps[0:64, 0:128])
    for h in range(4):
        S.dma("sp", st, D["on_s"][e, :, h, :], nout[16 * h:16 * h + 16, :])
    ps = nps()
    trs([(ps[0:4, 0:128], NPO, identF)])
    nout2 = AR.alloc("nout2", [4, 128], F32)
    cp("dve", nout2, ps[0:4, 0:128])
    S.dma("sp", st, D["on_p"][e], nout2)
    wob = WH["wob"]
    for ti, (t0, tn) in enumerate(TT):
        for dc in range(8):
            ps = nps()
            mmg(ps[:, 0:tn], [(wob[:, hh, dc * 128:(dc + 1) * 128], YB[:, hh, t0:t0 + tn]) for hh in range(4)])
            xadd(dc, t0, tn, ps)
    AR.release("H", "TM", "DR", "BC", "ALs", "YB", "PPb0", "PPb1", "PPd0", "PPd1", "SO0", "SO1", "SSQ0", "SSQ1", "junk", "QT0", "QT1", "QT2", "KT0", "KT1", "KW0", "KW1",
               "Vc0", "Vc1", "Sp0", "Sp1", "OF0", "OF1", "Cst", "Cb", "C0", "C0b", "Vblk", "QZ", "GH", "NT0", "NTO", "NPO",
               "post", "nout", "nout2")


def LNKS_AP(g_):
    return g_["lnks"]


from contextlib import ExitStack
from concourse.bass_utils import run_bass_kernel_spmd
import math

NP_ = 2048
NS_ = 64
NT_ = NP_ + NS_
TT = [(0, 512), (512, 512), (1024, 512), (1536, 512), (2048, 64)]
CH = [(128 * c, 128) for c in range(16)] + [(2048, 64)]
EPS = 1e-6
DFF = 2816
NFC = 22
LNKS = math.log(128.0 ** -0.5)
DEBUG_STOP = None


class Arena:
    def __init__(self, nc, es, nbytes):
        self.t = es.enter_context(nc.sbuf_tensor("arena", [128, nbytes // 2], BF16))
        self.nbytes = nbytes
        self.free = [(0, nbytes)]
        self.live = {}
        self.peak = 0

    def alloc(self, name, shape, dt):
        esz = 2 if dt == BF16 else 4
        n = 1
        for s in shape[1:]:
            n *= s
        nb = (n * esz + 63) // 64 * 64
        for i, (o, sz) in enumerate(self.free):
            if sz >= nb:
                off = o
                if sz == nb:
                    self.free.pop(i)
                else:
                    self.free[i] = (o + nb, sz - nb)
                break
        else:
            raise RuntimeError("arena OOM for %s (%d bytes); live=%s" % (name, nb, {k: v[1] for k, v in self.live.items()}))
        assert name not in self.live, name
        self.live[name] = (off, nb)
        used = self.nbytes - sum(s for _, s in self.free)
        self.peak = max(self.peak, used)
        base = self.t[0:shape[0], off // 2: off // 2 + nb // 2]
        if dt != BF16:
            base = base.bitcast(dt)
        v = base[:, 0:n]
        if len(shape) > 2:
            names = " ".join("d%d" % i for i in range(len(shape) - 1))
            kw = {"d%d" % i: shape[i + 1] for i in range(len(shape) - 2)}
            v = v.rearrange("p (%s) -> p %s" % (names, names), **kw)
        return v

    def release(self, *names):
        for name in names:
            off, nb = self.live.pop(name)
            self.free.append((off, nb))
        self.free.sort()
        m = []
        for o, s in self.free:
            if m and m[-1][0] + m[-1][1] == o:
                m[-1] = (m[-1][0], m[-1][1] + s)
            else:
                m.append((o, s))
        self.free = m


def mkap(base, off, dims):
    pst = list(base.ap)[0][0]
    return bass.AP(tensor=base.tensor, offset=base.offset + off, ap=[[pst, dims[0]]] + [list(d) for d in dims[1:]])


def build_program():
    nc = bass.Bass("TRN2", target_bir_lowering=False)
    D = {}

    def din(name, shape):
        D[name] = nc.dram_tensor(name, list(shape), F32, kind="ExternalInput").ap()

    def dout(name, shape):
        D[name] = nc.dram_tensor(name, list(shape), F32, kind="ExternalOutput").ap()

    din("xp", [2048, 1024]); din("xs", [64, 1024]); din("spool", [2, 16, 15, 512])
    din("sc", [2, 16, 4, 128, 128]); din("sn", [2, 16, 4, 128]); din("sm", [2, 16, 4]); din("sconv", [4, 16, 2, 2816])
    din("g_mix", [4, 1024]); din("w_in_even", [2, 1024, 2568]); din("b_gates", [2, 8]); din("w_pool", [2, 4, 128, 128])
    din("pool_scale", [2, 512]); din("g_head", [2, 512]); din("w_out_even", [2, 1024, 1024]); din("w_in_odd", [2, 1024, 2048])
    din("ln_v_g", [2, 1024]); din("ln_v_b", [2, 1024]); din("w_spatial", [2, 4, 128, 128]); din("b_spatial", [2, 4, 128])
    din("w_out_odd", [2, 1024, 1024]); din("g_ffn", [4, 1024]); din("w_ffn_gate", [4, 1024, 2816]); din("w_ffn_up", [4, 1024, 2816])
    din("conv_w", [4, 3, 2816]); din("conv_b", [4, 2816]); din("w_ffn_down", [4, 2816, 1024]); din("g_final", [1024])
    dout("yp", [2048, 1024]); dout("ys", [64, 1024]); dout("opool_p", [2, 15, 512]); dout("opool_s", [2, 16, 15, 512])
    dout("oc_p", [2, 4, 128, 128]); dout("oc_s", [2, 16, 4, 128, 128]); dout("on_p", [2, 4, 128]); dout("on_s", [2, 16, 4, 128])
    dout("om_p", [2, 4]); dout("om_s", [2, 16, 4]); dout("oconv_p", [4, 2, 2816]); dout("oconv_s", [4, 16, 2, 2816])
    dout("ov_s", [2, 64, 1024])
    if DEBUG_STOP is not None:
        dout("dbg_x", [128, 8, NT_])

    with ExitStack() as es:
        S = Sched(nc, es)
        AR = Arena(nc, es, 211968)
        PS = [es.enter_context(nc.psum_tensor("ps%d" % i, [128, 512], F32)) for i in range(8)]
        psrr = [0]

        NPS_BANKS = (0, 1, 2, 3, 4, 5, 7)

        def nps():
            psrr[0] = (psrr[0] + 1) % len(NPS_BANKS)
            return PS[NPS_BANKS[psrr[0]]]

        def aps(*xs):
            return [x for x in xs if x is not None and not isinstance(x, (int, float))]

        def act(out, in_, func, bias=None, scale=None, accum=None):
            kw = {}
            if bias is not None:
                kw["bias"] = bias
            if scale is not None:
                kw["scale"] = scale
            if accum is not None:
                kw["accum_out"] = accum
            S.op("act", lambda e: e.activation(out=out, in_=in_, func=func, **kw),
                 reads=aps(in_, bias, scale), writes=aps(out, accum))

        def tt(eng, out, in0, in1, op):
            S.op(eng, lambda e: e.tensor_tensor(out=out, in0=in0, in1=in1, op=op), reads=[in0, in1], writes=[out])

        def ts(eng, out, in0, s1, op0, s2=None, op1=None):
            if op1 is None:
                S.op(eng, lambda e: e.tensor_scalar(out=out, in0=in0, scalar1=s1, scalar2=None, op0=op0),
                     reads=aps(in0, s1), writes=[out])
            else:
                S.op(eng, lambda e: e.tensor_scalar(out=out, in0=in0, scalar1=s1, scalar2=s2, op0=op0, op1=op1),
                     reads=aps(in0, s1, s2), writes=[out])

        def stt(out, in0, sc, in1, op0, op1):
            S.op("dve", lambda e: e.scalar_tensor_tensor(out=out, in0=in0, scalar=sc, in1=in1, op0=op0, op1=op1),
                 reads=aps(in0, sc, in1), writes=[out])

        def cp(eng, out, in_):
            if eng == "act":
                act(out, in_, AF.Copy)
            else:
                S.op(eng, lambda e: e.tensor_copy(out=out, in_=in_), reads=[in_], writes=[out])

        def memset(eng, out, val):
            S.op(eng, lambda e: e.memset(out, val), writes=[out])

        def mmg(out, pairs):
            n = len(pairs)

            def fn(e):
                inst = None
                for i, (l, r) in enumerate(pairs):
                    inst = e.matmul(out, lhsT=l, rhs=r, start=(i == 0), stop=(i == n - 1))
                return inst
            rd = []
            for l, r in pairs:
                rd.append(l)
                rd.append(r)
            S.op("pe", fn, reads=rd, writes=[out])

        def mms(items):
            def fn(e):
                inst = None
                for (o, l, r) in items:
                    inst = e.matmul(o, lhsT=l, rhs=r, start=True, stop=True)
                return inst
            S.op("pe", fn, reads=[x for it in items for x in it[1:]], writes=[it[0] for it in items])

        def trs(items):
            def fn(e):
                inst = None
                for (o, i_, idn) in items:
                    inst = e.transpose(o, i_, idn)
                return inst
            S.op("pe", fn, reads=[x for it in items for x in it[1:]], writes=[it[0] for it in items])

        X = AR.alloc("X", [128, 8, NT_], F32)
        identF = AR.alloc("identF", [128, 128], F32)
        identB = AR.alloc("identB", [128, 128], BF16)
        causal = AR.alloc("causal", [128, 128], F32)
        onesF = AR.alloc("onesF", [128, 128], F32)
        onesB = AR.alloc("onesB", [128, 128], BF16)
        selH = AR.alloc("selH", [4, 4, 128], F32)
        bmask = AR.alloc("bmask", [64, 16, 4], F32)
        bones = AR.alloc("bones", [64, 16], F32)
        repS = AR.alloc("repS", [4, 16, 4], F32)
        invc = AR.alloc("invc", [128, 4, 16], F32)
        gmix = AR.alloc("gmix", [128, 4, 8], F32)
        gffn = AR.alloc("gffn", [128, 4, 8], F32)
        gfin = AR.alloc("gfin", [128, 8], F32)
        convw = AR.alloc("convw", [128, 4, 3, NFC], F32)
        convb = AR.alloc("convb", [128, 4, NFC], F32)
        pscale = AR.alloc("pscale", [128, 2, 4], F32)
        bgate = AR.alloc("bgate", [4, 2, 2], F32)
        nbf = AR.alloc("nbf", [4, 2], F32)
        RING = [AR.alloc("ring%d" % i, [128, 4096], BF16) for i in range(4)]
        RSLOT = [S.slot("wsem%d" % i) for i in range(4)]
        ringi = [0]
        ld = None
        st = None

        def ring_next():
            i = ringi[0] % 4
            ringi[0] += 1
            return RING[i], RSLOT[i]

        def wload(slotpair, dst, src, new_group=True):
            S.dma("pool", slotpair[1], dst, src, new_group=new_group)

        memset("pool", onesF, 1.0)
        S.op("pool", lambda e: e.affine_select(out=causal, in_=onesF, pattern=[[1, 128]], compare_op=ALU.is_ge, fill=0.0, base=0, channel_multiplier=-1), reads=[onesF], writes=[causal])
        S.op("pool", lambda e: e.affine_select(out=identF, in_=onesF, pattern=[[1, 128]], compare_op=ALU.is_equal, fill=0.0, base=0, channel_multiplier=-1), reads=[onesF], writes=[identF])
        cp("pool", identB, identF)
        cp("pool", onesB, onesF)
        for h in range(4):
            S.op("pool", lambda e, h=h: e.affine_select(out=selH[:, h, :], in_=onesF[0:4, :], pattern=[[0, 128]], compare_op=ALU.is_equal, fill=0.0, base=-h, channel_multiplier=1), reads=[onesF[0:4, :]], writes=[selH[:, h, :]])
        S.op("pool", lambda e: e.affine_select(out=bones, in_=onesF[0:64, 0:16], pattern=[[-4, 16]], compare_op=ALU.is_ge, fill=0.0, base=0, channel_multiplier=1), reads=[onesF[0:64, 0:16]], writes=[bones])
        S.op("pool", lambda e: e.affine_select(out=bones, in_=bones, pattern=[[4, 16]], compare_op=ALU.is_ge, fill=0.0, base=3, channel_multiplier=-1), reads=[bones], writes=[bones])
        tt("pool", bmask, mkap(bones, 0, [64, [1, 16], [0, 4]]), causal[0:64, 0:64].rearrange("p (s r) -> p s r", r=4), ALU.mult)
        S.op("pool", lambda e: e.affine_select(out=repS, in_=onesF[0:4, 0:64].rearrange("p (s r) -> p s r", r=4), pattern=[[0, 16], [1, 4]], compare_op=ALU.is_equal, fill=0.0, base=0, channel_multiplier=-1), reads=[onesF[0:4, 0:64]], writes=[repS])
        itmp = AR.alloc("itmp", [128, 16], I32)
        S.op("pool", lambda e: e.iota(itmp, pattern=[[1, 16]], base=1, channel_multiplier=0), writes=[itmp])
        cp("dve", invc[:, 0, :], itmp)
        for g in range(1, 4):
            ts("dve", invc[:, g, :], invc[:, 0, :], float(2 ** (g + 1)), ALU.min)
        ts("dve", invc[:, 0, :], invc[:, 0, :], 2.0, ALU.min)
        S.op("dve", lambda e: e.reciprocal(out=invc, in_=invc), reads=[invc], writes=[invc])
        AR.release("itmp")

        NCD = dict(allow_slow_non_contiguous=True)
        S.dma("act", ld, gmix, D["g_mix"].rearrange("l (c p) -> p l c", p=128), **NCD)
        S.dma("act", ld, gffn, D["g_ffn"].rearrange("l (c p) -> p l c", p=128), **NCD)
        S.dma("act", ld, gfin, D["g_final"].rearrange("(c p) -> p c", p=128), **NCD)
        S.dma("act", ld, pscale, D["pool_scale"].rearrange("e (g p) -> p e g", p=128), **NCD)
        S.dma("act", ld, bgate, D["b_gates"].rearrange("e (a h) -> h e a", a=2), **NCD)
        ts("dve", nbf, bgate[:, :, 1], -1.0, ALU.mult)
        for l in range(4):
            for j in range(3):
                S.dma("act", ld, convw[:, l, j, :], D["conv_w"][l, j, :].rearrange("(c p) -> p c", p=128), **NCD)
            S.dma("act", ld, convb[:, l, :], D["conv_b"][l, :].rearrange("(c p) -> p c", p=128), **NCD)

        XT = [AR.alloc("XT%d" % i, [128, 1024], F32) for i in range(6)]
        for c, (t0, tn) in enumerate(CH):
            xt = XT[c % 6]
            src = D["xp"][t0:t0 + tn, :] if c < 16 else D["xs"]
            S.dma("sp", ld, xt[0:tn, :], src)
            for half in range(2):
                ps = nps()
                trs([(ps[:, j * 128:j * 128 + tn], xt[0:tn, (4 * half + j) * 128:(4 * half + j + 1) * 128], identF[0:tn, 0:tn]) for j in range(4)])
                cp("act" if half == 0 else "dve", X[:, 4 * half:4 * half + 4, t0:t0 + tn],
                   ps[:, :].rearrange("p (j t) -> p j t", t=128)[:, :, 0:tn])
        AR.release(*["XT%d" % i for i in range(6)])

        def rmsnorm(gcol, H):
            SQ = [AR.alloc("SQ%d" % i, [128, 8, 512], BF16) for i in range(2)]
            RS = [AR.alloc("RS%d" % i, [128, 512], F32) for i in range(2)]
            for ti, (t0, tn) in enumerate(TT):
                sq = SQ[ti % 2]
                rs = RS[ti % 2]
                act(sq[:, :, 0:tn], X[:, :, t0:t0 + tn], AF.Square)
                ps = nps()
                mmg(ps[:, 0:tn], [(onesB, sq[:, dc, 0:tn]) for dc in range(8)])
                act(rs[:, 0:tn], ps[:, 0:tn], AF.Sqrt, bias=EPS, scale=1.0 / 1024)
                S.op("dve", lambda e, rs=rs, tn=tn: e.reciprocal(out=rs[:, 0:tn], in_=rs[:, 0:tn]), reads=[rs[:, 0:tn]], writes=[rs[:, 0:tn]])
                for dc in range(8):
                    stt(H[:, dc, t0:t0 + tn], X[:, dc, t0:t0 + tn], gcol[:, dc:dc + 1], rs[:, 0:tn], ALU.mult, ALU.mult)
            AR.release("SQ0", "SQ1", "RS0", "RS1")

        def xadd(dc, t0, tn, ps):
            tt("dve", X[:, dc, t0:t0 + tn], X[:, dc, t0:t0 + tn], ps[:, 0:tn], ALU.add)

        def ffn(l):
            H = AR.alloc("H", [128, 8, NT_], BF16)
            rmsnorm(gffn[:, l, :], H)
            G = AR.alloc("G", [128, 4, NT_], BF16)
            A = [AR.alloc("A%d" % i, [128, 514], F32) for i in range(2)]
            T = [AR.alloc("T%d" % i, [128, 512], F32) for i in range(2)]
            GE = [AR.alloc("GE%d" % i, [128, 512], F32) for i in range(2)]
            As = AR.alloc("As", [128, 16, 6], F32)
            Ts = AR.alloc("Ts", [128, 16, 4], F32)
            CST = AR.alloc("CST", [128, NFC, 32], F32)
            CSs = AR.alloc("CSs", [128, NFC, 32], F32)
            CSp = AR.alloc("CSp", [128, 2, NFC], F32)
            CTM = AR.alloc("CTM", [32, DFF], F32)
            S.dma("sp", ld, CTM, D["sconv"][l].rearrange("s j f -> (s j) f"))
            for f0 in range(0, NFC, 4):
                nf = min(4, NFC - f0)
                ps = nps()
                trs([(ps[:, j * 32:(j + 1) * 32], CTM[0:32, (f0 + j) * 128:(f0 + j + 1) * 128], identF[0:32, 0:32]) for j in range(nf)])
                cp("dve", CST[:, f0:f0 + nf, :], ps[:, 0:nf * 32].rearrange("p (j t) -> p j t", t=32))
            AR.release("CTM")
            k = 0
            q_gelu, q_mul = [], []

            def step_pipe(s2_new):
                m_prev = q_mul.pop(0) if q_mul else None
                if q_gelu:
                    q_gelu.pop(0)()
                if m_prev is not None:
                    m_prev()
                q_gelu.append(s2_new)
            FW = {}

            def fload(kind, b):
                if b >= 6:
                    return
                c0 = 512 * b
                ncol = min(512, DFF - c0)
                nfc = ncol // 128
                r = ring_next()
                if kind == "g":
                    w_ = r[0][:, 0:8 * ncol].rearrange("p (c f) -> p c f", c=8)
                    wload(r, w_, D["w_ffn_gate"][l][:, c0:c0 + ncol].rearrange("(c p) f -> p c f", p=128))
                elif kind == "u":
                    w_ = r[0][:, 0:8 * ncol].rearrange("p (c f) -> p c f", c=8)
                    wload(r, w_, D["w_ffn_up"][l][:, c0:c0 + ncol].rearrange("(c p) f -> p c f", p=128))
                else:
                    w_ = r[0][:, 0:nfc * 1024].rearrange("p (c f) -> p c f", c=nfc)
                    wload(r, w_, D["w_ffn_down"][l][c0:c0 + ncol, :].rearrange("(c p) f -> p c f", p=128))
                FW[(kind, b)] = w_

            fload("g", 0)
            fload("u", 0)
            fload("d", 0)
            for b in range(6):
                c0 = 512 * b
                ncol = min(512, DFF - c0)
                nfc = ncol // 128
                fload("g", b + 1)
                wg, wu, wd = FW[("g", b)], FW[("u", b)], FW[("d", b)]
                for fc in range(nfc):
                    F = 4 * b + fc
                    w0 = convw[:, l, 0, F:F + 1]
                    w1 = convw[:, l, 1, F:F + 1]
                    w2 = convw[:, l, 2, F:F + 1]
                    cb = convb[:, l, F:F + 1]
                    memset("pool", A[0][:, 0:2], 0.0)
                    for ti, (t0, tn) in enumerate(TT):
                        psA = PS[(0, 1, 4)[k % 3]]
                        psU = PS[(2, 3, 6, 5)[k % 4]]
                        tb = T[k % 2]
                        ge = GE[k % 2]
                        k += 1
                        mmg(psA[:, 0:tn], [(wg[:, dc, fc * 128:(fc + 1) * 128], H[:, dc, t0:t0 + tn]) for dc in range(8)])
                        mmg(psU[:, 0:tn], [(wu[:, dc, fc * 128:(fc + 1) * 128], H[:, dc, t0:t0 + tn]) for dc in range(8)])
                        if ti < 4:
                            ab = A[ti % 2]
                            act(ab[:, 2:514], psA[:, 0:512], AF.Copy)
                            if ti < 3:
                                cp("pool", A[(ti + 1) % 2][:, 0:2], ab[:, 512:514])
                            else:
                                cp("pool", CSp[:, :, F], ab[:, 512:514])
                            ts("pool", tb, ab[:, 0:512], w0, ALU.mult, cb, ALU.add)
                            stt(tb, ab[:, 1:513], w1, tb, ALU.mult, ALU.add)
                            stt(tb, ab[:, 2:514], w2, tb, ALU.mult, ALU.add)
                            def s2(ge=ge, tb=tb, dst=G[:, fc, t0:t0 + 512], pu=psU[:, 0:512]):
                                act(ge, tb, AF.Gelu_apprx_tanh)
                                q_mul.append(lambda: tt("dve", dst, ge, pu, ALU.mult))
                            step_pipe(s2)
                        else:
                            cp("pool", As[:, :, 0:2], CST[:, F, :].rearrange("p (s j) -> p s j", j=2))
                            act(As[:, :, 2:6], psA[:, 0:64].rearrange("p (s t) -> p s t", t=4), AF.Copy)
                            ts("pool", Ts, As[:, :, 0:4], w0, ALU.mult, cb, ALU.add)
                            stt(Ts, As[:, :, 1:5], w1, Ts, ALU.mult, ALU.add)
                            stt(Ts, As[:, :, 2:6], w2, Ts, ALU.mult, ALU.add)
                            def s2(ge=ge, dst=G[:, fc, 2048:2112], pu=psU):
                                act(ge[:, 0:64], Ts[:, :, :].rearrange("p s t -> p (s t)"), AF.Gelu_apprx_tanh)
                                q_mul.append(lambda: tt("dve", dst, ge[:, 0:64], pu[:, 0:64], ALU.mult))
                            step_pipe(s2)
                            cp("pool", CSs[:, F, :].rearrange("p (s j) -> p s j", j=2), As[:, :, 4:6])
                while q_gelu or q_mul:
                    if q_gelu:
                        q_gelu.pop(0)()
                    while q_mul:
                        q_mul.pop(0)()
                fload("u", b + 1)
                fload("d", b + 1)
                for ti, (t0, tn) in enumerate(TT):
                    for dc in range(8):
                        psD = PS[(4, 5, 7, 0, 1)[(ti * 8 + dc) % 5]]
                        mmg(psD[:, 0:tn], [(wd[:, fc, dc * 128:(dc + 1) * 128], G[:, fc, t0:t0 + tn]) for fc in range(nfc)])
                        xadd(dc, t0, tn, psD)
            AR.release("G", "H", "A0", "A1", "T0", "T1", "GE0", "GE1", "As", "Ts", "CST")
            COp = AR.alloc("COp", [NFC, 2, 128], F32)
            for j in range(2):
                ps = nps()
                trs([(ps[0:NFC, 0:128], CSp[:, j, :], identF)])
                cp("dve", COp[:, j, :], ps[0:NFC, 0:128])
                S.dma("sp", st, D["oconv_p"][l, j, :].rearrange("(c p) -> c p", p=128), COp[:, j, :])
            COs = AR.alloc("COs", [32, NFC, 128], F32)
            for f0 in range(0, NFC, 4):
                nf = min(4, NFC - f0)
                ps = nps()
                trs([(ps[0:32, j * 128:(j + 1) * 128], CSs[:, f0 + j, :], identF) for j in range(nf)])
                cp("dve", COs[:, f0:f0 + nf, :], ps[0:32, 0:nf * 128].rearrange("p (j t) -> p j t", t=128))
            S.dma("sp", st, D["oconv_s"][l].rearrange("s j f -> (s j) f"), COs[:, :, :].rearrange("p c f -> p (c f)"))
            AR.release("CSs", "CSp", "COp", "COs")

        def odd(o, l):
            def _ld(src):
                r = ring_next()
                wv_ = r[0][:, :].rearrange("p (c f) -> p c f", c=8)
                wload(r, wv_, src.rearrange("(c p) f -> p c f", p=128))
                return wv_
            WU = [_ld(D["w_in_odd"][o][:, 512 * b:512 * b + 512]) for b in range(2)]
            wv0 = _ld(D["w_in_odd"][o][:, 1024:1536])
            wv1 = _ld(D["w_in_odd"][o][:, 1536:2048])
            H = AR.alloc("H", [128, 8, NT_], BF16)
            rmsnorm(gmix[:, l, :], H)
            UT = AR.alloc("UT", [128, 8, NT_], BF16)
            lng = AR.alloc("lng", [128, 1024], F32)
            lnb = AR.alloc("lnb", [128, 1024], F32)
            BSg = AR.alloc("BSg", [128, 4, 128], F32)
            BS64g = AR.alloc("BS64g", [128, 4, 16, 4], F32)
            WsT = AR.alloc("WsT", [128, 4, 128], BF16)
            W64 = AR.alloc("W64", [64, 4, 64], BF16)
            wsp = AR.alloc("wsp", [128, 4, 128], F32)
            X4 = AR.alloc("X4", [4, 4, 16, 4], F32)
            S.dma("sp", ld, lng, D["ln_v_g"][o:o + 1, :].to_broadcast([128, 1024]))
            S.dma("sp", ld, lnb, D["ln_v_b"][o:o + 1, :].to_broadcast([128, 1024]))
            S.dma("sp", ld, BSg[:, :, :].rearrange("p g r -> p (g r)"), D["b_spatial"][o:o + 1].rearrange("a g r -> a (g r)").to_broadcast([128, 512]))
            S.dma("sp", ld, wsp, D["w_spatial"][o].rearrange("g r s -> r g s"))
            cp("pool", BS64g, mkap(BSg, 0, [128, [128, 4], [0, 16], [1, 4]]))
            ps = nps()
            trs([(ps[:, g * 128:(g + 1) * 128], wsp[:, g, :], identF) for g in range(4)])
            psv = ps[:, :].rearrange("p (g r) -> p g r", g=4)
            tt("dve", WsT, psv, mkap(causal, 0, [128, [0, 4], [1, 128]]), ALU.mult)
            cp("dve", X4, mkap(psv, 0, [4, [128, 4], [0, 16], [1, 4]]))
            ps2 = nps()
            mms([(ps2[0:64, 0:256], repS[:, :, :].rearrange("k s r -> k (s r)"), X4[:, :, :, :].rearrange("k g s r -> k (g s r)"))])
            tt("dve", W64, ps2[0:64, 0:256].rearrange("p (g t) -> p g t", g=4),
               mkap(bmask, 0, [64, [0, 4], [1, 64]]), ALU.mult)
            AR.release("wsp", "X4")
            for b in range(2):
                wv = WU[b]
                for fc in range(4):
                    for ti, (t0, tn) in enumerate(TT):
                        ps = nps()
                        mmg(ps[:, 0:tn], [(wv[:, dc, fc * 128:(fc + 1) * 128], H[:, dc, t0:t0 + tn]) for dc in range(8)])
                        act(UT[:, 4 * b + fc, t0:t0 + tn], ps[:, 0:tn], AF.Gelu_apprx_tanh)
            WO = [_ld(D["w_out_odd"][o][:, 512 * b:512 * b + 512]) for b in range(2)]
            V = [AR.alloc("V%d" % i, [128, 1024], F32) for i in range(3)]
            VLb = [AR.alloc("VLb%d" % i, [128, 1024], BF16) for i in range(3)]
            STt = [AR.alloc("STt%d" % i, [128, 16], F32) for i in range(3)]
            TMP = [AR.alloc("TMPo%d" % i, [128, 4, 128], F32) for i in range(2)]
            def vstage(c):
                t0, tn = CH[c]
                v = V[c % 3]
                vb = VLb[c % 3]
                sv = STt[c % 3]
                for j, wvj in enumerate((wv0, wv1)):
                    ps = nps()
                    mmg(ps[0:tn, :], [(H[:, dc, t0:t0 + tn], wvj[:, dc, :]) for dc in range(8)])
                    act(v[0:tn, 512 * j:512 * j + 512], ps[0:tn, :], AF.Gelu_apprx_tanh)
                    S.op("dve", lambda e, sv=sv, v=v, j=j, tn=tn: e.bn_stats(out=sv[0:tn, 6 * j:6 * j + 6], in_=v[0:tn, 512 * j:512 * j + 512]),
                         reads=[v[0:tn, 512 * j:512 * j + 512]], writes=[sv[0:tn, 6 * j:6 * j + 6]])
                S.op("dve", lambda e, sv=sv, tn=tn: e.bn_aggr(out=sv[0:tn, 12:14], in_=sv[0:tn, 0:12]), reads=[sv[0:tn, 0:12]], writes=[sv[0:tn, 12:14]])
                act(sv[0:tn, 14:15], sv[0:tn, 13:14], AF.Sqrt, bias=EPS, scale=1.0)
                S.op("dve", lambda e, sv=sv, tn=tn: e.reciprocal(out=sv[0:tn, 14:15], in_=sv[0:tn, 14:15]), reads=[sv[0:tn, 14:15]], writes=[sv[0:tn, 14:15]])
                ts("dve", sv[0:tn, 15:16], sv[0:tn, 12:13], sv[0:tn, 14:15], ALU.mult, -1.0, ALU.mult)
                act(v[0:tn, :], v[0:tn, :], AF.Identity, bias=sv[0:tn, 15:16], scale=sv[0:tn, 14:15])
                tt("pool", v[0:tn, :], v[0:tn, :], lng[0:tn, :], ALU.mult)
                if c < 16:
                    tt("pool", vb[0:tn, :], v[0:tn, :], lnb[0:tn, :], ALU.add)
                else:
                    tt("pool", v[0:tn, :], v[0:tn, :], lnb[0:tn, :], ALU.add)
                    S.dma("sp", st, D["ov_s"][o], v[0:tn, :])
                    cp("pool", vb[0:tn, :], v[0:tn, :])

            def sstage(c):
                t0, tn = CH[c]
                vb = VLb[c % 3]
                for half in range(2):
                    ps = nps()
                    items = []
                    for j in range(4):
                        dj = 4 * half + j
                        g = dj // 2
                        rhs = WsT[:, g, :] if c < 16 else W64[:, g, :]
                        items.append((ps[:, j * 128:j * 128 + tn], vb[0:tn, dj * 128:(dj + 1) * 128], rhs))
                    mms(items)
                    tm = TMP[half]
                    psv = ps[:, :].rearrange("p (a b t) -> p a b t", a=2, b=2)[:, :, :, 0:tn]
                    if c < 16:
                        bias = mkap(BSg, 2 * half * 128, [128, [128, 2], [0, 2], [1, tn]])
                    else:
                        bias = mkap(BS64g, 2 * half * 64, [128, [64, 2], [0, 2], [1, 64]])
                    tt("dve", tm[:, :, :].rearrange("p (a b) t -> p a b t", a=2)[:, :, :, 0:tn], psv, bias, ALU.add)
                    uv = UT[:, 4 * half:4 * half + 4, t0:t0 + tn]
                    tt("dve", uv, tm[:, :, 0:tn], uv, ALU.mult)
            for c in range(17):
                vstage(c)
                if c >= 2:
                    sstage(c - 2)
            sstage(15)
            sstage(16)
            AR.release("V0", "V1", "V2", "VLb0", "VLb1", "VLb2", "STt0", "STt1", "STt2", "TMPo0", "TMPo1", "H")
            for b in range(2):
                wo = WO[b]
                for dcl in range(4):
                    dc = 4 * b + dcl
                    for ti, (t0, tn) in enumerate(TT):
                        ps = nps()
                        mmg(ps[:, 0:tn], [(wo[:, fc, dcl * 128:(dcl + 1) * 128], UT[:, fc, t0:t0 + tn]) for fc in range(8)])
                        xadd(dc, t0, tn, ps)
            AR.release("UT", "lng", "lnb", "BSg", "BS64g", "WsT", "W64")

        EVEN_IMPL = {}

        def final():
            SQ = [AR.alloc("SQ%d" % i, [128, 8, 512], BF16) for i in range(2)]
            RS = [AR.alloc("RS%d" % i, [128, 512], F32) for i in range(2)]
            YF = [AR.alloc("YF%d" % i, [128, 8, 512], F32) for i in range(2)]
            YT = [AR.alloc("YT%d" % i, [128, 1024], F32) for i in range(6)]
            k = 0
            for ti, (t0, tn) in enumerate(TT):
                sq = SQ[ti % 2]
                rs = RS[ti % 2]
                yf = YF[ti % 2]
                act(sq[:, :, 0:tn], X[:, :, t0:t0 + tn], AF.Square)
                ps = nps()
                mmg(ps[:, 0:tn], [(onesB, sq[:, dc, 0:tn]) for dc in range(8)])
                act(rs[:, 0:tn], ps[:, 0:tn], AF.Sqrt, bias=EPS, scale=1.0 / 1024)
                S.op("dve", lambda e, rs=rs, tn=tn: e.reciprocal(out=rs[:, 0:tn], in_=rs[:, 0:tn]), reads=[rs[:, 0:tn]], writes=[rs[:, 0:tn]])
                for dc in range(8):
                    stt(yf[:, dc, 0:tn], X[:, dc, t0:t0 + tn], gfin[:, dc:dc + 1], rs[:, 0:tn], ALU.mult, ALU.mult)
                for c0 in range(0, tn, 128):
                    cn = min(128, tn - c0)
                    yt = YT[k % 6]
                    k += 1
                    for half in range(2):
                        ps = nps()
                        trs([(ps[0:cn, j * 128:(j + 1) * 128], yf[:, 4 * half + j, c0:c0 + cn], identF) for j in range(4)])
                        cp("act" if half == 0 else "dve", yt[0:cn, 512 * half:512 * half + 512], ps[0:cn, :])
                    if ti < 4:
                        S.dma("sp", st, D["yp"][t0 + c0:t0 + c0 + cn, :], yt[0:cn, :])
                    else:
                        S.dma("sp", st, D["ys"], yt[0:cn, :])
            AR.release("SQ0", "SQ1", "RS0", "RS1", "YF0", "YF1", *["YT%d" % i for i in range(6)])

        def dump_x():
            S.dma("sp", st, D["dbg_x"], X)

        ctx = dict(nc=nc, S=S, AR=AR, PS=PS, nps=nps, D=D, X=X, act=act, tt=tt, ts=ts, stt=stt, cp=cp, memset=memset,
                   mmg=mmg, mms=mms, trs=trs, rmsnorm=rmsnorm, xadd=xadd, ring_next=ring_next, wload=wload, ld=ld, st=st,
                   identF=identF, identB=identB, causal=causal, onesF=onesF, onesB=onesB, selH=selH, bmask=bmask, bones=bones,
                   invc=invc, gmix=gmix, pscale=pscale, bgate=bgate, nbf=nbf, NCD=NCD, lnks=LNKS)

        if DEBUG_STOP is None:
            for l in range(4):
                if l % 2 == 0:
                    even_layer(ctx, l // 2, l)
                else:
                    odd(l // 2, l)
                ffn(l)
            final()
        else:
            for kind, l in DEBUG_STOP:
                if kind == "even":
                    even_layer(ctx, l // 2, l)
                elif kind == "odd":
                    odd(l // 2, l)
                elif kind == "ffn":
                    ffn(l)
            dump_x()
        S.wait_all_dma("sp", S.slots)
        S.finish()
        import os as _os2
        if _os2.environ.get("SCHED_DUMP"):
            for en in S.ENG:
                print("==== engine", en)
                for rec in [r for r in S.oplog if r[0] == en][-int(_os2.environ["SCHED_DUMP"]):]:
                    print(rec)
        if getattr(S, "ps_multi", None):
            print("[kernel] WARNING multi-engine PSUM readers:", len(S.ps_multi), S.ps_multi[:6], flush=True)
        print("[kernel] arena peak bytes/partition:", AR.peak, " instr:", dict(S.cnt), " waits:", S.n_wait, flush=True)
    return nc


def _shard_inputs(inp):
    A = lambda a: np.ascontiguousarray(a, dtype=np.float32)
    wnames = ["g_mix", "w_in_even", "b_gates", "w_pool", "pool_scale", "g_head", "w_out_even", "w_in_odd", "ln_v_g", "ln_v_b",
              "w_spatial", "b_spatial", "w_out_odd", "g_ffn", "w_ffn_gate", "w_ffn_up", "conv_w", "conv_b", "w_ffn_down", "g_final"]
    W = {n: A(inp[n]) for n in wnames}
    maps = []
    for c in range(8):
        s = slice(16 * c, 16 * c + 16)
        m = dict(W)
        m["xp"] = A(inp["x_prompt"][c])
        m["xs"] = A(np.asarray(inp["x_sample"])[s].reshape(64, 1024))
        m["spool"] = A(np.asarray(inp["state_pool"])[:, s])
        m["sc"] = A(np.asarray(inp["state_mlstm_c"])[:, s])
        m["sn"] = A(np.asarray(inp["state_mlstm_n"])[:, s])
        m["sm"] = A(np.asarray(inp["state_mlstm_m"])[:, s])
        m["sconv"] = A(np.asarray(inp["state_ffn_conv"])[:, s])
        maps.append(m)
    return maps


def kernel(**inputs):
    inputs = {k: np.asarray(v) for k, v in inputs.items()}
    nc = build_program()
    maps = _shard_inputs(inputs)
    res = run_bass_kernel_spmd(nc, maps, core_ids=list(range(8)))
    R = res.results
    f = np.float32
    y_p = np.zeros((8, 2048, 1024), f); y_s = np.zeros((128, 4, 1024), f)
    pool_p = np.zeros((2, 8, 15, 512), f); pool_s = np.zeros((2, 128, 15, 512), f)
    c_p = np.zeros((2, 8, 4, 128, 128), f); c_s = np.zeros((2, 128, 4, 128, 128), f)
    n_p = np.zeros((2, 8, 4, 128), f); n_s = np.zeros((2, 128, 4, 128), f)
    m_p = np.zeros((2, 8, 4), f); m_s = np.zeros((2, 128, 4), f)
    cv_p = np.zeros((4, 8, 2, 2816), f); cv_s = np.zeros((4, 128, 2, 2816), f)
    v_s = np.zeros((2, 128, 4, 1024), f)
    for c in range(8):
        r = R[c]
        s = slice(16 * c, 16 * c + 16)
        y_p[c] = r["yp"]; y_s[s] = r["ys"].reshape(16, 4, 1024)
        pool_p[:, c] = r["opool_p"]; pool_s[:, s] = r["opool_s"]
        c_p[:, c] = r["oc_p"]; c_s[:, s] = r["oc_s"]
        n_p[:, c] = r["on_p"]; n_s[:, s] = r["on_s"]
        m_p[:, c] = r["om_p"]; m_s[:, s] = r["om_s"]
        cv_p[:, c] = r["oconv_p"]; cv_s[:, s] = r["oconv_s"]
        v_s[:, s] = r["ov_s"].reshape(2, 16, 4, 1024)
    return (y_p, y_s, pool_p, pool_s, c_p, c_s, n_p, n_s, m_p, m_s, cv_p, cv_s, v_s)
```
